# Optimizing a Trainium2 kernel written in Bass

```python
import math
import jax, jax.numpy as jnp
from jax import lax
import numpy as np

D_MODEL = 2048
BATCH = 8
SEQ = 2048
DEPTH = 2

MEM_LEN = 256
CHUNK = 128
Q_BLOCK = 128
D_A = D_MODEL // 2
A_GROUPS = 8
A_GROUP_DIM = D_A // A_GROUPS
D_B = D_MODEL // 4
B_HEADS = 4
B_HEAD_DIM = D_B // B_HEADS
D_C = D_MODEL // 4
C_HEADS = 4
C_HEAD_DIM = D_C // C_HEADS
SPLIT_SIZES = (D_A, D_A, D_A, D_B, D_B, D_B, D_B, D_C, D_C)
IN_WIDTH = sum(SPLIT_SIZES)
EPS = 1e-6

kernel_name = "hybrid_sgu_stickbreak_memxattn"


def rms_norm(x, g):
    xf = x.astype(jnp.float32)
    y = xf * lax.rsqrt(jnp.mean(xf * xf, axis=-1, keepdims=True) + EPS)
    return (y * g.astype(jnp.float32)).astype(x.dtype)


def layer_norm(x, g, b):
    xf = x.astype(jnp.float32)
    mu = jnp.mean(xf, axis=-1, keepdims=True)
    xc = xf - mu
    y = xc * lax.rsqrt(jnp.mean(xc * xc, axis=-1, keepdims=True) + EPS)
    return (y * g.astype(jnp.float32) + b.astype(jnp.float32)).astype(x.dtype)


def sgu_mixer(u, v, ln_g, ln_b, w_s, b_s):
    bsz, s_len, _ = v.shape
    n_chunks = s_len // CHUNK
    v = layer_norm(v, ln_g, ln_b)
    vc = v.reshape(bsz, n_chunks, CHUNK, A_GROUPS, A_GROUP_DIM)
    mask = jnp.tril(jnp.ones((CHUNK, CHUNK), dtype=bool))
    w = jnp.where(mask[None], w_s, jnp.zeros_like(w_s))
    mixed = jnp.einsum('gts,bcsgd->bctgd', w, vc) + b_s.T[None, None, :, :, None]
    return u * mixed.reshape(bsz, s_len, D_A)


def stick_breaking_attention(q, k, v):
    s_len = q.shape[1]
    scale = 1.0 / math.sqrt(q.shape[-1])
    outs = []
    for i in range(s_len // Q_BLOCK):
        start = i * Q_BLOCK
        kv_len = start + Q_BLOCK
        q_blk = q[:, start:kv_len]
        k_blk = k[:, :kv_len]
        v_blk = v[:, :kv_len]
        z = jnp.einsum('bthd,bshd->bhts', q_blk, k_blk).astype(jnp.float32) * scale
        t_idx = start + jnp.arange(Q_BLOCK)[:, None]
        s_idx = jnp.arange(kv_len)[None, :]
        causal = s_idx < t_idx
        log_beta = jax.nn.log_sigmoid(z)
        log_1mb = jnp.where(causal, jax.nn.log_sigmoid(-z), 0.0)
        rc = lax.cumsum(log_1mb, axis=3, reverse=True)
        after = jnp.pad(rc[..., 1:], ((0, 0), (0, 0), (0, 0), (0, 1)))
        a = jnp.where(causal, jnp.exp(log_beta + after), 0.0)
        outs.append(jnp.einsum('bhts,bshd->bthd', a.astype(v.dtype), v_blk))
    return jnp.concatenate(outs, axis=1)


def memory_attention(q, mem_k, mem_v, q_g, k_g):
    scale = 1.0 / math.sqrt(q.shape[-1])
    qn = rms_norm(q, q_g)
    kn = rms_norm(mem_k, k_g)
    s = jnp.einsum('bthd,bmhd->bhtm', qn, kn).astype(jnp.float32) * scale
    p = jax.nn.softmax(s, axis=-1)
    return jnp.einsum('bhtm,bmhd->bthd', p.astype(mem_v.dtype), mem_v)


def setup_inputs(seed: int = 0) -> dict:
    key = jax.random.key(seed)
    ks = jax.random.split(key, 16)
    f32 = jnp.float32
    x = jax.random.normal(ks[0], (BATCH, SEQ, D_MODEL), f32)
    mem = jax.random.normal(ks[1], (BATCH, MEM_LEN, D_MODEL), f32)
    norm_g = 1.0 + 0.01 * jax.random.normal(ks[2], (DEPTH, D_MODEL), f32)
    w_in = jax.random.normal(ks[3], (DEPTH, D_MODEL, IN_WIDTH), f32) * D_MODEL ** -0.5
    sgu_ln_g = 1.0 + 0.01 * jax.random.normal(ks[4], (DEPTH, D_A), f32)
    sgu_ln_b = 0.01 * jax.random.normal(ks[5], (DEPTH, D_A), f32)
    sgu_w = jax.random.normal(ks[6], (DEPTH, A_GROUPS, CHUNK, CHUNK), f32) * CHUNK ** -0.5
    sgu_b = 1.0 + 0.01 * jax.random.normal(ks[7], (DEPTH, A_GROUPS, CHUNK), f32)
    mem_norm_g = 1.0 + 0.01 * jax.random.normal(ks[8], (DEPTH, D_MODEL), f32)
    w_mem_kv = jax.random.normal(ks[9], (DEPTH, D_MODEL, 2 * D_C), f32) * D_MODEL ** -0.5
    q_norm_g = 1.0 + 0.01 * jax.random.normal(ks[10], (DEPTH, C_HEAD_DIM), f32)
    k_norm_g = 1.0 + 0.01 * jax.random.normal(ks[11], (DEPTH, C_HEAD_DIM), f32)
    w_out = jax.random.normal(ks[12], (DEPTH, D_MODEL, D_MODEL), f32) * D_MODEL ** -0.5
    return {"x": x, "mem": mem, "norm_g": norm_g, "w_in": w_in,
            "sgu_ln_g": sgu_ln_g, "sgu_ln_b": sgu_ln_b, "sgu_w": sgu_w, "sgu_b": sgu_b,
            "mem_norm_g": mem_norm_g, "w_mem_kv": w_mem_kv,
            "q_norm_g": q_norm_g, "k_norm_g": k_norm_g, "w_out": w_out}


def reference(x, mem, norm_g, w_in, sgu_ln_g, sgu_ln_b, sgu_w, sgu_b,
              mem_norm_g, w_mem_kv, q_norm_g, k_norm_g, w_out):
    bsz, s_len, _ = x.shape
    split_idx = list(np.cumsum(SPLIT_SIZES)[:-1])
    for l in range(DEPTH):
        h = rms_norm(x, norm_g[l])
        proj = jnp.matmul(h, w_in[l])
        u_a, v_a, z_a, q_b, k_b, v_b, z_b, q_c, z_c = jnp.split(proj, split_idx, axis=-1)

        u_a = jax.nn.gelu(u_a, approximate=False)
        v_a = jax.nn.gelu(v_a, approximate=False)
        y_a = sgu_mixer(u_a, v_a, sgu_ln_g[l], sgu_ln_b[l], sgu_w[l], sgu_b[l]) * jax.nn.silu(z_a)

        hb = (bsz, s_len, B_HEADS, B_HEAD_DIM)
        y_b = stick_breaking_attention(q_b.reshape(hb), k_b.reshape(hb), v_b.reshape(hb))
        y_b = y_b.reshape(bsz, s_len, D_B) * jax.nn.silu(z_b)

        mem_h = rms_norm(mem, mem_norm_g[l])
        mem_kv = jnp.matmul(mem_h, w_mem_kv[l])
        mem_k, mem_v = jnp.split(mem_kv, 2, axis=-1)
        hm = (bsz, mem.shape[1], C_HEADS, C_HEAD_DIM)
        y_c = memory_attention(q_c.reshape(bsz, s_len, C_HEADS, C_HEAD_DIM),
                               mem_k.reshape(hm), mem_v.reshape(hm), q_norm_g[l], k_norm_g[l])
        y_c = y_c.reshape(bsz, s_len, D_C) * jax.nn.silu(z_c)

        y = jnp.concatenate([y_a, y_b, y_c], axis=-1)
        x = x + jnp.matmul(y, w_out[l])
    return x
```

```python
import math
import numpy as np
import concourse.bass as bass
import concourse.mybir as mybir
from concourse.bass_utils import run_bass_kernel_spmd

F32 = mybir.dt.float32
BF16 = mybir.dt.bfloat16
U8 = mybir.dt.uint8
AF = mybir.ActivationFunctionType
ALU = mybir.AluOpType

S = 2048
D = 2048
MEM = 256
NL = 2
INW = 6144
EPS = 1e-6
NEG = -30000.0
NC = 16
SCALE = 1.0 / math.sqrt(128.0)

ENGS = ("pe", "act", "dve", "pool", "sp")


class Sched:
    def __init__(self, nc, esems, dma_sems):
        self.nc = nc
        self.prog = {e: [] for e in ENGS}
        self.cnt = {e: 0 for e in ENGS}
        self.esem = esems
        self.free_dsem = list(dma_sems)
        self.dsem = {}
        self.seen = {e: {} for e in ENGS}
        self.lastw = {}
        self.readers = {}

    def _waits(self, eng, reads, writes):
        toks = []
        for r in reads:
            t = self.lastw.get(r)
            if t is not None:
                toks.append(("raw", t))
        for w in writes:
            t = self.lastw.get(w)
            if t is not None:
                toks.append(("waw", t))
            for t in self.readers.get(w, ()):
                toks.append(("war", t))
        waits = []
        seen = self.seen[eng]
        for kind, (key, val, peng) in toks:
            if peng == eng and eng == "pe":
                continue
            if seen.get(key, 0) >= val:
                continue
            seen[key] = val
            waits.append((key, val))
        return waits

    def _commit(self, tok, reads, writes):
        for r in reads:
            self.readers.setdefault(r, []).append(tok)
        for w in writes:
            self.lastw[w] = tok
            self.readers[w] = []

    budget = None

    def _drop(self):
        if self.budget is None:
            return False
        if self.budget <= 0:
            return True
        self.budget -= 1
        return False

    def op(self, eng, fn, reads=(), writes=()):
        if self._drop():
            return
        waits = self._waits(eng, reads, writes)
        self.cnt[eng] += 1
        tok = (eng, self.cnt[eng], eng)
        self.prog[eng].append((waits, fn, eng, 1))
        self._commit(tok, reads, writes)

    def dma(self, queue, fn, reads, writes, key, force=False):
        if not force and self._drop():
            return (("d", key), 0, "dma")
        waits = self._waits(queue, reads, writes)
        if key not in self.dsem:
            self.dsem[key] = [self.free_dsem.pop(), 0]
        self.dsem[key][1] += 16
        tok = (("d", key), self.dsem[key][1], "dma")
        self.prog[queue].append((waits, fn, ("d", key), 16))
        self._commit(tok, reads, writes)
        return tok

    def semh(self, key):
        if isinstance(key, tuple) and key[0] == "d":
            return self.dsem[key[1]][0]
        return self.esem[key]

    def barrier(self):
        for e in ENGS:
            waits = []
            for o in ENGS:
                if o != e and self.cnt[o] > 0 and self.seen[e].get(o, 0) < self.cnt[o]:
                    self.seen[e][o] = self.cnt[o]
                    waits.append((o, self.cnt[o]))
            for k, (h, c) in self.dsem.items():
                kk = ("d", k)
                if c > 0 and self.seen[e].get(kk, 0) < c:
                    self.seen[e][kk] = c
                    waits.append((kk, c))
            if waits:
                self.prog[e].append((waits, None, None, 0))
        self.lastw = {}
        self.readers = {}

    def replay(self, eng, e):
        for waits, fn, skey, inc in self.prog[eng]:
            for key, val in waits:
                e.wait_ge(self.semh(key), val)
            if fn is None:
                continue
            ins = fn(e)
            ins.then_inc(self.semh(skey), inc)


def build_nc(n_layers=NL, dbg=False):
    nc = bass.Bass("TRN2", target_bir_lowering=False)
    x_d = nc.dram_tensor("x", [S, D], F32, kind="ExternalInput").ap()
    mem_d = nc.dram_tensor("mem", [MEM, D], F32, kind="ExternalInput").ap()
    norm_g_d = nc.dram_tensor("norm_g", [NL, D], F32, kind="ExternalInput").ap()
    w_in_d = nc.dram_tensor("w_in", [NL, D, INW], F32, kind="ExternalInput").ap()
    lng_d = nc.dram_tensor("sgu_ln_g", [NL, 1024], F32, kind="ExternalInput").ap()
    lnb_d = nc.dram_tensor("sgu_ln_b", [NL, 1024], F32, kind="ExternalInput").ap()
    sguw_d = nc.dram_tensor("sgu_w", [NL, 8, 128, 128], F32, kind="ExternalInput").ap()
    sgub_d = nc.dram_tensor("sgu_b", [NL, 8, 128], F32, kind="ExternalInput").ap()
    mng_d = nc.dram_tensor("mem_norm_g", [NL, D], F32, kind="ExternalInput").ap()
    wkv_d = nc.dram_tensor("w_mem_kv", [NL, D, 1024], F32, kind="ExternalInput").ap()
    qg_d = nc.dram_tensor("q_norm_g", [NL, 128], F32, kind="ExternalInput").ap()
    kg_d = nc.dram_tensor("k_norm_g", [NL, 128], F32, kind="ExternalInput").ap()
    wout_d = nc.dram_tensor("w_out", [NL, D, D], F32, kind="ExternalInput").ap()
    out_d = nc.dram_tensor("out", [S, D], F32, kind="ExternalOutput").ap()
    x1_d = nc.dram_tensor("x1_scratch", [S, D], F32).ap()
    if dbg:
        dbg_h = nc.dram_tensor("dbg_h", [128, NC * S], BF16, kind="ExternalOutput").ap()
        dbg_y = nc.dram_tensor("dbg_y", [128, NC * S], BF16, kind="ExternalOutput").ap()

    ARENA = 211968
    arena = nc.alloc_sbuf_tensor("arena", [128, ARENA], U8)
    ps = [nc.alloc_psum_tensor("ps%d" % i, [128, 512], F32) for i in range(8)]

    def view(off, dt, shape):
        esz = 4 if dt == F32 else 2
        n = 1
        for s_ in shape[1:]:
            n *= s_
        ap = arena[0:shape[0], off:off + n * esz].bitcast(dt)
        if len(shape) == 3:
            ap = ap.rearrange("p (a b) -> p a b", a=shape[1])
        return ap

    HT_OFF = 0
    YT_OFF = 65536
    W_OFF = 131072
    C_OFF = W_OFF + 3 * 8192
    SCR = C_OFF + 14592
    assert SCR + 32768 <= ARENA

    hT = view(HT_OFF, BF16, [128, NC, S])
    yT = view(YT_OFF, BF16, [128, NC, S])
    VA = view(YT_OFF + 32768, BF16, [128, 16, 1024])
    Wr = [view(W_OFF + i * 8192, BF16, [128, NC, 256]) for i in range(3)]
    ident_bf = view(C_OFF + 0, BF16, [128, 128])
    negU = view(C_OFF + 256, BF16, [128, 128])
    negOnes = view(C_OFF + 512, BF16, [128, 128])
    ones_bf = view(C_OFF + 768, BF16, [128, 128])
    negmask = view(C_OFF + 1024, BF16, [128, 128])
    ident_f = view(C_OFF + 1280, F32, [128, 128])
    neghalf = view(C_OFF + 1792, F32, [128, 512])
    kgq = view(C_OFF + 3840, F32, [128, 1])
    qg = view(C_OFF + 3844, F32, [128, 1])
    kg = view(C_OFF + 3848, F32, [128, 1])
    st = view(C_OFF + 3872, F32, [128, 4, 6])
    mv = view(C_OFF + 3968, F32, [128, 2])
    t1 = view(C_OFF + 3976, F32, [128, 1])
    rstd = view(C_OFF + 3980, F32, [128, 1])
    ST2 = [st, view(C_OFF + 3984, F32, [128, 4, 6])]
    MV2 = [mv, view(C_OFF + 4080, F32, [128, 2])]
    T12 = [t1, view(C_OFF + 4088, F32, [128, 1])]
    RS2 = [rstd, view(C_OFF + 4092, F32, [128, 1])]
    knT = view(C_OFF + 4096, BF16, [128, 4, 256])
    memv = view(C_OFF + 6144, BF16, [128, 2, 512])
    wT = view(C_OFF + 8192, BF16, [128, 8, 128])
    bhi = view(C_OFF + 10240, BF16, [128, 1024])
    blo = view(C_OFF + 12288, BF16, [128, 1024])
    ones0 = view(C_OFF + 14336, BF16, [128, 128])
    XT = [view(YT_OFF + i * 8192, F32, [128, D]) for i in range(3)]
    XN = [view(YT_OFF + 24576 + i * 4096, BF16, [128, D]) for i in range(2)]
    GB = view(YT_OFF + 32768, F32, [128, D])
    MGB = view(YT_OFF + 40960, F32, [128, D])
    XS = [view(YT_OFF + 49152 + i * 8192, F32, [128, D]) for i in range(2)]

    with nc.semaphore("s_pe") as s_pe, nc.semaphore("s_act") as s_act, \
            nc.semaphore("s_dve") as s_dve, nc.semaphore("s_pool") as s_pool, \
            nc.semaphore("s_sp") as s_sp:
        import contextlib
        with contextlib.ExitStack() as es:
            dma_sems = [es.enter_context(nc.semaphore("dsem%d" % i)) for i in range(24)]
            sch = Sched(nc, {"pe": s_pe, "act": s_act, "dve": s_dve, "pool": s_pool, "sp": s_sp}, dma_sems)
            _emit_program(nc, sch, locals(), n_layers, dbg)
            with nc.Block() as block:
                @block.tensor
                def _(e):
                    sch.replay("pe", e)

                @block.scalar
                def _(e):
                    sch.replay("act", e)

                @block.vector
                def _(e):
                    sch.replay("dve", e)

                @block.gpsimd
                def _(e):
                    sch.replay("pool", e)

                @block.sync
                def _(e):
                    sch.replay("sp", e)
    return nc


def _emit_program(nc, sch, V, n_layers, dbg):
    import os
    STOP = int(os.environ.get("KSTOP", "99")) if dbg else 99
    g = V
    ps = g["ps"]
    hT, yT, VA, Wr = g["hT"], g["yT"], g["VA"], g["Wr"]
    view = g["view"]
    SCR = g["SCR"]
    HT_OFF = g["HT_OFF"]
    ident_bf, negU, negOnes, ones_bf, negmask, ident_f, neghalf = (
        g["ident_bf"], g["negU"], g["negOnes"], g["ones_bf"], g["negmask"], g["ident_f"], g["neghalf"])
    kgq, qg, kg, st, mv, t1, rstd = g["kgq"], g["qg"], g["kg"], g["st"], g["mv"], g["t1"], g["rstd"]
    knT, memv, wT, bhi, blo, ones0 = g["knT"], g["memv"], g["wT"], g["bhi"], g["blo"], g["ones0"]
    XT, XN, GB, MGB = g["XT"], g["XN"], g["GB"], g["MGB"]
    ST2, MV2, T12, RS2 = g["ST2"], g["MV2"], g["T12"], g["RS2"]
    XS = g["XS"]
    op, dma = sch.op, sch.dma

    def psb(i, n=512, off=0):
        return ps[i][:, off:off + n]

    def ps_bf(i):
        return ps[i][:, :].bitcast(BF16)

    op("pool", lambda e: e.memset(ones_bf, 1.0), writes=["ones"])
    op("pool", lambda e: e.memset(negOnes, -1.0), writes=["negones"])
    op("pool", lambda e: e.memset(neghalf, -0.5), writes=["neghalf"])
    op("pool", lambda e: e.memset(negmask, NEG), writes=["negmask"])
    op("pool", lambda e: e.memset(ident_f, 1.0), writes=["identf"])
    op("pool", lambda e: e.affine_select(ident_bf, ones_bf, [[-1, 128]], ALU.is_equal, 0.0, base=0, channel_multiplier=1),
       reads=["ones"], writes=["ident"])
    op("pool", lambda e: e.affine_select(ident_f, ident_f, [[-1, 128]], ALU.is_equal, 0.0, base=0, channel_multiplier=1),
       reads=["identf"], writes=["identf"])
    op("pool", lambda e: e.affine_select(negU, negOnes, [[-1, 128]], ALU.is_ge, 0.0, base=0, channel_multiplier=1),
       reads=["negones"], writes=["negU"])
    op("pool", lambda e: e.affine_select(negmask, negmask, [[-1, 128]], ALU.is_ge, 0.0, base=0, channel_multiplier=1),
       reads=["negmask"], writes=["negmask"])
    op("pool", lambda e: e.memset(bhi, 0.0), writes=["bhi"])
    op("pool", lambda e: e.memset(blo, 0.0), writes=["blo"])
    op("pool", lambda e: e.affine_select(ones0, ones_bf, [[0, 128]], ALU.is_equal, 0.0, base=0, channel_multiplier=1),
       reads=["ones"], writes=["ones0"])
    sch.barrier()

    wlist = []
    for l_ in range(n_layers):
        for j in range(4):
            wlist.append([(g["wkv_d"][l_, :, j * 256:(j + 1) * 256], 0, 256)])
        for j in range(8):
            wlist.append([(g["w_in_d"][l_, :, j * 128:(j + 1) * 128], 0, 128),
                          (g["w_in_d"][l_, :, 2048 + j * 128:2048 + (j + 1) * 128], 128, 128)])
        for j in range(4):
            wlist.append([(g["w_in_d"][l_, :, 1024 + j * 256:1024 + (j + 1) * 256], 0, 256)])
        for h_ in range(4):
            cq = 3072 + h_ * 128
            wlist.append([(g["w_in_d"][l_, :, cq:cq + 128], 0, 128), (g["w_in_d"][l_, :, cq + 512:cq + 640], 128, 128)])
            wlist.append([(g["w_in_d"][l_, :, cq + 1024:cq + 1152], 0, 128), (g["w_in_d"][l_, :, cq + 1536:cq + 1664], 128, 128)])
        for h_ in range(4):
            cq = 5120 + h_ * 128
            wlist.append([(g["w_in_d"][l_, :, cq:cq + 128], 0, 128), (g["w_in_d"][l_, :, cq + 512:cq + 640], 128, 128)])
        for n_ in range(8):
            wlist.append([(g["wout_d"][l_, :, n_ * 256:(n_ + 1) * 256], 0, 256)])
    wstate = {"issued": 0, "used": 0}

    def wissue_upto(k):
        while wstate["issued"] < min(k, len(wlist)):
            idx = wstate["issued"]
            wstate["issued"] += 1
            i = idx % 3
            for kk, (src, c0, ncols) in enumerate(wlist[idx]):
                dst = Wr[i][:, :, c0:c0 + ncols]
                s_ = src.rearrange("(c p) f -> p c f", p=128)
                tok = dma("pool", (lambda e, dst=dst, s_=s_: e.dma_start(out=dst, in_=s_)),
                          reads=[], writes=([("W", i)] if kk == 0 else []), key=("W", i))
            sch.lastw[("W", i)] = tok

    def wload(parts=None):
        k = wstate["used"]
        wstate["used"] += 1
        wissue_upto(k + 3)
        return k % 3

    def hkeys(G):
        return [("hT", tb) for tb in range(4 * G, 4 * G + 4)] + [("hTa", tb) for tb in range(4 * G, 4 * G + 4)]

    def proj_fm(slot, c0, G, bank):
        pairs = [(Wr[slot][:, c, c0:c0 + 128], hT[:, c, G * 512:(G + 1) * 512]) for c in range(NC)]

        def fn(e, pairs=pairs, bank=bank):
            for i, (l, r) in enumerate(pairs):
                ins = e.matmul(psb(bank), l, r, start=(i == 0), stop=(i == NC - 1))
            return ins
        op("pe", fn, reads=[("W", slot)] + hkeys(G), writes=[("ps", bank)])

    def norm_transpose(src_ap, gbt, dst_fn, nblk, tag):
        def stA(tb):
            xs = tb % 3
            pz = tb % 2
            xt = XT[xs]
            st_, mv_, t1_ = ST2[pz], MV2[pz], T12[pz]
            dma("sp", (lambda e, xt=xt, tb=tb: e.dma_start(out=xt, in_=src_ap[tb * 128:(tb + 1) * 128, :])),
                reads=[(tag + "src", tb)], writes=[("xt", xs)], key=("xt", xs))
            xg_ = XS[tb % 2]
            op("pool", (lambda e, xt=xt, xg_=xg_: e.tensor_tensor(xg_, xt, gbt, ALU.mult)),
               reads=[("xt", xs), tag + "gb"], writes=[("xs", tb % 2)])
            for k in range(4):
                op("dve", (lambda e, xt=xt, k=k, st_=st_: e.bn_stats(st_[:, k, :], xt[:, k * 512:(k + 1) * 512])),
                   reads=[("xt", xs)], writes=[("st", pz, k)])
            op("dve", lambda e, st_=st_, mv_=mv_: e.bn_aggr(mv_, st_.rearrange("p a b -> p (a b)")),
               reads=[("st", pz, k) for k in range(4)], writes=[("mv", pz)])
            op("dve", lambda e, mv_=mv_, t1_=t1_: e.scalar_tensor_tensor(t1_, mv_[:, 0:1], mv_[:, 0:1], mv_[:, 1:2], ALU.mult, ALU.add),
               reads=[("mv", pz)], writes=[("t1", pz)])
            op("dve", lambda e, t1_=t1_: e.tensor_scalar(t1_, t1_, EPS, None, ALU.add), reads=[("t1", pz)], writes=[("t1", pz)])

        def stR(tb):
            pz = tb % 2
            t1_, rs_ = T12[pz], RS2[pz]
            op("act", lambda e, t1_=t1_: e.activation(t1_, t1_, AF.Ln), reads=[("t1", pz)], writes=[("t1", pz)])
            op("act", lambda e, t1_=t1_, rs_=rs_: e.activation(rs_, t1_, AF.Exp, scale=-0.5), reads=[("t1", pz)], writes=[("rstd", pz)])

        def stB1(tb):
            ns = tb % 2
            xn = XN[ns]
            xs_ = XS[ns]
            rs_ = RS2[ns]
            op("act", (lambda e, xn=xn, xs_=xs_, rs_=rs_: e.activation(xn, xs_, AF.Copy, scale=rs_)),
               reads=[("xs", ns), ("rstd", ns)], writes=[("xn", ns)])
            b0 = 2 * (tb % 2)
            for half in range(2):
                bank = b0 + half

                def fn(e, xn=xn, bank=bank, half=half):
                    pb = ps_bf(bank)
                    for cc in range(8):
                        c = half * 8 + cc
                        ins = e.transpose(pb[:, cc * 128:(cc + 1) * 128], xn[:, c * 128:(c + 1) * 128], ident_bf)
                    return ins
                op("pe", fn, reads=[("xn", ns), "ident"], writes=[("ps", bank)])

        def stB2(tb):
            b0 = 2 * (tb % 2)
            for half in range(2):
                bank = b0 + half
                dst, keys = dst_fn(tb, half)
                src3 = ps_bf(bank).rearrange("p (a b) -> p a b", a=8)
                op("act", (lambda e, dst=dst, src3=src3: e.copy(dst, src3)), reads=[("ps", bank)], writes=keys)

        for tb in range(min(2, nblk)):
            stA(tb)
            stR(tb)
        for tb in range(nblk + 1):
            if tb < nblk:
                stB1(tb)
            if 0 <= tb - 1 < nblk:
                stB2(tb - 1)
            if tb + 2 < nblk:
                stA(tb + 2)
                stR(tb + 2)

    def finish_dbg():
        sch.budget = None
        for c in range(NC):
            dma("sp", lambda e, c=c: e.dma_start(out=g["dbg_h"][:, c * S:(c + 1) * S], in_=hT[:, c, :]), reads=[], writes=["dbgh"], key="o0")
            dma("sp", lambda e, c=c: e.dma_start(out=g["dbg_y"][:, c * S:(c + 1) * S], in_=yT[:, c, :]), reads=[], writes=["dbgy"], key="o1")
        sch.barrier()

    if STOP <= 0:
        finish_dbg()
        return
    for l in range(n_layers):
        src_ap = g["x_d"] if l == 0 else g["out_d"]
        dst_ap = g["out_d"]
        dma("sp", lambda e, l=l: e.dma_start(out=GB, in_=g["norm_g_d"][l:l + 1, :].partition_broadcast(128)),
            reads=[], writes=["xgb"], key="c0")
        dma("sp", lambda e, l=l: e.dma_start(out=MGB, in_=g["mng_d"][l:l + 1, :].partition_broadcast(128)),
            reads=[], writes=["mgb"], key="c1")
        dma("sp", lambda e, l=l: e.dma_start(out=qg, in_=g["qg_d"][l:l + 1, :].rearrange("a p -> p a")),
            reads=[], writes=["qg"], key="c2")
        dma("sp", lambda e, l=l: e.dma_start(out=kg, in_=g["kg_d"][l:l + 1, :].rearrange("a p -> p a")),
            reads=[], writes=["kg"], key="c3")
        sguw = view(SCR, F32, [128, 8, 128])
        dma("sp", lambda e, l=l, sguw=sguw: e.dma_start(out=sguw, in_=g["sguw_d"][l].rearrange("g t s -> t g s")),
            reads=[], writes=["sguw"], key="c4")
        bf_ = view(SCR + 4096, F32, [1, 1024])
        bl_ = view(SCR + 8192, F32, [1, 1024])
        dma("sp", lambda e, l=l, bf_=bf_: e.dma_start(out=bf_, in_=g["sgub_d"][l:l + 1].rearrange("a g t -> a (g t)")),
            reads=[], writes=["bf"], key="c5")
        op("dve", lambda e: e.scalar_tensor_tensor(kgq, qg, SCALE, kg, ALU.mult, ALU.mult), reads=["qg", "kg"], writes=["kgq"])
        op("dve", lambda e, bf_=bf_: e.tensor_copy(bhi[0:1, :], bf_), reads=["bf"], writes=["bhi"])
        op("dve", lambda e, bf_=bf_, bl_=bl_: e.tensor_tensor(bl_, bf_, bhi[0:1, :], ALU.subtract), reads=["bf", "bhi"], writes=["bl"])
        op("dve", lambda e, bl_=bl_: e.tensor_copy(blo[0:1, :], bl_), reads=["bl"], writes=["blo"])
        wtf = view(SCR + 12288, F32, [128, 128])
        for gi in range(8):
            op("pe", lambda e, gi=gi, sguw=sguw: e.transpose(psb(7, 128), sguw[:, gi, :], ident_f),
               reads=["sguw", "identf"], writes=[("ps", 7)])
            op("act", lambda e, wtf=wtf: e.copy(wtf, psb(7, 128)), reads=[("ps", 7)], writes=["wtf"])
            op("pool", lambda e, gi=gi, wtf=wtf: e.affine_select(wT[:, gi, :], wtf, [[1, 128]], ALU.is_ge, 0.0,
                                                                  base=0, channel_multiplier=-1),
               reads=["wtf"], writes=["wT"])

        def dst_h(tb, half):
            return hT[:, half * 8:(half + 1) * 8, tb * 128:(tb + 1) * 128], [("hT", tb)] if half == 1 else [("hTa", tb)]
        norm_transpose(src_ap, GB, dst_h, 16, "x")

        if STOP <= 1:
            sch.barrier()
            finish_dbg()
            return
        if dbg and "KBUD" in os.environ:
            sch.budget = int(os.environ["KBUD"])
        memh = view(SCR + 16384, BF16, [128, NC, 256])

        def dst_m(tb, half):
            return memh[:, half * 8:(half + 1) * 8, tb * 128:(tb + 1) * 128], [("memh", tb, half)]
        norm_transpose(g["mem_d"], MGB, dst_m, 2, "m")
        memh_keys = [("memh", tb, half) for tb in range(2) for half in range(2)]
        kf = view(SCR + 24576, F32, [128, 256])
        ksq = view(SCR + 25600, BF16, [128, 256])
        kt_ = view(SCR + 26112, F32, [128, 256])
        rk = view(SCR + 27136, F32, [128, 256])
        for j in range(4):
            slot = wload([(g["wkv_d"][l, :, j * 256:(j + 1) * 256], 0, 256)])
            if j < 2:
                for hh in range(2):
                    h = 2 * j + hh
                    pairs = [(Wr[slot][:, c, hh * 128:(hh + 1) * 128], memh[:, c, :]) for c in range(NC)]

                    def fn(e, pairs=pairs):
                        for i, (l_, r) in enumerate(pairs):
                            ins = e.matmul(psb(4, 256), l_, r, start=(i == 0), stop=(i == NC - 1))
                        return ins
                    op("pe", fn, reads=[("W", slot)] + memh_keys, writes=[("ps", 4)])
                    op("act", lambda e: e.activation(ksq, psb(4, 256), AF.Square), reads=[("ps", 4)], writes=["ksq"])
                    op("dve", lambda e: e.tensor_copy(kf, psb(4, 256)), reads=[("ps", 4), "ksq"], writes=["kf"])
                    op("pe", lambda e: e.matmul(psb(6, 256), ones_bf, ksq, start=True, stop=True),
                       reads=["ksq", "ones"], writes=[("ps", 6)])
                    op("act", lambda e: e.activation(kt_, psb(6, 256), AF.Ln, bias=EPS, scale=1.0 / 128.0),
                       reads=[("ps", 6)], writes=["kt"])
                    op("act", lambda e: e.activation(rk, kt_, AF.Exp, scale=-0.5), reads=["kt"], writes=["rk"])
                    op("dve", lambda e, h=h: e.scalar_tensor_tensor(knT[:, h, :], kf, kgq, rk, ALU.mult, ALU.mult),
                       reads=["kf", "rk", "kgq"], writes=[("knT", h)])
            else:
                jj = j - 2
                for mb in range(2):
                    pairs = [(memh[:, c, mb * 128:(mb + 1) * 128], Wr[slot][:, c, :]) for c in range(NC)]

                    def fn(e, pairs=pairs):
                        for i, (l_, r) in enumerate(pairs):
                            ins = e.matmul(psb(5, 256), l_, r, start=(i == 0), stop=(i == NC - 1))
                        return ins
                    op("pe", fn, reads=[("W", slot)] + memh_keys, writes=[("ps", 5)])
                    op("act", lambda e, mb=mb, jj=jj: e.copy(memv[:, mb, jj * 256:(jj + 1) * 256], psb(5, 256)),
                       reads=[("ps", 5)], writes=[("memv", mb, jj)])
        sch.barrier()

        if STOP <= 2:
            finish_dbg()
            return
        lng = view(SCR + 0, F32, [128, 1024])
        lnb = view(SCR + 4096, F32, [128, 1024])
        dma("sp", lambda e, l=l: e.dma_start(out=lng, in_=g["lng_d"][l:l + 1, :].partition_broadcast(128)),
            reads=[], writes=["lng"], key="c0")
        dma("sp", lambda e, l=l: e.dma_start(out=lnb, in_=g["lnb_d"][l:l + 1, :].partition_broadcast(128)),
            reads=[], writes=["lnb"], key="c1")
        gu_t = [view(SCR + 8192 + i * 2048, F32, [128, 512]) for i in range(2)]
        th_t = [view(SCR + 12288 + i * 2048, F32, [128, 512]) for i in range(2)]
        lnt = [view(SCR + 16384 + i * 4096, F32, [128, 1024]) for i in range(2)]
        it = 0
        for j in range(8):
            slot = wload([(g["w_in_d"][l, :, j * 128:(j + 1) * 128], 0, 128),
                          (g["w_in_d"][l, :, 2048 + j * 128:2048 + (j + 1) * 128], 128, 128)])
            for G in range(4):
                bu = it % 2
                bz = 2 + it % 2
                gu = gu_t[it % 2]
                th = th_t[it % 2]
                it += 1
                proj_fm(slot, 0, G, bu)
                proj_fm(slot, 128, G, bz)
                op("act", lambda e, gu=gu, bu=bu: e.activation(gu, psb(bu), AF.Gelu), reads=[("ps", bu)], writes=[("gu", bu)])
                op("act", lambda e, th=th, bz=bz: e.activation(th, psb(bz), AF.Tanh, scale=0.5), reads=[("ps", bz)], writes=[("th", bz)])
                op("dve", lambda e, th=th, bz=bz: e.scalar_tensor_tensor(th, th, 1.0, psb(bz), ALU.add, ALU.mult),
                   reads=[("th", bz), ("ps", bz)], writes=[("th", bz)])
                op("dve", lambda e, th=th, gu=gu, j=j, G=G: e.scalar_tensor_tensor(yT[:, j, G * 512:(G + 1) * 512], th, 0.5, gu, ALU.mult, ALU.mult),
                   reads=[("th", bz), ("gu", bu)], writes=[("yT", j, G)])
        it = 0
        for j in range(4):
            slot = wload([(g["w_in_d"][l, :, 1024 + j * 256:1024 + (j + 1) * 256], 0, 256)])
            for tb in range(16):
                bank = 4 + it % 2
                it += 1
                pairs = [(hT[:, c, tb * 128:(tb + 1) * 128], Wr[slot][:, c, :]) for c in range(NC)]

                def fn(e, pairs=pairs, bank=bank):
                    for i, (l_, r) in enumerate(pairs):
                        ins = e.matmul(psb(bank, 256), l_, r, start=(i == 0), stop=(i == NC - 1))
                    return ins
                op("pe", fn, reads=[("W", slot), ("hT", tb), ("hTa", tb)], writes=[("ps", bank)])
                op("act", lambda e, tb=tb, j=j, bank=bank: e.activation(VA[:, tb, j * 256:(j + 1) * 256], psb(bank, 256), AF.Gelu),
                   reads=[("ps", bank)], writes=[("VA", tb, j)])
        def lnA(tb):
            pz = tb % 2
            st_, mv_, t1_, rs_ = ST2[pz], MV2[pz], T12[pz], RS2[pz]
            for k in range(2):
                op("dve", lambda e, tb=tb, k=k, st_=st_: e.bn_stats(st_[:, k, :], VA[:, tb, k * 512:(k + 1) * 512]),
                   reads=[("VA", tb, 2 * k), ("VA", tb, 2 * k + 1)], writes=[("lst", pz, k)])
            op("dve", lambda e, st_=st_, mv_=mv_: e.bn_aggr(mv_, st_[:, 0:2, :].rearrange("p a b -> p (a b)")),
               reads=[("lst", pz, 0), ("lst", pz, 1)], writes=[("lmv", pz)])
            op("dve", lambda e, mv_=mv_, t1_=t1_: e.tensor_scalar(t1_, mv_[:, 1:2], EPS, None, ALU.add),
               reads=[("lmv", pz)], writes=[("lt1", pz)])
            op("pool", lambda e, t1_=t1_, rs_=rs_: e.tensor_tensor(rs_, t1_, neghalf[:, 0:1], ALU.pow),
               reads=[("lt1", pz)], writes=[("lrs", pz)])

        def lnB(tb):
            pz = tb % 2
            mv_, rs_ = MV2[pz], RS2[pz]
            tmp = lnt[tb % 2]
            op("dve", lambda e, tb=tb, tmp=tmp, mv_=mv_: e.scalar_tensor_tensor(tmp, VA[:, tb, :], mv_[:, 0:1], lng, ALU.subtract, ALU.mult),
               reads=[("VA", tb, j) for j in range(4)] + [("lmv", pz), "lng"], writes=[("lnt", tb % 2)])
            op("dve", lambda e, tb=tb, tmp=tmp, rs_=rs_: e.scalar_tensor_tensor(VA[:, tb, :], tmp, rs_, lnb, ALU.mult, ALU.add),
               reads=[("lnt", tb % 2), ("lrs", pz), "lnb"], writes=[("VA", tb, j) for j in range(4)])

        lnA(0)
        for tb in range(16):
            if tb + 1 < 16:
                lnA(tb + 1)
            lnB(tb)
        it = 0
        for G in range(4):
            for gi in range(8):
                bank = 6 + it % 2
                it += 1

                def fn(e, G=G, gi=gi, bank=bank):
                    for cc in range(4):
                        c = 4 * G + cc
                        o = psb(bank, 128, cc * 128)
                        e.matmul(o, VA[:, c, gi * 128:(gi + 1) * 128], wT[:, gi, :], start=True, stop=False)
                        e.matmul(o, ones0, bhi[:, gi * 128:(gi + 1) * 128], start=False, stop=False)
                        ins = e.matmul(o, ones0, blo[:, gi * 128:(gi + 1) * 128], start=False, stop=True)
                    return ins
                op("pe", fn, reads=[("VA", 4 * G + cc, jj) for cc in range(4) for jj in range(4)] + ["wT", "bhi", "blo", "ones0"],
                   writes=[("ps", bank)])
                op("dve", lambda e, G=G, gi=gi, bank=bank: e.tensor_tensor(yT[:, gi, G * 512:(G + 1) * 512], psb(bank),
                                                                            yT[:, gi, G * 512:(G + 1) * 512], ALU.mult),
                   reads=[("ps", bank), ("yT", gi, G)], writes=[("yT", gi, G)])
        sch.barrier()

        if STOP <= 3:
            finish_dbg()
            return
        qT = view(SCR + 0, BF16, [128, S])
        kT = view(SCR + 4096, BF16, [128, S])
        vh = view(SCR + 8192, BF16, [128, 16, 128])
        zs = view(SCR + 12288, BF16, [128, S])
        E_t = [view(SCR + 16384 + i * 2048, F32, [128, 512]) for i in range(2)]
        sp_t = [view(SCR + 20480 + i * 1024, BF16, [128, 512]) for i in range(3)]
        R_t = [view(SCR + 23552 + i * 1024, BF16, [128, 512]) for i in range(3)]
        A_t = [view(SCR + 26624 + i * 1024, BF16, [128, 512]) for i in range(3)]
        ez_t = [view(SCR + 29696 + i * 2048, F32, [128, 512]) for i in range(2)]
        assert SCR + 33792 <= 211968
        pj = 0
        for h in range(4):
            c_q = 3072 + h * 128
            slot1 = wload([(g["w_in_d"][l, :, c_q:c_q + 128], 0, 128),
                           (g["w_in_d"][l, :, c_q + 512:c_q + 640], 128, 128)])
            for G in range(4):
                b = pj % 2
                pj += 1
                proj_fm(slot1, 0, G, b)
                op("act", lambda e, G=G, b=b: e.activation(qT[:, G * 512:(G + 1) * 512], psb(b), AF.Copy, scale=SCALE),
                   reads=[("ps", b)], writes=[("qT", G)])
                b = pj % 2
                pj += 1
                proj_fm(slot1, 128, G, b)
                op("dve", lambda e, G=G, b=b: e.tensor_copy(kT[:, G * 512:(G + 1) * 512], psb(b)),
                   reads=[("ps", b)], writes=[("kT", G)])
            slot2 = wload()
            for G in range(4):
                b = pj % 2
                pj += 1
                ez = ez_t[G % 2]
                proj_fm(slot2, 128, G, b)
                op("act", lambda e, ez=ez, b=b: e.activation(ez, psb(b), AF.Exp, scale=-1.0), reads=[("ps", b)], writes=[("ez", G % 2)])
                op("act", lambda e, ez=ez: e.activation(ez, ez, AF.Ln, bias=1.0), reads=[("ez", G % 2)], writes=[("ez", G % 2)])
                op("act", lambda e, ez=ez: e.activation(ez, ez, AF.Exp, scale=-1.0), reads=[("ez", G % 2)], writes=[("ez", G % 2)])
                op("dve", lambda e, ez=ez, G=G, b=b: e.tensor_tensor(zs[:, G * 512:(G + 1) * 512], psb(b), ez, ALU.mult),
                   reads=[("ps", b), ("ez", G % 2)], writes=[("zs", G)])
            for t4 in range(4):
                b = pj % 2
                pj += 1

                def fn(e, t4=t4, b=b, slot2=slot2):
                    for tt in range(4):
                        tb = 4 * t4 + tt
                        for c in range(NC):
                            ins = e.matmul(psb(b, 128, tt * 128), hT[:, c, tb * 128:(tb + 1) * 128], Wr[slot2][:, c, 0:128],
                                           start=(c == 0), stop=(c == NC - 1))
                    return ins
                op("pe", fn, reads=[("W", slot2)] + hkeys(t4), writes=[("ps", b)])
                op("act", lambda e, t4=t4, b=b: e.copy(vh[:, 4 * t4:4 * t4 + 4, :], psb(b).rearrange("p (a b) -> p a b", a=4)),
                   reads=[("ps", b)], writes=[("vh", t4)])
            units = []
            for G in range(4):
                nb = 4 * G + 4
                for k_, bidx in enumerate(range(nb - 1, -1, -1)):
                    i_d = bidx - 4 * G
                    c0 = 128 * i_d if i_d >= 0 else 0
                    units.append(dict(G=G, bidx=bidx, diag=(i_d >= 0), c0=c0, ncol=512 - c0, first=(k_ == 0), last=(bidx == 0)))
            nU = len(units)

            def S1(u):
                U = units[u]
                G, bidx, diag, c0, ncol = U["G"], U["bidx"], U["diag"], U["c0"], U["ncol"]
                zb = 2 + u % 4
                E = E_t[u % 2]
                sp = sp_t[u % 3]
                kblk = kT[:, bidx * 128:(bidx + 1) * 128]
                qcols = qT[:, G * 512 + c0:G * 512 + 512]

                def fz(e):
                    ins = e.matmul(psb(zb, ncol), kblk, qcols, start=True, stop=not diag)
                    if diag:
                        ins = e.matmul(psb(zb, 128), ident_bf, negmask, start=False, stop=True)
                    return ins
                op("pe", fz, reads=[("kT", bidx // 4), ("qT", G), "ident", "negmask"], writes=[("ps", zb)])
                op("act", lambda e: e.activation(E[:, 0:ncol], psb(zb, ncol), AF.Exp), reads=[("ps", zb)], writes=[("E", u % 2)])

            def S1b(u):
                ncol = units[u]["ncol"]
                E = E_t[u % 2]
                sp = sp_t[u % 3]
                op("act", lambda e: e.activation(sp[:, 0:ncol], E[:, 0:ncol], AF.Ln, bias=1.0),
                   reads=[("E", u % 2)], writes=[("sp", u % 3)])

            def S2(u):
                U = units[u]
                G, bidx, diag, c0, ncol, first = U["G"], U["bidx"], U["diag"], U["c0"], U["ncol"], U["first"]
                zb = 2 + u % 4
                sp = sp_t[u % 3]
                A = A_t[u % 3]
                Rcur = R_t[u % 3]
                Rnext = R_t[(u + 1) % 3]

                kblk2 = kT[:, bidx * 128:(bidx + 1) * 128]
                qcols2 = qT[:, G * 512 + c0:G * 512 + 512]

                def fp(e):
                    if os.environ.get("KZ"):
                        e.matmul(psb(zb, ncol), kblk2, qcols2, start=True, stop=not diag)
                        if diag:
                            e.matmul(psb(zb, 128), ident_bf, negmask, start=False, stop=True)
                    ins = e.matmul(psb(zb, ncol), negU, sp[:, 0:ncol], start=False, stop=True, skip_group_check=True)
                    if not first:
                        if diag:
                            ins = e.matmul(psb(zb, ncol - 128, 128), negOnes, Rcur[:, c0 + 128:512], start=False, stop=True,
                                           skip_group_check=True)
                        else:
                            ins = e.matmul(psb(zb, 512), negOnes, Rcur, start=False, stop=True, skip_group_check=True)
                    return ins
                op("pe", fp, reads=[("sp", u % 3), ("R", u % 3), "negU", "negones"], writes=[("ps", zb)])
                if not U["last"]:
                    if first:
                        op("pool", lambda e: e.tensor_copy(Rnext[:, c0:512], sp[:, 0:ncol]),
                           reads=[("sp", u % 3)], writes=[("R", (u + 1) % 3)])
                    elif diag:
                        op("pool", lambda e: e.tensor_copy(Rnext[:, c0:c0 + 128], sp[:, 0:128]),
                           reads=[("sp", u % 3)], writes=[("R", (u + 1) % 3)])
                        op("pool", lambda e: e.tensor_tensor(Rnext[:, c0 + 128:512], Rcur[:, c0 + 128:512], sp[:, 128:ncol], ALU.add),
                           reads=[("sp", u % 3), ("R", u % 3)], writes=[("R", (u + 1) % 3)])
                    else:
                        op("pool", lambda e: e.tensor_tensor(Rnext, Rcur, sp, ALU.add),
                           reads=[("sp", u % 3), ("R", u % 3)], writes=[("R", (u + 1) % 3)])

            def S2b(u):
                ncol = units[u]["ncol"]
                zb = 2 + u % 4
                A = A_t[u % 3]
                op("act", lambda e: e.activation(A[:, 0:ncol], psb(zb, ncol), AF.Exp), reads=[("ps", zb)], writes=[("A", u % 3)])

            def S3(u, h=h):
                U = units[u]
                G, bidx, diag, c0, ncol, first, last = U["G"], U["bidx"], U["diag"], U["c0"], U["ncol"], U["first"], U["last"]
                yb = 6 + G % 2
                A = A_t[u % 3]
                vblk = vh[:, bidx, :]

                def fy(e):
                    if first:
                        ins = e.matmul(psb(yb, ncol, c0), vblk, A[:, 0:ncol], start=True, stop=True)
                    elif diag:
                        e.matmul(psb(yb, 128, c0), vblk, A[:, 0:128], start=False, stop=True, skip_group_check=True)
                        ins = e.matmul(psb(yb, ncol - 128, c0 + 128), vblk, A[:, 128:ncol], start=False, stop=True,
                                       skip_group_check=True)
                    else:
                        ins = e.matmul(psb(yb, 512), vblk, A, start=False, stop=True, skip_group_check=True)
                    return ins
                op("pe", fy, reads=[("vh", bidx // 4), ("A", u % 3)], writes=[("ps", yb)])
                if last:
                    op("dve", lambda e: e.tensor_tensor(yT[:, 8 + h, G * 512:(G + 1) * 512], psb(yb),
                                                        zs[:, G * 512:(G + 1) * 512], ALU.mult),
                       reads=[("ps", yb), ("zs", G)], writes=[("yT", 8 + h, G)])

            for i in range(nU + 3):
                if i < nU:
                    S1(i)
                if 0 <= i - 1 < nU:
                    S2(i - 1)
                if 0 <= i - 2 < nU:
                    S2b(i - 2)
                if i < nU:
                    S1b(i)
                if 0 <= i - 3 < nU:
                    S3(i - 3)
        sch.barrier()

        if STOP <= 4:
            finish_dbg()
            return
        def cbuf(p_):
            o = SCR + p_ * 15360
            return dict(sq=view(o, BF16, [128, 512]), tq=view(o + 1024, F32, [128, 512]), rq=view(o + 3072, F32, [128, 512]),
                        qh=view(o + 5120, BF16, [128, 512]), ez=view(o + 6144, F32, [128, 512]), zsc=view(o + 8192, BF16, [128, 512]),
                        PT=[view(o + 9216 + i * 1024, BF16, [128, 512]) for i in range(2)],
                        rd=view(o + 11264, F32, [128, 512]), t2=view(o + 13312, F32, [128, 512]))
        CB = [cbuf(0), cbuf(1)]
        assert SCR + 2 * 15360 <= 211968
        cits = [(h, G) for h in range(4) for G in range(4)]
        cslot = {}

        def c_projq(it):
            h, G = cits[it]
            if G == 0:
                cslot[h] = wload()
            proj_fm(cslot[h], 0, G, it % 2)

        def c_projz(it):
            h, G = cits[it]
            proj_fm(cslot[h], 128, G, 2 + it % 2)

        c_projq(0)
        c_projz(0)
        for it, (h, G) in enumerate(cits):
            p_ = it % 2
            cb = CB[p_]
            qb = it % 2
            zb = 2 + it % 2
            sq, tq, rq, qh, ez, zsc, PT, rd, t2 = (cb["sq"], cb["tq"], cb["rq"], cb["qh"], cb["ez"], cb["zsc"], cb["PT"], cb["rd"], cb["t2"])
            K = lambda n_: (n_, p_)
            op("act", lambda e, sq=sq, qb=qb: e.activation(sq, psb(qb), AF.Square), reads=[("ps", qb)], writes=[K("sq")])
            op("pe", lambda e, sq=sq: e.matmul(psb(6), ones_bf, sq, start=True, stop=True), reads=[K("sq"), "ones"], writes=[("ps", 6)])
            if it + 1 < len(cits):
                c_projq(it + 1)
            op("act", lambda e, tq=tq: e.activation(tq, psb(6), AF.Ln, bias=EPS, scale=1.0 / 128.0), reads=[("ps", 6)], writes=[K("tq")])
            op("act", lambda e, tq=tq, rq=rq: e.activation(rq, tq, AF.Exp, scale=-0.5), reads=[K("tq")], writes=[K("rq")])
            op("dve", lambda e, qh=qh, rq=rq, qb=qb: e.tensor_tensor(qh, psb(qb), rq, ALU.mult),
               reads=[("ps", qb), K("sq"), K("rq")], writes=[K("qh")])
            op("act", lambda e, ez=ez, zb=zb: e.activation(ez, psb(zb), AF.Exp, scale=-1.0), reads=[("ps", zb)], writes=[K("ez")])
            op("act", lambda e, ez=ez: e.activation(ez, ez, AF.Ln, bias=1.0), reads=[K("ez")], writes=[K("ez")])
            op("act", lambda e, ez=ez: e.activation(ez, ez, AF.Exp, scale=-1.0), reads=[K("ez")], writes=[K("ez")])
            op("dve", lambda e, zsc=zsc, ez=ez, zb=zb: e.tensor_tensor(zsc, psb(zb), ez, ALU.mult), reads=[("ps", zb), K("ez")], writes=[K("zsc")])
            for mb in range(2):
                op("pe", lambda e, mb=mb, h=h, qh=qh: e.matmul(psb(4 + mb), knT[:, h, mb * 128:(mb + 1) * 128], qh, start=True, stop=True),
                   reads=[("knT", h), K("qh")], writes=[("ps", 4 + mb)])
            if it + 1 < len(cits):
                c_projz(it + 1)
            for mb in range(2):
                op("act", lambda e, mb=mb, PT=PT: e.activation(PT[mb], psb(4 + mb), AF.Exp), reads=[("ps", 4 + mb)], writes=[K("PT%d" % mb)])

            def fo(e, h=h, PT=PT):
                e.matmul(psb(7), memv[:, 0, h * 128:(h + 1) * 128], PT[0], start=True, stop=False)
                e.matmul(psb(7), memv[:, 1, h * 128:(h + 1) * 128], PT[1], start=False, stop=True)
                e.matmul(psb(6), ones_bf, PT[0], start=True, stop=False)
                return e.matmul(psb(6), ones_bf, PT[1], start=False, stop=True)
            op("pe", fo, reads=[K("PT0"), K("PT1"), "ones"] + [("memv", mb, jj) for mb in range(2) for jj in range(2)],
               writes=[("ps", 7), ("ps", 6)])
            op("dve", lambda e, rd=rd: e.reciprocal(rd, psb(6)), reads=[("ps", 6)], writes=[K("rd")])
            op("dve", lambda e, t2=t2, rd=rd: e.tensor_tensor(t2, psb(7), rd, ALU.mult), reads=[("ps", 7), K("rd")], writes=[K("t2")])
            op("dve", lambda e, h=h, G=G, t2=t2, zsc=zsc: e.tensor_tensor(yT[:, 12 + h, G * 512:(G + 1) * 512], t2, zsc, ALU.mult),
               reads=[K("t2"), K("zsc")], writes=[("yT", 12 + h, G)])
        sch.barrier()

        if dbg and l == int(os.environ.get("KL", "0")):
            finish_dbg()
        if STOP <= 5:
            return

        NXR = 6
        xr_t = [view(HT_OFF + i * 1024, F32, [128, 256]) for i in range(NXR)]
        os_t = [view(HT_OFF + 8192 + i * 1024, F32, [128, 256]) for i in range(4)]
        p3 = [(n, tb) for n in range(8) for tb in range(16)]

        def p3_load(k, src_ap=src_ap):
            n, tb = p3[k]
            xr = xr_t[k % NXR]
            dma("sp", lambda e: e.dma_start(out=xr, in_=src_ap[tb * 128:(tb + 1) * 128, n * 256:(n + 1) * 256]),
                reads=[], writes=[("xr", k % NXR)], key=("xr", k % NXR))

        for k in range(3):
            p3_load(k)
        slot = None
        for k, (n, tb) in enumerate(p3):
            if tb == 0:
                slot = wload()
            if k + 3 < len(p3):
                p3_load(k + 3)
            bank = k % 8
            xr = xr_t[k % NXR]
            ost = os_t[k % 4]
            pairs = [(yT[:, c, tb * 128:(tb + 1) * 128], Wr[slot][:, c, :]) for c in range(NC)]

            def fn(e, pairs=pairs, bank=bank):
                for i, (l_, r) in enumerate(pairs):
                    ins = e.matmul(psb(bank, 256), l_, r, start=(i == 0), stop=(i == NC - 1))
                return ins
            op("pe", fn, reads=[("W", slot)], writes=[("ps", bank)])
            op("dve", lambda e, ost=ost, xr=xr, bank=bank: e.tensor_tensor(ost, psb(bank, 256), xr, ALU.add),
               reads=[("ps", bank), ("xr", k % NXR)], writes=[("os", k % 4)])
            dma("sp", lambda e, ost=ost, tb=tb, n=n, dst_ap=dst_ap: e.dma_start(out=dst_ap[tb * 128:(tb + 1) * 128, n * 256:(n + 1) * 256], in_=ost),
                reads=[("os", k % 4)], writes=[("xdst", tb, n)], key=("os", k % 4))
        sch.barrier()


_NC_CACHE = {}


def kernel(x, mem, norm_g, w_in, sgu_ln_g, sgu_ln_b, sgu_w, sgu_b, mem_norm_g, w_mem_kv, q_norm_g, k_norm_g, w_out):
    if "nc" not in _NC_CACHE:
        _NC_CACHE["nc"] = build_nc()
    nc = _NC_CACHE["nc"]
    f = lambda a: np.ascontiguousarray(np.asarray(a, dtype=np.float32))
    shared = {"norm_g": f(norm_g), "w_in": f(w_in), "sgu_ln_g": f(sgu_ln_g), "sgu_ln_b": f(sgu_ln_b),
              "sgu_w": f(sgu_w), "sgu_b": f(sgu_b), "mem_norm_g": f(mem_norm_g), "w_mem_kv": f(w_mem_kv),
              "q_norm_g": f(q_norm_g), "k_norm_g": f(k_norm_g), "w_out": f(w_out)}
    x = f(x)
    mem = f(mem)
    in_maps = [dict(shared, x=x[b], mem=mem[b]) for b in range(8)]
    res = run_bass_kernel_spmd(nc, in_maps, core_ids=list(range(8)))
    return np.stack([np.asarray(r["out"], dtype=np.float32) for r in res.results], axis=0)
```

```python
import math
import numpy as np
import concourse.bass as bass
import concourse.mybir as mybir
from concourse.bass_utils import run_bass_kernel_spmd

F32 = mybir.dt.float32
BF16 = mybir.dt.bfloat16
U8 = mybir.dt.uint8
AF = mybir.ActivationFunctionType
ALU = mybir.AluOpType

S = 2048
D = 2048
MEM = 256
NL = 2
INW = 6144
EPS = 1e-6
NEG = -30000.0
NC = 16
SCALE = 1.0 / math.sqrt(128.0)

ENGS = ("pe", "act", "dve", "pool", "sp")


class Sched:
    def __init__(self, nc, esems, dma_sems):
        self.nc = nc
        self.prog = {e: [] for e in ENGS}
        self.cnt = {e: 0 for e in ENGS}
        self.esem = esems
        self.free_dsem = list(dma_sems)
        self.dsem = {}
        self.seen = {e: {} for e in ENGS}
        self.lastw = {}
        self.readers = {}

    def _waits(self, eng, reads, writes):
        toks = []
        for r in reads:
            t = self.lastw.get(r)
            if t is not None:
                toks.append(("raw", t))
        for w in writes:
            t = self.lastw.get(w)
            if t is not None:
                toks.append(("waw", t))
            for t in self.readers.get(w, ()):
                toks.append(("war", t))
        waits = []
        seen = self.seen[eng]
        for kind, (key, val, peng) in toks:
            if peng == eng and eng == "pe":
                continue
            if seen.get(key, 0) >= val:
                continue
            seen[key] = val
            waits.append((key, val))
        return waits

    def _commit(self, tok, reads, writes):
        for r in reads:
            self.readers.setdefault(r, []).append(tok)
        for w in writes:
            self.lastw[w] = tok
            self.readers[w] = []

    budget = None

    def _drop(self):
        if self.budget is None:
            return False
        if self.budget <= 0:
            return True
        self.budget -= 1
        return False

    def op(self, eng, fn, reads=(), writes=()):
        if self._drop():
            return
        waits = self._waits(eng, reads, writes)
        self.cnt[eng] += 1
        tok = (eng, self.cnt[eng], eng)
        self.prog[eng].append((waits, fn, eng, 1))
        self._commit(tok, reads, writes)

    def dma(self, queue, fn, reads, writes, key, force=False):
        if not force and self._drop():
            return (("d", key), 0, "dma")
        waits = self._waits(queue, reads, writes)
        if key not in self.dsem:
            self.dsem[key] = [self.free_dsem.pop(), 0]
        self.dsem[key][1] += 16
        tok = (("d", key), self.dsem[key][1], "dma")
        self.prog[queue].append((waits, fn, ("d", key), 16))
        self._commit(tok, reads, writes)
        return tok

    def semh(self, key):
        if isinstance(key, tuple) and key[0] == "d":
            return self.dsem[key[1]][0]
        return self.esem[key]

    def barrier(self):
        for e in ENGS:
            waits = []
            for o in ENGS:
                if o != e and self.cnt[o] > 0 and self.seen[e].get(o, 0) < self.cnt[o]:
                    self.seen[e][o] = self.cnt[o]
                    waits.append((o, self.cnt[o]))
            for k, (h, c) in self.dsem.items():
                kk = ("d", k)
                if c > 0 and self.seen[e].get(kk, 0) < c:
                    self.seen[e][kk] = c
                    waits.append((kk, c))
            if waits:
                self.prog[e].append((waits, None, None, 0))
        self.lastw = {}
        self.readers = {}

    def replay(self, eng, e):
        for waits, fn, skey, inc in self.prog[eng]:
            for key, val in waits:
                e.wait_ge(self.semh(key), val)
            if fn is None:
                continue
            ins = fn(e)
            ins.then_inc(self.semh(skey), inc)


def build_nc(n_layers=NL, dbg=False):
    nc = bass.Bass("TRN2", target_bir_lowering=False)
    x_d = nc.dram_tensor("x", [S, D], F32, kind="ExternalInput").ap()
    mem_d = nc.dram_tensor("mem", [MEM, D], F32, kind="ExternalInput").ap()
    norm_g_d = nc.dram_tensor("norm_g", [NL, D], F32, kind="ExternalInput").ap()
    w_in_d = nc.dram_tensor("w_in", [NL, D, INW], F32, kind="ExternalInput").ap()
    lng_d = nc.dram_tensor("sgu_ln_g", [NL, 1024], F32, kind="ExternalInput").ap()
    lnb_d = nc.dram_tensor("sgu_ln_b", [NL, 1024], F32, kind="ExternalInput").ap()
    sguw_d = nc.dram_tensor("sgu_w", [NL, 8, 128, 128], F32, kind="ExternalInput").ap()
    sgub_d = nc.dram_tensor("sgu_b", [NL, 8, 128], F32, kind="ExternalInput").ap()
    mng_d = nc.dram_tensor("mem_norm_g", [NL, D], F32, kind="ExternalInput").ap()
    wkv_d = nc.dram_tensor("w_mem_kv", [NL, D, 1024], F32, kind="ExternalInput").ap()
    qg_d = nc.dram_tensor("q_norm_g", [NL, 128], F32, kind="ExternalInput").ap()
    kg_d = nc.dram_tensor("k_norm_g", [NL, 128], F32, kind="ExternalInput").ap()
    wout_d = nc.dram_tensor("w_out", [NL, D, D], F32, kind="ExternalInput").ap()
    out_d = nc.dram_tensor("out", [S, D], F32, kind="ExternalOutput").ap()
    x1_d = nc.dram_tensor("x1_scratch", [S, D], F32).ap()
    if dbg:
        dbg_h = nc.dram_tensor("dbg_h", [128, NC * S], BF16, kind="ExternalOutput").ap()
        dbg_y = nc.dram_tensor("dbg_y", [128, NC * S], BF16, kind="ExternalOutput").ap()

    ARENA = 211968
    arena = nc.alloc_sbuf_tensor("arena", [128, ARENA], U8)
    ps = [nc.alloc_psum_tensor("ps%d" % i, [128, 512], F32) for i in range(8)]

    def view(off, dt, shape):
        esz = 4 if dt == F32 else 2
        n = 1
        for s_ in shape[1:]:
            n *= s_
        ap = arena[0:shape[0], off:off + n * esz].bitcast(dt)
        if len(shape) == 3:
            ap = ap.rearrange("p (a b) -> p a b", a=shape[1])
        return ap

    HT_OFF = 0
    YT_OFF = 65536
    W_OFF = 131072
    C_OFF = W_OFF + 3 * 8192
    SCR = C_OFF + 14592
    assert SCR + 32768 <= ARENA

    hT = view(HT_OFF, BF16, [128, NC, S])
    yT = view(YT_OFF, BF16, [128, NC, S])
    VA = view(YT_OFF + 32768, BF16, [128, 16, 1024])
    Wr = [view(W_OFF + i * 8192, BF16, [128, NC, 256]) for i in range(3)]
    ident_bf = view(C_OFF + 0, BF16, [128, 128])
    negU = view(C_OFF + 256, BF16, [128, 128])
    negOnes = view(C_OFF + 512, BF16, [128, 128])
    ones_bf = view(C_OFF + 768, BF16, [128, 128])
    negmask = view(C_OFF + 1024, BF16, [128, 128])
    ident_f = view(C_OFF + 1280, F32, [128, 128])
    neghalf = view(C_OFF + 1792, F32, [128, 512])
    kgq = view(C_OFF + 3840, F32, [128, 1])
    qg = view(C_OFF + 3844, F32, [128, 1])
    kg = view(C_OFF + 3848, F32, [128, 1])
    st = view(C_OFF + 3872, F32, [128, 4, 6])
    mv = view(C_OFF + 3968, F32, [128, 2])
    t1 = view(C_OFF + 3976, F32, [128, 1])
    rstd = view(C_OFF + 3980, F32, [128, 1])
    ST2 = [st, view(C_OFF + 3984, F32, [128, 4, 6])]
    MV2 = [mv, view(C_OFF + 4080, F32, [128, 2])]
    T12 = [t1, view(C_OFF + 4088, F32, [128, 1])]
    RS2 = [rstd, view(C_OFF + 4092, F32, [128, 1])]
    knT = view(C_OFF + 4096, BF16, [128, 4, 256])
    memv = view(C_OFF + 6144, BF16, [128, 2, 512])
    wT = view(C_OFF + 8192, BF16, [128, 8, 128])
    bhi = view(C_OFF + 10240, BF16, [128, 1024])
    blo = view(C_OFF + 12288, BF16, [128, 1024])
    ones0 = view(C_OFF + 14336, BF16, [128, 128])
    XT = [view(YT_OFF + i * 8192, F32, [128, D]) for i in range(3)]
    XN = [view(YT_OFF + 24576 + i * 4096, BF16, [128, D]) for i in range(2)]
    GB = view(YT_OFF + 32768, F32, [128, D])
    MGB = view(YT_OFF + 40960, F32, [128, D])
    XS = [view(YT_OFF + 49152 + i * 8192, F32, [128, D]) for i in range(2)]

    with nc.semaphore("s_pe") as s_pe, nc.semaphore("s_act") as s_act, \
            nc.semaphore("s_dve") as s_dve, nc.semaphore("s_pool") as s_pool, \
            nc.semaphore("s_sp") as s_sp:
        import contextlib
        with contextlib.ExitStack() as es:
            dma_sems = [es.enter_context(nc.semaphore("dsem%d" % i)) for i in range(24)]
            sch = Sched(nc, {"pe": s_pe, "act": s_act, "dve": s_dve, "pool": s_pool, "sp": s_sp}, dma_sems)
            _emit_program(nc, sch, locals(), n_layers, dbg)
            with nc.Block() as block:
                @block.tensor
                def _(e):
                    sch.replay("pe", e)

                @block.scalar
                def _(e):
                    sch.replay("act", e)

                @block.vector
                def _(e):
                    sch.replay("dve", e)

                @block.gpsimd
                def _(e):
                    sch.replay("pool", e)

                @block.sync
                def _(e):
                    sch.replay("sp", e)
    return nc


def _emit_program(nc, sch, V, n_layers, dbg):
    import os
    STOP = int(os.environ.get("KSTOP", "99")) if dbg else 99
    g = V
    ps = g["ps"]
    hT, yT, VA, Wr = g["hT"], g["yT"], g["VA"], g["Wr"]
    view = g["view"]
    SCR = g["SCR"]
    HT_OFF = g["HT_OFF"]
    ident_bf, negU, negOnes, ones_bf, negmask, ident_f, neghalf = (
        g["ident_bf"], g["negU"], g["negOnes"], g["ones_bf"], g["negmask"], g["ident_f"], g["neghalf"])
    kgq, qg, kg, st, mv, t1, rstd = g["kgq"], g["qg"], g["kg"], g["st"], g["mv"], g["t1"], g["rstd"]
    knT, memv, wT, bhi, blo, ones0 = g["knT"], g["memv"], g["wT"], g["bhi"], g["blo"], g["ones0"]
    XT, XN, GB, MGB = g["XT"], g["XN"], g["GB"], g["MGB"]
    ST2, MV2, T12, RS2 = g["ST2"], g["MV2"], g["T12"], g["RS2"]
    XS = g["XS"]
    op, dma = sch.op, sch.dma

    def psb(i, n=512, off=0):
        return ps[i][:, off:off + n]

    def ps_bf(i):
        return ps[i][:, :].bitcast(BF16)

    op("pool", lambda e: e.memset(ones_bf, 1.0), writes=["ones"])
    op("pool", lambda e: e.memset(negOnes, -1.0), writes=["negones"])
    op("pool", lambda e: e.memset(neghalf, -0.5), writes=["neghalf"])
    op("pool", lambda e: e.memset(negmask, NEG), writes=["negmask"])
    op("pool", lambda e: e.memset(ident_f, 1.0), writes=["identf"])
    op("pool", lambda e: e.affine_select(ident_bf, ones_bf, [[-1, 128]], ALU.is_equal, 0.0, base=0, channel_multiplier=1),
       reads=["ones"], writes=["ident"])
    op("pool", lambda e: e.affine_select(ident_f, ident_f, [[-1, 128]], ALU.is_equal, 0.0, base=0, channel_multiplier=1),
       reads=["identf"], writes=["identf"])
    op("pool", lambda e: e.affine_select(negU, negOnes, [[-1, 128]], ALU.is_ge, 0.0, base=0, channel_multiplier=1),
       reads=["negones"], writes=["negU"])
    op("pool", lambda e: e.affine_select(negmask, negmask, [[-1, 128]], ALU.is_ge, 0.0, base=0, channel_multiplier=1),
       reads=["negmask"], writes=["negmask"])
    op("pool", lambda e: e.memset(bhi, 0.0), writes=["bhi"])
    op("pool", lambda e: e.memset(blo, 0.0), writes=["blo"])
    op("pool", lambda e: e.affine_select(ones0, ones_bf, [[0, 128]], ALU.is_equal, 0.0, base=0, channel_multiplier=1),
       reads=["ones"], writes=["ones0"])
    sch.barrier()

    wlist = []
    for l_ in range(n_layers):
        for j in range(4):
            wlist.append([(g["wkv_d"][l_, :, j * 256:(j + 1) * 256], 0, 256)])
        for j in range(4):
            wlist.append([(g["w_in_d"][l_, :, 1024 + j * 256:1024 + (j + 1) * 256], 0, 256)])
        for j in range(8):
            wlist.append([(g["w_in_d"][l_, :, j * 128:(j + 1) * 128], 0, 128),
                          (g["w_in_d"][l_, :, 2048 + j * 128:2048 + (j + 1) * 128], 128, 128)])
        for h_ in range(4):
            cq = 3072 + h_ * 128
            wlist.append([(g["w_in_d"][l_, :, cq:cq + 128], 0, 128), (g["w_in_d"][l_, :, cq + 512:cq + 640], 128, 128)])
            wlist.append([(g["w_in_d"][l_, :, cq + 1024:cq + 1152], 0, 128), (g["w_in_d"][l_, :, cq + 1536:cq + 1664], 128, 128)])
        for h_ in range(4):
            cq = 5120 + h_ * 128
            wlist.append([(g["w_in_d"][l_, :, cq:cq + 128], 0, 128), (g["w_in_d"][l_, :, cq + 512:cq + 640], 128, 128)])
        for n_ in range(8):
            wlist.append([(g["wout_d"][l_, :, n_ * 256:(n_ + 1) * 256], 0, 256)])
    wstate = {"issued": 0, "used": 0}

    def wissue_upto(k):
        while wstate["issued"] < min(k, len(wlist)):
            idx = wstate["issued"]
            wstate["issued"] += 1
            i = idx % 3
            for kk, (src, c0, ncols) in enumerate(wlist[idx]):
                dst = Wr[i][:, :, c0:c0 + ncols]
                s_ = src.rearrange("(c p) f -> p c f", p=128)
                tok = dma("pool", (lambda e, dst=dst, s_=s_: e.dma_start(out=dst, in_=s_)),
                          reads=[], writes=([("W", i)] if kk == 0 else []), key=("W", i))
            sch.lastw[("W", i)] = tok

    def wload(parts=None):
        k = wstate["used"]
        wstate["used"] += 1
        wissue_upto(k + 3)
        return k % 3

    def hkeys(G):
        return [("hT", tb) for tb in range(4 * G, 4 * G + 4)] + [("hTa", tb) for tb in range(4 * G, 4 * G + 4)]

    def proj_fm(slot, c0, G, bank):
        pairs = [(Wr[slot][:, c, c0:c0 + 128], hT[:, c, G * 512:(G + 1) * 512]) for c in range(NC)]

        def fn(e, pairs=pairs, bank=bank):
            for i, (l, r) in enumerate(pairs):
                ins = e.matmul(psb(bank), l, r, start=(i == 0), stop=(i == NC - 1))
            return ins
        op("pe", fn, reads=[("W", slot)] + hkeys(G), writes=[("ps", bank)])

    def norm_transpose(src_ap, gbt, dst_fn, nblk, tag):
        def stA(tb):
            xs = tb % 3
            pz = tb % 2
            xt = XT[xs]
            st_, mv_, t1_ = ST2[pz], MV2[pz], T12[pz]
            dma("sp", (lambda e, xt=xt, tb=tb: e.dma_start(out=xt, in_=src_ap[tb * 128:(tb + 1) * 128, :])),
                reads=[(tag + "src", tb)], writes=[("xt", xs)], key=("xt", xs))
            for k in range(4):
                op("dve", (lambda e, xt=xt, k=k, st_=st_: e.bn_stats(st_[:, k, :], xt[:, k * 512:(k + 1) * 512])),
                   reads=[("xt", xs)], writes=[("st", pz, k)])
            op("dve", lambda e, st_=st_, mv_=mv_: e.bn_aggr(mv_, st_.rearrange("p a b -> p (a b)")),
               reads=[("st", pz, k) for k in range(4)], writes=[("mv", pz)])
            op("dve", lambda e, mv_=mv_, t1_=t1_: e.scalar_tensor_tensor(t1_, mv_[:, 0:1], mv_[:, 0:1], mv_[:, 1:2], ALU.mult, ALU.add),
               reads=[("mv", pz)], writes=[("t1", pz)])
            op("dve", lambda e, t1_=t1_: e.tensor_scalar(t1_, t1_, EPS, None, ALU.add), reads=[("t1", pz)], writes=[("t1", pz)])

        def stR(tb):
            pz = tb % 2
            t1_, rs_ = T12[pz], RS2[pz]
            op("act", lambda e, t1_=t1_: e.activation(t1_, t1_, AF.Ln), reads=[("t1", pz)], writes=[("t1", pz)])
            op("act", lambda e, t1_=t1_, rs_=rs_: e.activation(rs_, t1_, AF.Exp, scale=-0.5), reads=[("t1", pz)], writes=[("rstd", pz)])

        def stB1(tb):
            ns = tb % 2
            xn = XN[ns]
            xt = XT[tb % 3]
            rs_ = RS2[ns]
            op("dve", (lambda e, xn=xn, xt=xt, rs_=rs_: e.scalar_tensor_tensor(xn, xt, rs_, gbt, ALU.mult, ALU.mult)),
               reads=[("xt", tb % 3), ("rstd", ns), tag + "gb"], writes=[("xn", ns)])
            b0 = 2 * (tb % 2)
            for half in range(2):
                bank = b0 + half

                def fn(e, xn=xn, bank=bank, half=half):
                    pb = ps_bf(bank)
                    for cc in range(8):
                        c = half * 8 + cc
                        ins = e.transpose(pb[:, cc * 128:(cc + 1) * 128], xn[:, c * 128:(c + 1) * 128], ident_bf)
                    return ins
                op("pe", fn, reads=[("xn", ns), "ident"], writes=[("ps", bank)])

        def stB2(tb):
            b0 = 2 * (tb % 2)
            for half in range(2):
                bank = b0 + half
                dst, keys = dst_fn(tb, half)
                src3 = ps_bf(bank).rearrange("p (a b) -> p a b", a=8)
                op("act", (lambda e, dst=dst, src3=src3: e.copy(dst, src3)), reads=[("ps", bank)], writes=keys)

        for tb in range(min(2, nblk)):
            stA(tb)
            stR(tb)
        for tb in range(nblk + 1):
            if tb < nblk:
                stB1(tb)
            if 0 <= tb - 1 < nblk:
                stB2(tb - 1)
            if tb + 2 < nblk:
                stA(tb + 2)
                stR(tb + 2)

    def finish_dbg():
        sch.budget = None
        for c in range(NC):
            dma("sp", lambda e, c=c: e.dma_start(out=g["dbg_h"][:, c * S:(c + 1) * S], in_=hT[:, c, :]), reads=[], writes=["dbgh"], key="o0")
            dma("sp", lambda e, c=c: e.dma_start(out=g["dbg_y"][:, c * S:(c + 1) * S], in_=yT[:, c, :]), reads=[], writes=["dbgy"], key="o1")
        sch.barrier()

    if STOP <= 0:
        finish_dbg()
        return
    for l in range(n_layers):
        src_ap = g["x_d"] if l == 0 else g["out_d"]
        dst_ap = g["out_d"]
        dma("sp", lambda e, l=l: e.dma_start(out=GB, in_=g["norm_g_d"][l:l + 1, :].partition_broadcast(128)),
            reads=[], writes=["xgb"], key="c0")
        dma("sp", lambda e, l=l: e.dma_start(out=MGB, in_=g["mng_d"][l:l + 1, :].partition_broadcast(128)),
            reads=[], writes=["mgb"], key="c1")
        dma("sp", lambda e, l=l: e.dma_start(out=qg, in_=g["qg_d"][l:l + 1, :].rearrange("a p -> p a")),
            reads=[], writes=["qg"], key="c2")
        dma("sp", lambda e, l=l: e.dma_start(out=kg, in_=g["kg_d"][l:l + 1, :].rearrange("a p -> p a")),
            reads=[], writes=["kg"], key="c3")
        sguw = view(SCR, F32, [128, 8, 128])
        dma("sp", lambda e, l=l, sguw=sguw: e.dma_start(out=sguw, in_=g["sguw_d"][l].rearrange("g t s -> t g s")),
            reads=[], writes=["sguw"], key="c4")
        bf_ = view(SCR + 4096, F32, [1, 1024])
        bl_ = view(SCR + 8192, F32, [1, 1024])
        dma("sp", lambda e, l=l, bf_=bf_: e.dma_start(out=bf_, in_=g["sgub_d"][l:l + 1].rearrange("a g t -> a (g t)")),
            reads=[], writes=["bf"], key="c5")
        op("dve", lambda e: e.scalar_tensor_tensor(kgq, qg, SCALE, kg, ALU.mult, ALU.mult), reads=["qg", "kg"], writes=["kgq"])
        op("dve", lambda e, bf_=bf_: e.tensor_copy(bhi[0:1, :], bf_), reads=["bf"], writes=["bhi"])
        op("dve", lambda e, bf_=bf_, bl_=bl_: e.tensor_tensor(bl_, bf_, bhi[0:1, :], ALU.subtract), reads=["bf", "bhi"], writes=["bl"])
        op("dve", lambda e, bl_=bl_: e.tensor_copy(blo[0:1, :], bl_), reads=["bl"], writes=["blo"])
        wtf = view(SCR + 12288, F32, [128, 128])
        for gi in range(8):
            op("pe", lambda e, gi=gi, sguw=sguw: e.transpose(psb(7, 128), sguw[:, gi, :], ident_f),
               reads=["sguw", "identf"], writes=[("ps", 7)])
            op("act", lambda e, wtf=wtf: e.copy(wtf, psb(7, 128)), reads=[("ps", 7)], writes=["wtf"])
            op("pool", lambda e, gi=gi, wtf=wtf: e.affine_select(wT[:, gi, :], wtf, [[1, 128]], ALU.is_ge, 0.0,
                                                                  base=0, channel_multiplier=-1),
               reads=["wtf"], writes=["wT"])

        def dst_h(tb, half):
            return hT[:, half * 8:(half + 1) * 8, tb * 128:(tb + 1) * 128], [("hT", tb)] if half == 1 else [("hTa", tb)]
        norm_transpose(src_ap, GB, dst_h, 16, "x")

        if STOP <= 1:
            sch.barrier()
            finish_dbg()
            return
        if dbg and "KBUD" in os.environ:
            sch.budget = int(os.environ["KBUD"])
        memh = view(SCR + 16384, BF16, [128, NC, 256])

        def dst_m(tb, half):
            return memh[:, half * 8:(half + 1) * 8, tb * 128:(tb + 1) * 128], [("memh", tb, half)]
        norm_transpose(g["mem_d"], MGB, dst_m, 2, "m")
        memh_keys = [("memh", tb, half) for tb in range(2) for half in range(2)]
        kf = view(SCR + 24576, F32, [128, 256])
        ksq = view(SCR + 25600, BF16, [128, 256])
        kt_ = view(SCR + 26112, F32, [128, 256])
        rk = view(SCR + 27136, F32, [128, 256])
        for j in range(4):
            slot = wload([(g["wkv_d"][l, :, j * 256:(j + 1) * 256], 0, 256)])
            if j < 2:
                for hh in range(2):
                    h = 2 * j + hh
                    pairs = [(Wr[slot][:, c, hh * 128:(hh + 1) * 128], memh[:, c, :]) for c in range(NC)]

                    def fn(e, pairs=pairs):
                        for i, (l_, r) in enumerate(pairs):
                            ins = e.matmul(psb(4, 256), l_, r, start=(i == 0), stop=(i == NC - 1))
                        return ins
                    op("pe", fn, reads=[("W", slot)] + memh_keys, writes=[("ps", 4)])
                    op("act", lambda e: e.activation(ksq, psb(4, 256), AF.Square), reads=[("ps", 4)], writes=["ksq"])
                    op("dve", lambda e: e.tensor_copy(kf, psb(4, 256)), reads=[("ps", 4), "ksq"], writes=["kf"])
                    op("pe", lambda e: e.matmul(psb(6, 256), ones_bf, ksq, start=True, stop=True),
                       reads=["ksq", "ones"], writes=[("ps", 6)])
                    op("act", lambda e: e.activation(kt_, psb(6, 256), AF.Ln, bias=EPS, scale=1.0 / 128.0),
                       reads=[("ps", 6)], writes=["kt"])
                    op("act", lambda e: e.activation(rk, kt_, AF.Exp, scale=-0.5), reads=["kt"], writes=["rk"])
                    op("dve", lambda e, h=h: e.scalar_tensor_tensor(knT[:, h, :], kf, kgq, rk, ALU.mult, ALU.mult),
                       reads=["kf", "rk", "kgq"], writes=[("knT", h)])
            else:
                jj = j - 2
                for mb in range(2):
                    pairs = [(memh[:, c, mb * 128:(mb + 1) * 128], Wr[slot][:, c, :]) for c in range(NC)]

                    def fn(e, pairs=pairs):
                        for i, (l_, r) in enumerate(pairs):
                            ins = e.matmul(psb(5, 256), l_, r, start=(i == 0), stop=(i == NC - 1))
                        return ins
                    op("pe", fn, reads=[("W", slot)] + memh_keys, writes=[("ps", 5)])
                    op("act", lambda e, mb=mb, jj=jj: e.copy(memv[:, mb, jj * 256:(jj + 1) * 256], psb(5, 256)),
                       reads=[("ps", 5)], writes=[("memv", mb, jj)])
        sch.barrier()

        if STOP <= 2:
            finish_dbg()
            return
        lng = view(SCR + 0, F32, [128, 1024])
        lnb = view(SCR + 4096, F32, [128, 1024])
        dma("sp", lambda e, l=l: e.dma_start(out=lng, in_=g["lng_d"][l:l + 1, :].partition_broadcast(128)),
            reads=[], writes=["lng"], key="c0")
        dma("sp", lambda e, l=l: e.dma_start(out=lnb, in_=g["lnb_d"][l:l + 1, :].partition_broadcast(128)),
            reads=[], writes=["lnb"], key="c1")
        gu_t = [view(SCR + 8192 + i * 2048, F32, [128, 512]) for i in range(2)]
        th_t = [view(SCR + 12288 + i * 2048, F32, [128, 512]) for i in range(2)]
        lnt = [view(SCR + 16384 + i * 4096, F32, [128, 1024]) for i in range(2)]
        it = 0
        for j in range(4):
            slot = wload([(g["w_in_d"][l, :, 1024 + j * 256:1024 + (j + 1) * 256], 0, 256)])
            for tb in range(16):
                bank = 4 + it % 2
                it += 1
                pairs = [(hT[:, c, tb * 128:(tb + 1) * 128], Wr[slot][:, c, :]) for c in range(NC)]

                def fn(e, pairs=pairs, bank=bank):
                    for i, (l_, r) in enumerate(pairs):
                        ins = e.matmul(psb(bank, 256), l_, r, start=(i == 0), stop=(i == NC - 1))
                    return ins
                op("pe", fn, reads=[("W", slot), ("hT", tb), ("hTa", tb)], writes=[("ps", bank)])
                op("act", lambda e, tb=tb, j=j, bank=bank: e.activation(VA[:, tb, j * 256:(j + 1) * 256], psb(bank, 256), AF.Gelu),
                   reads=[("ps", bank)], writes=[("VA", tb, j)])
        def lnA(tb):
            pz = tb % 2
            st_, mv_, t1_, rs_ = ST2[pz], MV2[pz], T12[pz], RS2[pz]
            for k in range(2):
                op("dve", lambda e, tb=tb, k=k, st_=st_: e.bn_stats(st_[:, k, :], VA[:, tb, k * 512:(k + 1) * 512]),
                   reads=[("VA", tb, 2 * k), ("VA", tb, 2 * k + 1)], writes=[("lst", pz, k)])
            op("dve", lambda e, st_=st_, mv_=mv_: e.bn_aggr(mv_, st_[:, 0:2, :].rearrange("p a b -> p (a b)")),
               reads=[("lst", pz, 0), ("lst", pz, 1)], writes=[("lmv", pz)])
            op("dve", lambda e, mv_=mv_, t1_=t1_: e.tensor_scalar(t1_, mv_[:, 1:2], EPS, None, ALU.add),
               reads=[("lmv", pz)], writes=[("lt1", pz)])
            op("pool", lambda e, t1_=t1_, rs_=rs_: e.tensor_tensor(rs_, t1_, neghalf[:, 0:1], ALU.pow),
               reads=[("lt1", pz)], writes=[("lrs", pz)])

        def lnB(tb):
            pz = tb % 2
            mv_, rs_ = MV2[pz], RS2[pz]
            tmp = lnt[tb % 2]
            op("dve", lambda e, tb=tb, tmp=tmp, mv_=mv_: e.scalar_tensor_tensor(tmp, VA[:, tb, :], mv_[:, 0:1], lng, ALU.subtract, ALU.mult),
               reads=[("VA", tb, j) for j in range(4)] + [("lmv", pz), "lng"], writes=[("lnt", tb % 2)])
            op("dve", lambda e, tb=tb, tmp=tmp, rs_=rs_: e.scalar_tensor_tensor(VA[:, tb, :], tmp, rs_, lnb, ALU.mult, ALU.add),
               reads=[("lnt", tb % 2), ("lrs", pz), "lnb"], writes=[("VA", tb, j) for j in range(4)])

        lnA(0)
        it = 0
        for j in range(8):
            slot = wload([(g["w_in_d"][l, :, j * 128:(j + 1) * 128], 0, 128),
                          (g["w_in_d"][l, :, 2048 + j * 128:2048 + (j + 1) * 128], 128, 128)])
            for G in range(4):
                bu = it % 2
                bz = 2 + it % 2
                gu = gu_t[it % 2]
                th = th_t[it % 2]
                if it % 2 == 0:
                    tb_ln = it // 2
                    if tb_ln + 1 < 16:
                        lnA(tb_ln + 1)
                    lnB(tb_ln)
                it += 1
                proj_fm(slot, 0, G, bu)
                proj_fm(slot, 128, G, bz)
                op("act", lambda e, gu=gu, bu=bu: e.activation(gu, psb(bu), AF.Gelu), reads=[("ps", bu)], writes=[("gu", bu)])
                op("act", lambda e, th=th, bz=bz: e.activation(th, psb(bz), AF.Tanh, scale=0.5), reads=[("ps", bz)], writes=[("th", bz)])
                op("dve", lambda e, th=th, bz=bz: e.scalar_tensor_tensor(th, th, 1.0, psb(bz), ALU.add, ALU.mult),
                   reads=[("th", bz), ("ps", bz)], writes=[("th", bz)])
                op("dve", lambda e, th=th, gu=gu, j=j, G=G: e.scalar_tensor_tensor(yT[:, j, G * 512:(G + 1) * 512], th, 0.5, gu, ALU.mult, ALU.mult),
                   reads=[("th", bz), ("gu", bu)], writes=[("yT", j, G)])
        it = 0
        for G in range(4):
            for gi in range(8):
                bank = 6 + it % 2
                it += 1

                def fn(e, G=G, gi=gi, bank=bank):
                    for cc in range(4):
                        c = 4 * G + cc
                        o = psb(bank, 128, cc * 128)
                        e.matmul(o, VA[:, c, gi * 128:(gi + 1) * 128], wT[:, gi, :], start=True, stop=False)
                        e.matmul(o, ones0, bhi[:, gi * 128:(gi + 1) * 128], start=False, stop=False)
                        ins = e.matmul(o, ones0, blo[:, gi * 128:(gi + 1) * 128], start=False, stop=True)
                    return ins
                op("pe", fn, reads=[("VA", 4 * G + cc, jj) for cc in range(4) for jj in range(4)] + ["wT", "bhi", "blo", "ones0"],
                   writes=[("ps", bank)])
                op("dve", lambda e, G=G, gi=gi, bank=bank: e.tensor_tensor(yT[:, gi, G * 512:(G + 1) * 512], psb(bank),
                                                                            yT[:, gi, G * 512:(G + 1) * 512], ALU.mult),
                   reads=[("ps", bank), ("yT", gi, G)], writes=[("yT", gi, G)])
        sch.barrier()

        if STOP <= 3:
            finish_dbg()
            return
        qT = view(SCR + 0, BF16, [128, S])
        kT = view(SCR + 4096, BF16, [128, S])
        vh = view(SCR + 8192, BF16, [128, 16, 128])
        zs = view(SCR + 12288, BF16, [128, S])
        E_t = [view(SCR + 16384 + i * 2048, F32, [128, 512]) for i in range(2)]
        sp_t = [view(SCR + 20480 + i * 1024, BF16, [128, 512]) for i in range(3)]
        R_t = [view(SCR + 23552 + i * 1024, BF16, [128, 512]) for i in range(3)]
        A_t = [view(SCR + 26624 + i * 1024, BF16, [128, 512]) for i in range(3)]
        ez_t = [view(SCR + 29696 + i * 2048, F32, [128, 512]) for i in range(2)]
        assert SCR + 33792 <= 211968
        pj = 0
        for h in range(4):
            c_q = 3072 + h * 128
            slot1 = wload([(g["w_in_d"][l, :, c_q:c_q + 128], 0, 128),
                           (g["w_in_d"][l, :, c_q + 512:c_q + 640], 128, 128)])
            for G in range(4):
                b = pj % 2
                pj += 1
                proj_fm(slot1, 0, G, b)
                op("act", lambda e, G=G, b=b: e.activation(qT[:, G * 512:(G + 1) * 512], psb(b), AF.Copy, scale=SCALE),
                   reads=[("ps", b)], writes=[("qT", G)])
                b = pj % 2
                pj += 1
                proj_fm(slot1, 128, G, b)
                op("dve", lambda e, G=G, b=b: e.tensor_copy(kT[:, G * 512:(G + 1) * 512], psb(b)),
                   reads=[("ps", b)], writes=[("kT", G)])
            slot2 = wload()
            for G in range(4):
                b = pj % 2
                pj += 1
                ez = ez_t[G % 2]
                proj_fm(slot2, 128, G, b)
                op("act", lambda e, ez=ez, b=b: e.activation(ez, psb(b), AF.Exp, scale=-1.0), reads=[("ps", b)], writes=[("ez", G % 2)])
                op("act", lambda e, ez=ez: e.activation(ez, ez, AF.Ln, bias=1.0), reads=[("ez", G % 2)], writes=[("ez", G % 2)])
                op("act", lambda e, ez=ez: e.activation(ez, ez, AF.Exp, scale=-1.0), reads=[("ez", G % 2)], writes=[("ez", G % 2)])
                op("dve", lambda e, ez=ez, G=G, b=b: e.tensor_tensor(zs[:, G * 512:(G + 1) * 512], psb(b), ez, ALU.mult),
                   reads=[("ps", b), ("ez", G % 2)], writes=[("zs", G)])
            for t4 in range(4):
                b = pj % 2
                pj += 1

                def fn(e, t4=t4, b=b, slot2=slot2):
                    for tt in range(4):
                        tb = 4 * t4 + tt
                        for c in range(NC):
                            ins = e.matmul(psb(b, 128, tt * 128), hT[:, c, tb * 128:(tb + 1) * 128], Wr[slot2][:, c, 0:128],
                                           start=(c == 0), stop=(c == NC - 1))
                    return ins
                op("pe", fn, reads=[("W", slot2)] + hkeys(t4), writes=[("ps", b)])
                op("act", lambda e, t4=t4, b=b: e.copy(vh[:, 4 * t4:4 * t4 + 4, :], psb(b).rearrange("p (a b) -> p a b", a=4)),
                   reads=[("ps", b)], writes=[("vh", t4)])
            units = []
            for G in range(4):
                nb = 4 * G + 4
                for k_, bidx in enumerate(range(nb - 1, -1, -1)):
                    i_d = bidx - 4 * G
                    c0 = 128 * i_d if i_d >= 0 else 0
                    units.append(dict(G=G, bidx=bidx, diag=(i_d >= 0), c0=c0, ncol=512 - c0, first=(k_ == 0), last=(bidx == 0)))
            nU = len(units)

            def S1(u):
                U = units[u]
                G, bidx, diag, c0, ncol = U["G"], U["bidx"], U["diag"], U["c0"], U["ncol"]
                zb = 2 + u % 4
                E = E_t[u % 2]
                sp = sp_t[u % 3]
                kblk = kT[:, bidx * 128:(bidx + 1) * 128]
                qcols = qT[:, G * 512 + c0:G * 512 + 512]

                def fz(e):
                    ins = e.matmul(psb(zb, ncol), kblk, qcols, start=True, stop=not diag)
                    if diag:
                        ins = e.matmul(psb(zb, 128), ident_bf, negmask, start=False, stop=True)
                    return ins
                op("pe", fz, reads=[("kT", bidx // 4), ("qT", G), "ident", "negmask"], writes=[("ps", zb)])
                op("act", lambda e: e.activation(E[:, 0:ncol], psb(zb, ncol), AF.Exp), reads=[("ps", zb)], writes=[("E", u % 2)])

            def S1b(u):
                ncol = units[u]["ncol"]
                E = E_t[u % 2]
                sp = sp_t[u % 3]
                op("act", lambda e: e.activation(sp[:, 0:ncol], E[:, 0:ncol], AF.Ln, bias=1.0),
                   reads=[("E", u % 2)], writes=[("sp", u % 3)])

            def S2(u):
                U = units[u]
                G, bidx, diag, c0, ncol, first = U["G"], U["bidx"], U["diag"], U["c0"], U["ncol"], U["first"]
                zb = 2 + u % 4
                sp = sp_t[u % 3]
                A = A_t[u % 3]
                Rcur = R_t[u % 3]
                Rnext = R_t[(u + 1) % 3]

                kblk2 = kT[:, bidx * 128:(bidx + 1) * 128]
                qcols2 = qT[:, G * 512 + c0:G * 512 + 512]

                def fp(e):
                    if os.environ.get("KZ"):
                        e.matmul(psb(zb, ncol), kblk2, qcols2, start=True, stop=not diag)
                        if diag:
                            e.matmul(psb(zb, 128), ident_bf, negmask, start=False, stop=True)
                    ins = e.matmul(psb(zb, ncol), negU, sp[:, 0:ncol], start=False, stop=True, skip_group_check=True)
                    if not first:
                        if diag:
                            ins = e.matmul(psb(zb, ncol - 128, 128), negOnes, Rcur[:, c0 + 128:512], start=False, stop=True,
                                           skip_group_check=True)
                        else:
                            ins = e.matmul(psb(zb, 512), negOnes, Rcur, start=False, stop=True, skip_group_check=True)
                    return ins
                op("pe", fp, reads=[("sp", u % 3), ("R", u % 3), "negU", "negones"], writes=[("ps", zb)])
                if not U["last"]:
                    if first:
                        op("pool", lambda e: e.tensor_copy(Rnext[:, c0:512], sp[:, 0:ncol]),
                           reads=[("sp", u % 3)], writes=[("R", (u + 1) % 3)])
                    elif diag:
                        op("pool", lambda e: e.tensor_copy(Rnext[:, c0:c0 + 128], sp[:, 0:128]),
                           reads=[("sp", u % 3)], writes=[("R", (u + 1) % 3)])
                        op("pool", lambda e: e.tensor_tensor(Rnext[:, c0 + 128:512], Rcur[:, c0 + 128:512], sp[:, 128:ncol], ALU.add),
                           reads=[("sp", u % 3), ("R", u % 3)], writes=[("R", (u + 1) % 3)])
                    else:
                        op("pool", lambda e: e.tensor_tensor(Rnext, Rcur, sp, ALU.add),
                           reads=[("sp", u % 3), ("R", u % 3)], writes=[("R", (u + 1) % 3)])

            def S2b(u):
                ncol = units[u]["ncol"]
                zb = 2 + u % 4
                A = A_t[u % 3]
                op("act", lambda e: e.activation(A[:, 0:ncol], psb(zb, ncol), AF.Exp), reads=[("ps", zb)], writes=[("A", u % 3)])

            def S3(u, h=h):
                U = units[u]
                G, bidx, diag, c0, ncol, first, last = U["G"], U["bidx"], U["diag"], U["c0"], U["ncol"], U["first"], U["last"]
                yb = 6 + G % 2
                A = A_t[u % 3]
                vblk = vh[:, bidx, :]

                def fy(e):
                    if first:
                        ins = e.matmul(psb(yb, ncol, c0), vblk, A[:, 0:ncol], start=True, stop=True)
                    elif diag:
                        e.matmul(psb(yb, 128, c0), vblk, A[:, 0:128], start=False, stop=True, skip_group_check=True)
                        ins = e.matmul(psb(yb, ncol - 128, c0 + 128), vblk, A[:, 128:ncol], start=False, stop=True,
                                       skip_group_check=True)
                    else:
                        ins = e.matmul(psb(yb, 512), vblk, A, start=False, stop=True, skip_group_check=True)
                    return ins
                op("pe", fy, reads=[("vh", bidx // 4), ("A", u % 3)], writes=[("ps", yb)])
                if last:
                    op("dve", lambda e: e.tensor_tensor(yT[:, 8 + h, G * 512:(G + 1) * 512], psb(yb),
                                                        zs[:, G * 512:(G + 1) * 512], ALU.mult),
                       reads=[("ps", yb), ("zs", G)], writes=[("yT", 8 + h, G)])

            for i in range(nU + 3):
                if i < nU:
                    S1(i)
                if 0 <= i - 1 < nU:
                    S2(i - 1)
                if 0 <= i - 2 < nU:
                    S2b(i - 2)
                if i < nU:
                    S1b(i)
                if 0 <= i - 3 < nU:
                    S3(i - 3)
        sch.barrier()

        if STOP <= 4:
            finish_dbg()
            return
        def cbuf(p_):
            o = SCR + p_ * 15360
            return dict(sq=view(o, BF16, [128, 512]), tq=view(o + 1024, F32, [128, 512]), rq=view(o + 3072, F32, [128, 512]),
                        qh=view(o + 5120, BF16, [128, 512]), ez=view(o + 6144, F32, [128, 512]), zsc=view(o + 8192, BF16, [128, 512]),
                        PT=[view(o + 9216 + i * 1024, BF16, [128, 512]) for i in range(2)],
                        rd=view(o + 11264, F32, [128, 512]), t2=view(o + 13312, F32, [128, 512]))
        CB = [cbuf(0), cbuf(1)]
        assert SCR + 2 * 15360 <= 211968
        cits = [(h, G) for h in range(4) for G in range(4)]
        cslot = {}

        def c_projq(it):
            h, G = cits[it]
            if G == 0:
                cslot[h] = wload()
            proj_fm(cslot[h], 0, G, it % 2)

        def c_projz(it):
            h, G = cits[it]
            proj_fm(cslot[h], 128, G, 2 + it % 2)

        c_projq(0)
        c_projz(0)
        for it, (h, G) in enumerate(cits):
            p_ = it % 2
            cb = CB[p_]
            qb = it % 2
            zb = 2 + it % 2
            sq, tq, rq, qh, ez, zsc, PT, rd, t2 = (cb["sq"], cb["tq"], cb["rq"], cb["qh"], cb["ez"], cb["zsc"], cb["PT"], cb["rd"], cb["t2"])
            K = lambda n_: (n_, p_)
            op("act", lambda e, sq=sq, qb=qb: e.activation(sq, psb(qb), AF.Square), reads=[("ps", qb)], writes=[K("sq")])
            op("pe", lambda e, sq=sq: e.matmul(psb(6), ones_bf, sq, start=True, stop=True), reads=[K("sq"), "ones"], writes=[("ps", 6)])
            if it + 1 < len(cits):
                c_projq(it + 1)
            op("act", lambda e, tq=tq: e.activation(tq, psb(6), AF.Ln, bias=EPS, scale=1.0 / 128.0), reads=[("ps", 6)], writes=[K("tq")])
            op("act", lambda e, tq=tq, rq=rq: e.activation(rq, tq, AF.Exp, scale=-0.5), reads=[K("tq")], writes=[K("rq")])
            op("dve", lambda e, qh=qh, rq=rq, qb=qb: e.tensor_tensor(qh, psb(qb), rq, ALU.mult),
               reads=[("ps", qb), K("sq"), K("rq")], writes=[K("qh")])
            op("act", lambda e, ez=ez, zb=zb: e.activation(ez, psb(zb), AF.Exp, scale=-1.0), reads=[("ps", zb)], writes=[K("ez")])
            op("act", lambda e, ez=ez: e.activation(ez, ez, AF.Ln, bias=1.0), reads=[K("ez")], writes=[K("ez")])
            op("act", lambda e, ez=ez: e.activation(ez, ez, AF.Exp, scale=-1.0), reads=[K("ez")], writes=[K("ez")])
            op("dve", lambda e, zsc=zsc, ez=ez, zb=zb: e.tensor_tensor(zsc, psb(zb), ez, ALU.mult), reads=[("ps", zb), K("ez")], writes=[K("zsc")])
            for mb in range(2):
                op("pe", lambda e, mb=mb, h=h, qh=qh: e.matmul(psb(4 + mb), knT[:, h, mb * 128:(mb + 1) * 128], qh, start=True, stop=True),
                   reads=[("knT", h), K("qh")], writes=[("ps", 4 + mb)])
            if it + 1 < len(cits):
                c_projz(it + 1)
            for mb in range(2):
                op("act", lambda e, mb=mb, PT=PT: e.activation(PT[mb], psb(4 + mb), AF.Exp), reads=[("ps", 4 + mb)], writes=[K("PT%d" % mb)])

            def fo(e, h=h, PT=PT):
                e.matmul(psb(7), memv[:, 0, h * 128:(h + 1) * 128], PT[0], start=True, stop=False)
                e.matmul(psb(7), memv[:, 1, h * 128:(h + 1) * 128], PT[1], start=False, stop=True)
                e.matmul(psb(6), ones_bf, PT[0], start=True, stop=False)
                return e.matmul(psb(6), ones_bf, PT[1], start=False, stop=True)
            op("pe", fo, reads=[K("PT0"), K("PT1"), "ones"] + [("memv", mb, jj) for mb in range(2) for jj in range(2)],
               writes=[("ps", 7), ("ps", 6)])
            op("dve", lambda e, rd=rd: e.reciprocal(rd, psb(6)), reads=[("ps", 6)], writes=[K("rd")])
            op("dve", lambda e, t2=t2, rd=rd: e.tensor_tensor(t2, psb(7), rd, ALU.mult), reads=[("ps", 7), K("rd")], writes=[K("t2")])
            op("dve", lambda e, h=h, G=G, t2=t2, zsc=zsc: e.tensor_tensor(yT[:, 12 + h, G * 512:(G + 1) * 512], t2, zsc, ALU.mult),
               reads=[K("t2"), K("zsc")], writes=[("yT", 12 + h, G)])
        sch.barrier()

        if dbg and l == int(os.environ.get("KL", "0")):
            finish_dbg()
        if STOP <= 5:
            return

        NXR = 6
        xr_t = [view(HT_OFF + i * 1024, F32, [128, 256]) for i in range(NXR)]
        os_t = [view(HT_OFF + 8192 + i * 1024, F32, [128, 256]) for i in range(4)]
        p3 = [(n, tb) for n in range(8) for tb in range(16)]

        def p3_load(k, src_ap=src_ap):
            n, tb = p3[k]
            xr = xr_t[k % NXR]
            dma("sp", lambda e: e.dma_start(out=xr, in_=src_ap[tb * 128:(tb + 1) * 128, n * 256:(n + 1) * 256]),
                reads=[], writes=[("xr", k % NXR)], key=("xr", k % NXR))

        for k in range(3):
            p3_load(k)
        slot = None
        for k, (n, tb) in enumerate(p3):
            if tb == 0:
                slot = wload()
            if k + 3 < len(p3):
                p3_load(k + 3)
            bank = k % 8
            xr = xr_t[k % NXR]
            ost = os_t[k % 4]
            pairs = [(yT[:, c, tb * 128:(tb + 1) * 128], Wr[slot][:, c, :]) for c in range(NC)]

            def fn(e, pairs=pairs, bank=bank):
                for i, (l_, r) in enumerate(pairs):
                    ins = e.matmul(psb(bank, 256), l_, r, start=(i == 0), stop=(i == NC - 1))
                return ins
            op("pe", fn, reads=[("W", slot)], writes=[("ps", bank)])
            op("dve", lambda e, ost=ost, xr=xr, bank=bank: e.tensor_tensor(ost, psb(bank, 256), xr, ALU.add),
               reads=[("ps", bank), ("xr", k % NXR)], writes=[("os", k % 4)])
            dma("sp", lambda e, ost=ost, tb=tb, n=n, dst_ap=dst_ap: e.dma_start(out=dst_ap[tb * 128:(tb + 1) * 128, n * 256:(n + 1) * 256], in_=ost),
                reads=[("os", k % 4)], writes=[("xdst", tb, n)], key=("os", k % 4))
        sch.barrier()


_NC_CACHE = {}


def kernel(x, mem, norm_g, w_in, sgu_ln_g, sgu_ln_b, sgu_w, sgu_b, mem_norm_g, w_mem_kv, q_norm_g, k_norm_g, w_out):
    if "nc" not in _NC_CACHE:
        _NC_CACHE["nc"] = build_nc()
    nc = _NC_CACHE["nc"]
    f = lambda a: np.ascontiguousarray(np.asarray(a, dtype=np.float32))
    shared = {"norm_g": f(norm_g), "w_in": f(w_in), "sgu_ln_g": f(sgu_ln_g), "sgu_ln_b": f(sgu_ln_b),
              "sgu_w": f(sgu_w), "sgu_b": f(sgu_b), "mem_norm_g": f(mem_norm_g), "w_mem_kv": f(w_mem_kv),
              "q_norm_g": f(q_norm_g), "k_norm_g": f(k_norm_g), "w_out": f(w_out)}
    x = f(x)
    mem = f(mem)
    in_maps = [dict(shared, x=x[b], mem=mem[b]) for b in range(8)]
    res = run_bass_kernel_spmd(nc, in_maps, core_ids=list(range(8)))
    return np.stack([np.asarray(r["out"], dtype=np.float32) for r in res.results], axis=0)
```

```python
import math
import numpy as np
import concourse.bass as bass
import concourse.mybir as mybir
from concourse.bass_utils import run_bass_kernel_spmd

F32 = mybir.dt.float32
BF16 = mybir.dt.bfloat16
U8 = mybir.dt.uint8
AF = mybir.ActivationFunctionType
ALU = mybir.AluOpType

S = 2048
D = 2048
MEM = 256
NL = 2
INW = 6144
EPS = 1e-6
NEG = -30000.0
NC = 16
SCALE = 1.0 / math.sqrt(128.0)

ENGS = ("pe", "act", "dve", "pool", "sp")


class Sched:
    def __init__(self, nc, esems, dma_sems):
        self.nc = nc
        self.prog = {e: [] for e in ENGS}
        self.cnt = {e: 0 for e in ENGS}
        self.esem = esems
        self.free_dsem = list(dma_sems)
        self.dsem = {}
        self.seen = {e: {} for e in ENGS}
        self.lastw = {}
        self.readers = {}

    def _waits(self, eng, reads, writes):
        toks = []
        for r in reads:
            t = self.lastw.get(r)
            if t is not None:
                toks.append(("raw", t))
        for w in writes:
            t = self.lastw.get(w)
            if t is not None:
                toks.append(("waw", t))
            for t in self.readers.get(w, ()):
                toks.append(("war", t))
        waits = []
        seen = self.seen[eng]
        for kind, (key, val, peng) in toks:
            if peng == eng and eng == "pe":
                continue
            if seen.get(key, 0) >= val:
                continue
            seen[key] = val
            waits.append((key, val))
        return waits

    def _commit(self, tok, reads, writes):
        for r in reads:
            self.readers.setdefault(r, []).append(tok)
        for w in writes:
            self.lastw[w] = tok
            self.readers[w] = []

    budget = None

    def _drop(self):
        if self.budget is None:
            return False
        if self.budget <= 0:
            return True
        self.budget -= 1
        return False

    def op(self, eng, fn, reads=(), writes=()):
        if self._drop():
            return
        waits = self._waits(eng, reads, writes)
        self.cnt[eng] += 1
        tok = (eng, self.cnt[eng], eng)
        self.prog[eng].append((waits, fn, eng, 1))
        self._commit(tok, reads, writes)

    def dma(self, queue, fn, reads, writes, key, force=False):
        if not force and self._drop():
            return (("d", key), 0, "dma")
        waits = self._waits(queue, reads, writes)
        if key not in self.dsem:
            self.dsem[key] = [self.free_dsem.pop(), 0]
        self.dsem[key][1] += 16
        tok = (("d", key), self.dsem[key][1], "dma")
        self.prog[queue].append((waits, fn, ("d", key), 16))
        self._commit(tok, reads, writes)
        return tok

    def semh(self, key):
        if isinstance(key, tuple) and key[0] == "d":
            return self.dsem[key[1]][0]
        return self.esem[key]

    def barrier(self):
        for e in ENGS:
            waits = []
            for o in ENGS:
                if o != e and self.cnt[o] > 0 and self.seen[e].get(o, 0) < self.cnt[o]:
                    self.seen[e][o] = self.cnt[o]
                    waits.append((o, self.cnt[o]))
            for k, (h, c) in self.dsem.items():
                kk = ("d", k)
                if c > 0 and self.seen[e].get(kk, 0) < c:
                    self.seen[e][kk] = c
                    waits.append((kk, c))
            if waits:
                self.prog[e].append((waits, None, None, 0))
        self.lastw = {}
        self.readers = {}

    def replay(self, eng, e):
        for waits, fn, skey, inc in self.prog[eng]:
            for key, val in waits:
                e.wait_ge(self.semh(key), val)
            if fn is None:
                continue
            ins = fn(e)
            ins.then_inc(self.semh(skey), inc)


def build_nc(n_layers=NL, dbg=False):
    nc = bass.Bass("TRN2", target_bir_lowering=False)
    x_d = nc.dram_tensor("x", [S, D], F32, kind="ExternalInput").ap()
    mem_d = nc.dram_tensor("mem", [MEM, D], F32, kind="ExternalInput").ap()
    norm_g_d = nc.dram_tensor("norm_g", [NL, D], F32, kind="ExternalInput").ap()
    w_in_d = nc.dram_tensor("w_in", [NL, D, INW], F32, kind="ExternalInput").ap()
    lng_d = nc.dram_tensor("sgu_ln_g", [NL, 1024], F32, kind="ExternalInput").ap()
    lnb_d = nc.dram_tensor("sgu_ln_b", [NL, 1024], F32, kind="ExternalInput").ap()
    sguw_d = nc.dram_tensor("sgu_w", [NL, 8, 128, 128], F32, kind="ExternalInput").ap()
    sgub_d = nc.dram_tensor("sgu_b", [NL, 8, 128], F32, kind="ExternalInput").ap()
    mng_d = nc.dram_tensor("mem_norm_g", [NL, D], F32, kind="ExternalInput").ap()
    wkv_d = nc.dram_tensor("w_mem_kv", [NL, D, 1024], F32, kind="ExternalInput").ap()
    qg_d = nc.dram_tensor("q_norm_g", [NL, 128], F32, kind="ExternalInput").ap()
    kg_d = nc.dram_tensor("k_norm_g", [NL, 128], F32, kind="ExternalInput").ap()
    wout_d = nc.dram_tensor("w_out", [NL, D, D], F32, kind="ExternalInput").ap()
    out_d = nc.dram_tensor("out", [S, D], F32, kind="ExternalOutput").ap()
    x1_d = nc.dram_tensor("x1_scratch", [S, D], F32).ap()
    if dbg:
        dbg_h = nc.dram_tensor("dbg_h", [128, NC * S], BF16, kind="ExternalOutput").ap()
        dbg_y = nc.dram_tensor("dbg_y", [128, NC * S], BF16, kind="ExternalOutput").ap()

    ARENA = 211968
    arena = nc.alloc_sbuf_tensor("arena", [128, ARENA], U8)
    ps = [nc.alloc_psum_tensor("ps%d" % i, [128, 512], F32) for i in range(8)]

    def view(off, dt, shape):
        esz = 4 if dt == F32 else 2
        n = 1
        for s_ in shape[1:]:
            n *= s_
        ap = arena[0:shape[0], off:off + n * esz].bitcast(dt)
        if len(shape) == 3:
            ap = ap.rearrange("p (a b) -> p a b", a=shape[1])
        return ap

    HT_OFF = 0
    YT_OFF = 65536
    W_OFF = 131072
    C_OFF = W_OFF + 3 * 8192
    SCR = C_OFF + 14592
    assert SCR + 32768 <= ARENA

    hT = view(HT_OFF, BF16, [128, NC, S])
    yT = view(YT_OFF, BF16, [128, NC, S])
    VA = view(YT_OFF + 32768, BF16, [128, 16, 1024])
    Wr = [view(W_OFF + i * 8192, BF16, [128, NC, 256]) for i in range(3)]
    ident_bf = view(C_OFF + 0, BF16, [128, 128])
    negU = view(C_OFF + 256, BF16, [128, 128])
    negOnes = view(C_OFF + 512, BF16, [128, 128])
    ones_bf = view(C_OFF + 768, BF16, [128, 128])
    negmask = view(C_OFF + 1024, BF16, [128, 128])
    ident_f = view(C_OFF + 1280, F32, [128, 128])
    neghalf = view(C_OFF + 1792, F32, [128, 512])
    kgq = view(C_OFF + 3840, F32, [128, 1])
    qg = view(C_OFF + 3844, F32, [128, 1])
    kg = view(C_OFF + 3848, F32, [128, 1])
    st = view(C_OFF + 3872, F32, [128, 4, 6])
    mv = view(C_OFF + 3968, F32, [128, 2])
    t1 = view(C_OFF + 3976, F32, [128, 1])
    rstd = view(C_OFF + 3980, F32, [128, 1])
    ST2 = [st, view(C_OFF + 3984, F32, [128, 4, 6])]
    MV2 = [mv, view(C_OFF + 4080, F32, [128, 2])]
    T12 = [t1, view(C_OFF + 4088, F32, [128, 1])]
    RS2 = [rstd, view(C_OFF + 4092, F32, [128, 1])]
    knT = view(C_OFF + 4096, BF16, [128, 4, 256])
    memv = view(C_OFF + 6144, BF16, [128, 2, 512])
    wT = view(C_OFF + 8192, BF16, [128, 8, 128])
    bhi = view(C_OFF + 10240, BF16, [128, 1024])
    blo = view(C_OFF + 12288, BF16, [128, 1024])
    ones0 = view(C_OFF + 14336, BF16, [128, 128])
    XT = [view(YT_OFF + i * 8192, F32, [128, D]) for i in range(3)]
    XN = [view(YT_OFF + 24576 + i * 4096, BF16, [128, D]) for i in range(2)]
    GB = view(YT_OFF + 32768, F32, [128, D])
    MGB = view(YT_OFF + 40960, F32, [128, D])
    XS = [view(YT_OFF + 49152 + i * 8192, F32, [128, D]) for i in range(2)]

    with nc.semaphore("s_pe") as s_pe, nc.semaphore("s_act") as s_act, \
            nc.semaphore("s_dve") as s_dve, nc.semaphore("s_pool") as s_pool, \
            nc.semaphore("s_sp") as s_sp:
        import contextlib
        with contextlib.ExitStack() as es:
            dma_sems = [es.enter_context(nc.semaphore("dsem%d" % i)) for i in range(24)]
            sch = Sched(nc, {"pe": s_pe, "act": s_act, "dve": s_dve, "pool": s_pool, "sp": s_sp}, dma_sems)
            _emit_program(nc, sch, locals(), n_layers, dbg)
            with nc.Block() as block:
                @block.tensor
                def _(e):
                    sch.replay("pe", e)

                @block.scalar
                def _(e):
                    sch.replay("act", e)

                @block.vector
                def _(e):
                    sch.replay("dve", e)

                @block.gpsimd
                def _(e):
                    sch.replay("pool", e)

                @block.sync
                def _(e):
                    sch.replay("sp", e)
    return nc


def _emit_program(nc, sch, V, n_layers, dbg):
    import os
    STOP = int(os.environ.get("KSTOP", "99")) if dbg else 99
    g = V
    ps = g["ps"]
    hT, yT, VA, Wr = g["hT"], g["yT"], g["VA"], g["Wr"]
    view = g["view"]
    SCR = g["SCR"]
    HT_OFF = g["HT_OFF"]
    ident_bf, negU, negOnes, ones_bf, negmask, ident_f, neghalf = (
        g["ident_bf"], g["negU"], g["negOnes"], g["ones_bf"], g["negmask"], g["ident_f"], g["neghalf"])
    kgq, qg, kg, st, mv, t1, rstd = g["kgq"], g["qg"], g["kg"], g["st"], g["mv"], g["t1"], g["rstd"]
    knT, memv, wT, bhi, blo, ones0 = g["knT"], g["memv"], g["wT"], g["bhi"], g["blo"], g["ones0"]
    XT, XN, GB, MGB = g["XT"], g["XN"], g["GB"], g["MGB"]
    ST2, MV2, T12, RS2 = g["ST2"], g["MV2"], g["T12"], g["RS2"]
    XS = g["XS"]
    op, dma = sch.op, sch.dma

    def psb(i, n=512, off=0):
        return ps[i][:, off:off + n]

    def ps_bf(i):
        return ps[i][:, :].bitcast(BF16)

    op("pool", lambda e: e.memset(ones_bf, 1.0), writes=["ones"])
    op("pool", lambda e: e.memset(negOnes, -1.0), writes=["negones"])
    op("pool", lambda e: e.memset(neghalf, -0.5), writes=["neghalf"])
    op("pool", lambda e: e.memset(negmask, NEG), writes=["negmask"])
    op("pool", lambda e: e.memset(ident_f, 1.0), writes=["identf"])
    op("pool", lambda e: e.affine_select(ident_bf, ones_bf, [[-1, 128]], ALU.is_equal, 0.0, base=0, channel_multiplier=1),
       reads=["ones"], writes=["ident"])
    op("pool", lambda e: e.affine_select(ident_f, ident_f, [[-1, 128]], ALU.is_equal, 0.0, base=0, channel_multiplier=1),
       reads=["identf"], writes=["identf"])
    op("pool", lambda e: e.affine_select(negU, negOnes, [[-1, 128]], ALU.is_ge, 0.0, base=0, channel_multiplier=1),
       reads=["negones"], writes=["negU"])
    op("pool", lambda e: e.affine_select(negmask, negmask, [[-1, 128]], ALU.is_ge, 0.0, base=0, channel_multiplier=1),
       reads=["negmask"], writes=["negmask"])
    op("pool", lambda e: e.memset(bhi, 0.0), writes=["bhi"])
    op("pool", lambda e: e.memset(blo, 0.0), writes=["blo"])
    op("pool", lambda e: e.affine_select(ones0, ones_bf, [[0, 128]], ALU.is_equal, 0.0, base=0, channel_multiplier=1),
       reads=["ones"], writes=["ones0"])
    sch.barrier()

    wlist = []
    for l_ in range(n_layers):
        for j in range(4):
            wlist.append([(g["wkv_d"][l_, :, j * 256:(j + 1) * 256], 0, 256)])
        for j in range(4):
            wlist.append([(g["w_in_d"][l_, :, 1024 + j * 256:1024 + (j + 1) * 256], 0, 256)])
        for j in range(8):
            wlist.append([(g["w_in_d"][l_, :, j * 128:(j + 1) * 128], 0, 128),
                          (g["w_in_d"][l_, :, 2048 + j * 128:2048 + (j + 1) * 128], 128, 128)])
        for h_ in range(4):
            cq = 3072 + h_ * 128
            wlist.append([(g["w_in_d"][l_, :, cq:cq + 128], 0, 128), (g["w_in_d"][l_, :, cq + 512:cq + 640], 128, 128)])
            wlist.append([(g["w_in_d"][l_, :, cq + 1024:cq + 1152], 0, 128), (g["w_in_d"][l_, :, cq + 1536:cq + 1664], 128, 128)])
        for h_ in range(4):
            cq = 5120 + h_ * 128
            wlist.append([(g["w_in_d"][l_, :, cq:cq + 128], 0, 128), (g["w_in_d"][l_, :, cq + 512:cq + 640], 128, 128)])
        for n_ in range(8):
            wlist.append([(g["wout_d"][l_, :, n_ * 256:(n_ + 1) * 256], 0, 256)])
    wstate = {"issued": 0, "used": 0}

    def wissue_upto(k):
        while wstate["issued"] < min(k, len(wlist)):
            idx = wstate["issued"]
            wstate["issued"] += 1
            i = idx % 3
            for kk, (src, c0, ncols) in enumerate(wlist[idx]):
                dst = Wr[i][:, :, c0:c0 + ncols]
                s_ = src.rearrange("(c p) f -> p c f", p=128)
                tok = dma("pool", (lambda e, dst=dst, s_=s_: e.dma_start(out=dst, in_=s_)),
                          reads=[], writes=([("W", i)] if kk == 0 else []), key=("W", i))
            sch.lastw[("W", i)] = tok

    def wload(parts=None):
        k = wstate["used"]
        wstate["used"] += 1
        wissue_upto(k + 3)
        return k % 3

    def hkeys(G):
        return [("hT", tb) for tb in range(4 * G, 4 * G + 4)] + [("hTa", tb) for tb in range(4 * G, 4 * G + 4)]

    def proj_fm(slot, c0, G, bank):
        pairs = [(Wr[slot][:, c, c0:c0 + 128], hT[:, c, G * 512:(G + 1) * 512]) for c in range(NC)]

        def fn(e, pairs=pairs, bank=bank):
            for i, (l, r) in enumerate(pairs):
                ins = e.matmul(psb(bank), l, r, start=(i == 0), stop=(i == NC - 1))
            return ins
        op("pe", fn, reads=[("W", slot)] + hkeys(G), writes=[("ps", bank)])

    def norm_transpose(src_ap, gbt, dst_fn, nblk, tag):
        def stA(tb):
            xs = tb % 3
            pz = tb % 2
            xt = XT[xs]
            st_, mv_, t1_ = ST2[pz], MV2[pz], T12[pz]
            dma("sp", (lambda e, xt=xt, tb=tb: e.dma_start(out=xt, in_=src_ap[tb * 128:(tb + 1) * 128, :])),
                reads=[(tag + "src", tb)], writes=[("xt", xs)], key=("xt", xs))
            for k in range(4):
                op("dve", (lambda e, xt=xt, k=k, st_=st_: e.bn_stats(st_[:, k, :], xt[:, k * 512:(k + 1) * 512])),
                   reads=[("xt", xs)], writes=[("st", pz, k)])
            op("dve", lambda e, st_=st_, mv_=mv_: e.bn_aggr(mv_, st_.rearrange("p a b -> p (a b)")),
               reads=[("st", pz, k) for k in range(4)], writes=[("mv", pz)])
            op("dve", lambda e, mv_=mv_, t1_=t1_: e.scalar_tensor_tensor(t1_, mv_[:, 0:1], mv_[:, 0:1], mv_[:, 1:2], ALU.mult, ALU.add),
               reads=[("mv", pz)], writes=[("t1", pz)])
            op("dve", lambda e, t1_=t1_: e.tensor_scalar(t1_, t1_, EPS, None, ALU.add), reads=[("t1", pz)], writes=[("t1", pz)])

        def stR(tb):
            pz = tb % 2
            t1_, rs_ = T12[pz], RS2[pz]
            op("act", lambda e, t1_=t1_: e.activation(t1_, t1_, AF.Ln), reads=[("t1", pz)], writes=[("t1", pz)])
            op("act", lambda e, t1_=t1_, rs_=rs_: e.activation(rs_, t1_, AF.Exp, scale=-0.5), reads=[("t1", pz)], writes=[("rstd", pz)])

        def stB1(tb):
            ns = tb % 2
            xn = XN[ns]
            xt = XT[tb % 3]
            rs_ = RS2[ns]
            op("dve", (lambda e, xn=xn, xt=xt, rs_=rs_: e.scalar_tensor_tensor(xn, xt, rs_, gbt, ALU.mult, ALU.mult)),
               reads=[("xt", tb % 3), ("rstd", ns), tag + "gb"], writes=[("xn", ns)])
            b0 = 2 * (tb % 2)
            for half in range(2):
                bank = b0 + half

                def fn(e, xn=xn, bank=bank, half=half):
                    pb = ps_bf(bank)
                    for cc in range(8):
                        c = half * 8 + cc
                        ins = e.transpose(pb[:, cc * 128:(cc + 1) * 128], xn[:, c * 128:(c + 1) * 128], ident_bf)
                    return ins
                op("pe", fn, reads=[("xn", ns), "ident"], writes=[("ps", bank)])

        def stB2(tb):
            b0 = 2 * (tb % 2)
            for half in range(2):
                bank = b0 + half
                dst, keys = dst_fn(tb, half)
                src3 = ps_bf(bank).rearrange("p (a b) -> p a b", a=8)
                op("act", (lambda e, dst=dst, src3=src3: e.copy(dst, src3)), reads=[("ps", bank)], writes=keys)

        for tb in range(min(2, nblk)):
            stA(tb)
            stR(tb)
        for tb in range(nblk + 1):
            if tb < nblk:
                stB1(tb)
            if 0 <= tb - 1 < nblk:
                stB2(tb - 1)
            if tb + 2 < nblk:
                stA(tb + 2)
                stR(tb + 2)

    def finish_dbg():
        sch.budget = None
        for c in range(NC):
            dma("sp", lambda e, c=c: e.dma_start(out=g["dbg_h"][:, c * S:(c + 1) * S], in_=hT[:, c, :]), reads=[], writes=["dbgh"], key="o0")
            dma("sp", lambda e, c=c: e.dma_start(out=g["dbg_y"][:, c * S:(c + 1) * S], in_=yT[:, c, :]), reads=[], writes=["dbgy"], key="o1")
        sch.barrier()

    if STOP <= 0:
        finish_dbg()
        return
    for l in range(n_layers):
        src_ap = g["x_d"] if l == 0 else g["out_d"]
        dst_ap = g["out_d"]
        dma("sp", lambda e, l=l: e.dma_start(out=GB, in_=g["norm_g_d"][l:l + 1, :].partition_broadcast(128)),
            reads=[], writes=["xgb"], key="c0")
        dma("sp", lambda e, l=l: e.dma_start(out=MGB, in_=g["mng_d"][l:l + 1, :].partition_broadcast(128)),
            reads=[], writes=["mgb"], key="c1")
        dma("sp", lambda e, l=l: e.dma_start(out=qg, in_=g["qg_d"][l:l + 1, :].rearrange("a p -> p a")),
            reads=[], writes=["qg"], key="c2")
        dma("sp", lambda e, l=l: e.dma_start(out=kg, in_=g["kg_d"][l:l + 1, :].rearrange("a p -> p a")),
            reads=[], writes=["kg"], key="c3")
        sguw = view(SCR, F32, [128, 8, 128])
        dma("sp", lambda e, l=l, sguw=sguw: e.dma_start(out=sguw, in_=g["sguw_d"][l].rearrange("g t s -> t g s")),
            reads=[], writes=["sguw"], key="c4")
        bf_ = view(SCR + 4096, F32, [1, 1024])
        bl_ = view(SCR + 8192, F32, [1, 1024])
        dma("sp", lambda e, l=l, bf_=bf_: e.dma_start(out=bf_, in_=g["sgub_d"][l:l + 1].rearrange("a g t -> a (g t)")),
            reads=[], writes=["bf"], key="c5")
        op("dve", lambda e: e.scalar_tensor_tensor(kgq, qg, SCALE, kg, ALU.mult, ALU.mult), reads=["qg", "kg"], writes=["kgq"])
        op("dve", lambda e, bf_=bf_: e.tensor_copy(bhi[0:1, :], bf_), reads=["bf"], writes=["bhi"])
        op("dve", lambda e, bf_=bf_, bl_=bl_: e.tensor_tensor(bl_, bf_, bhi[0:1, :], ALU.subtract), reads=["bf", "bhi"], writes=["bl"])
        op("dve", lambda e, bl_=bl_: e.tensor_copy(blo[0:1, :], bl_), reads=["bl"], writes=["blo"])
        wtf = view(SCR + 12288, F32, [128, 128])
        for gi in range(8):
            op("pe", lambda e, gi=gi, sguw=sguw: e.transpose(psb(7, 128), sguw[:, gi, :], ident_f),
               reads=["sguw", "identf"], writes=[("ps", 7)])
            op("act", lambda e, wtf=wtf: e.copy(wtf, psb(7, 128)), reads=[("ps", 7)], writes=["wtf"])
            op("pool", lambda e, gi=gi, wtf=wtf: e.affine_select(wT[:, gi, :], wtf, [[1, 128]], ALU.is_ge, 0.0,
                                                                  base=0, channel_multiplier=-1),
               reads=["wtf"], writes=["wT"])

        def dst_h(tb, half):
            return hT[:, half * 8:(half + 1) * 8, tb * 128:(tb + 1) * 128], [("hT", tb)] if half == 1 else [("hTa", tb)]
        norm_transpose(src_ap, GB, dst_h, 16, "x")

        if STOP <= 1:
            sch.barrier()
            finish_dbg()
            return
        if dbg and "KBUD" in os.environ:
            sch.budget = int(os.environ["KBUD"])
        memh = view(SCR + 16384, BF16, [128, NC, 256])

        def dst_m(tb, half):
            return memh[:, half * 8:(half + 1) * 8, tb * 128:(tb + 1) * 128], [("memh", tb, half)]
        norm_transpose(g["mem_d"], MGB, dst_m, 2, "m")
        memh_keys = [("memh", tb, half) for tb in range(2) for half in range(2)]
        kf = view(SCR + 24576, F32, [128, 256])
        ksq = view(SCR + 25600, BF16, [128, 256])
        kt_ = view(SCR + 26112, F32, [128, 256])
        rk = view(SCR + 27136, F32, [128, 256])
        for j in range(4):
            slot = wload([(g["wkv_d"][l, :, j * 256:(j + 1) * 256], 0, 256)])
            if j < 2:
                for hh in range(2):
                    h = 2 * j + hh
                    pairs = [(Wr[slot][:, c, hh * 128:(hh + 1) * 128], memh[:, c, :]) for c in range(NC)]

                    def fn(e, pairs=pairs):
                        for i, (l_, r) in enumerate(pairs):
                            ins = e.matmul(psb(4, 256), l_, r, start=(i == 0), stop=(i == NC - 1))
                        return ins
                    op("pe", fn, reads=[("W", slot)] + memh_keys, writes=[("ps", 4)])
                    op("act", lambda e: e.activation(ksq, psb(4, 256), AF.Square), reads=[("ps", 4)], writes=["ksq"])
                    op("dve", lambda e: e.tensor_copy(kf, psb(4, 256)), reads=[("ps", 4), "ksq"], writes=["kf"])
                    op("pe", lambda e: e.matmul(psb(6, 256), ones_bf, ksq, start=True, stop=True),
                       reads=["ksq", "ones"], writes=[("ps", 6)])
                    op("act", lambda e: e.activation(kt_, psb(6, 256), AF.Ln, bias=EPS, scale=1.0 / 128.0),
                       reads=[("ps", 6)], writes=["kt"])
                    op("act", lambda e: e.activation(rk, kt_, AF.Exp, scale=-0.5), reads=["kt"], writes=["rk"])
                    op("dve", lambda e, h=h: e.scalar_tensor_tensor(knT[:, h, :], kf, kgq, rk, ALU.mult, ALU.mult),
                       reads=["kf", "rk", "kgq"], writes=[("knT", h)])
            else:
                jj = j - 2
                for mb in range(2):
                    pairs = [(memh[:, c, mb * 128:(mb + 1) * 128], Wr[slot][:, c, :]) for c in range(NC)]

                    def fn(e, pairs=pairs):
                        for i, (l_, r) in enumerate(pairs):
                            ins = e.matmul(psb(5, 256), l_, r, start=(i == 0), stop=(i == NC - 1))
                        return ins
                    op("pe", fn, reads=[("W", slot)] + memh_keys, writes=[("ps", 5)])
                    op("act", lambda e, mb=mb, jj=jj: e.copy(memv[:, mb, jj * 256:(jj + 1) * 256], psb(5, 256)),
                       reads=[("ps", 5)], writes=[("memv", mb, jj)])
        sch.barrier()

        if STOP <= 2:
            finish_dbg()
            return
        lng = view(SCR + 0, F32, [128, 1024])
        lnb = view(SCR + 4096, F32, [128, 1024])
        dma("sp", lambda e, l=l: e.dma_start(out=lng, in_=g["lng_d"][l:l + 1, :].partition_broadcast(128)),
            reads=[], writes=["lng"], key="c0")
        dma("sp", lambda e, l=l: e.dma_start(out=lnb, in_=g["lnb_d"][l:l + 1, :].partition_broadcast(128)),
            reads=[], writes=["lnb"], key="c1")
        gu_t = [view(SCR + 8192 + i * 2048, F32, [128, 512]) for i in range(2)]
        th_t = [view(SCR + 12288 + i * 2048, F32, [128, 512]) for i in range(2)]
        lnt = [view(SCR + 16384 + i * 4096, F32, [128, 1024]) for i in range(2)]
        it = 0
        for j in range(4):
            slot = wload([(g["w_in_d"][l, :, 1024 + j * 256:1024 + (j + 1) * 256], 0, 256)])
            for tb in range(16):
                bank = 4 + it % 2
                it += 1
                pairs = [(hT[:, c, tb * 128:(tb + 1) * 128], Wr[slot][:, c, :]) for c in range(NC)]

                def fn(e, pairs=pairs, bank=bank):
                    for i, (l_, r) in enumerate(pairs):
                        ins = e.matmul(psb(bank, 256), l_, r, start=(i == 0), stop=(i == NC - 1))
                    return ins
                op("pe", fn, reads=[("W", slot), ("hT", tb), ("hTa", tb)], writes=[("ps", bank)])
                op("act", lambda e, tb=tb, j=j, bank=bank: e.activation(VA[:, tb, j * 256:(j + 1) * 256], psb(bank, 256), AF.Gelu),
                   reads=[("ps", bank)], writes=[("VA", tb, j)])
        def lnA(tb):
            pz = tb % 2
            st_, mv_, t1_, rs_ = ST2[pz], MV2[pz], T12[pz], RS2[pz]
            for k in range(2):
                op("dve", lambda e, tb=tb, k=k, st_=st_: e.bn_stats(st_[:, k, :], VA[:, tb, k * 512:(k + 1) * 512]),
                   reads=[("VA", tb, 2 * k), ("VA", tb, 2 * k + 1)], writes=[("lst", pz, k)])
            op("dve", lambda e, st_=st_, mv_=mv_: e.bn_aggr(mv_, st_[:, 0:2, :].rearrange("p a b -> p (a b)")),
               reads=[("lst", pz, 0), ("lst", pz, 1)], writes=[("lmv", pz)])
            op("dve", lambda e, mv_=mv_, t1_=t1_: e.tensor_scalar(t1_, mv_[:, 1:2], EPS, None, ALU.add),
               reads=[("lmv", pz)], writes=[("lt1", pz)])
            op("pool", lambda e, t1_=t1_, rs_=rs_: e.tensor_tensor(rs_, t1_, neghalf[:, 0:1], ALU.pow),
               reads=[("lt1", pz)], writes=[("lrs", pz)])

        def lnB(tb):
            pz = tb % 2
            mv_, rs_ = MV2[pz], RS2[pz]
            tmp = lnt[tb % 2]
            op("dve", lambda e, tb=tb, tmp=tmp, mv_=mv_: e.scalar_tensor_tensor(tmp, VA[:, tb, :], mv_[:, 0:1], lng, ALU.subtract, ALU.mult),
               reads=[("VA", tb, j) for j in range(4)] + [("lmv", pz), "lng"], writes=[("lnt", tb % 2)])
            op("dve", lambda e, tb=tb, tmp=tmp, rs_=rs_: e.scalar_tensor_tensor(VA[:, tb, :], tmp, rs_, lnb, ALU.mult, ALU.add),
               reads=[("lnt", tb % 2), ("lrs", pz), "lnb"], writes=[("VA", tb, j) for j in range(4)])

        lnA(0)
        it = 0
        for j in range(8):
            slot = wload([(g["w_in_d"][l, :, j * 128:(j + 1) * 128], 0, 128),
                          (g["w_in_d"][l, :, 2048 + j * 128:2048 + (j + 1) * 128], 128, 128)])
            for G in range(4):
                bu = it % 2
                bz = 2 + it % 2
                gu = gu_t[it % 2]
                th = th_t[it % 2]
                if it % 2 == 0:
                    tb_ln = it // 2
                    if tb_ln + 1 < 16:
                        lnA(tb_ln + 1)
                    lnB(tb_ln)
                it += 1
                proj_fm(slot, 0, G, bu)
                proj_fm(slot, 128, G, bz)
                op("act", lambda e, gu=gu, bu=bu: e.activation(gu, psb(bu), AF.Gelu), reads=[("ps", bu)], writes=[("gu", bu)])
                op("act", lambda e, th=th, bz=bz: e.activation(th, psb(bz), AF.Tanh, scale=0.5), reads=[("ps", bz)], writes=[("th", bz)])
                op("dve", lambda e, th=th, bz=bz: e.scalar_tensor_tensor(th, th, 1.0, psb(bz), ALU.add, ALU.mult),
                   reads=[("th", bz), ("ps", bz)], writes=[("th", bz)])
                op("dve", lambda e, th=th, gu=gu, j=j, G=G: e.scalar_tensor_tensor(yT[:, j, G * 512:(G + 1) * 512], th, 0.5, gu, ALU.mult, ALU.mult),
                   reads=[("th", bz), ("gu", bu)], writes=[("yT", j, G)])
        it = 0
        for G in range(4):
            for gi in range(8):
                bank = 6 + it % 2
                it += 1

                def fn(e, G=G, gi=gi, bank=bank):
                    for cc in range(4):
                        c = 4 * G + cc
                        o = psb(bank, 128, cc * 128)
                        e.matmul(o, VA[:, c, gi * 128:(gi + 1) * 128], wT[:, gi, :], start=True, stop=False)
                        e.matmul(o, ones0, bhi[:, gi * 128:(gi + 1) * 128], start=False, stop=False)
                        ins = e.matmul(o, ones0, blo[:, gi * 128:(gi + 1) * 128], start=False, stop=True)
                    return ins
                op("pe", fn, reads=[("VA", 4 * G + cc, jj) for cc in range(4) for jj in range(4)] + ["wT", "bhi", "blo", "ones0"],
                   writes=[("ps", bank)])
                op("dve", lambda e, G=G, gi=gi, bank=bank: e.tensor_tensor(yT[:, gi, G * 512:(G + 1) * 512], psb(bank),
                                                                            yT[:, gi, G * 512:(G + 1) * 512], ALU.mult),
                   reads=[("ps", bank), ("yT", gi, G)], writes=[("yT", gi, G)])
        sch.barrier()

        if STOP <= 3:
            finish_dbg()
            return
        qT = view(SCR + 0, BF16, [128, S])
        kT = view(SCR + 4096, BF16, [128, S])
        vh = view(SCR + 8192, BF16, [128, 16, 128])
        zs = view(SCR + 12288, BF16, [128, S])
        E_t = [view(SCR + 16384 + i * 2048, F32, [128, 512]) for i in range(2)]
        sp_t = [view(SCR + 20480 + i * 1024, BF16, [128, 512]) for i in range(3)]
        R_t = [view(SCR + 23552 + i * 1024, BF16, [128, 512]) for i in range(3)]
        A_t = [view(SCR + 26624 + i * 1024, BF16, [128, 512]) for i in range(3)]
        ez_t = [view(SCR + 29696 + i * 2048, F32, [128, 512]) for i in range(2)]
        assert SCR + 33792 <= 211968
        pj = 0
        for h in range(4):
            c_q = 3072 + h * 128
            slot1 = wload([(g["w_in_d"][l, :, c_q:c_q + 128], 0, 128),
                           (g["w_in_d"][l, :, c_q + 512:c_q + 640], 128, 128)])
            for G in range(4):
                b = pj % 2
                pj += 1
                proj_fm(slot1, 0, G, b)
                op("act", lambda e, G=G, b=b: e.activation(qT[:, G * 512:(G + 1) * 512], psb(b), AF.Copy, scale=SCALE),
                   reads=[("ps", b)], writes=[("qT", G)])
                b = pj % 2
                pj += 1
                proj_fm(slot1, 128, G, b)
                op("dve", lambda e, G=G, b=b: e.tensor_copy(kT[:, G * 512:(G + 1) * 512], psb(b)),
                   reads=[("ps", b)], writes=[("kT", G)])
            slot2 = wload()
            for G in range(4):
                b = pj % 2
                pj += 1
                ez = ez_t[G % 2]
                proj_fm(slot2, 128, G, b)
                op("act", lambda e, ez=ez, b=b: e.activation(ez, psb(b), AF.Exp, scale=-1.0), reads=[("ps", b)], writes=[("ez", G % 2)])
                op("act", lambda e, ez=ez: e.activation(ez, ez, AF.Ln, bias=1.0), reads=[("ez", G % 2)], writes=[("ez", G % 2)])
                op("act", lambda e, ez=ez: e.activation(ez, ez, AF.Exp, scale=-1.0), reads=[("ez", G % 2)], writes=[("ez", G % 2)])
                op("dve", lambda e, ez=ez, G=G, b=b: e.tensor_tensor(zs[:, G * 512:(G + 1) * 512], psb(b), ez, ALU.mult),
                   reads=[("ps", b), ("ez", G % 2)], writes=[("zs", G)])
            for t4 in range(4):
                b = pj % 2
                pj += 1

                def fn(e, t4=t4, b=b, slot2=slot2):
                    for tt in range(4):
                        tb = 4 * t4 + tt
                        for c in range(NC):
                            ins = e.matmul(psb(b, 128, tt * 128), hT[:, c, tb * 128:(tb + 1) * 128], Wr[slot2][:, c, 0:128],
                                           start=(c == 0), stop=(c == NC - 1))
                    return ins
                op("pe", fn, reads=[("W", slot2)] + hkeys(t4), writes=[("ps", b)])
                op("act", lambda e, t4=t4, b=b: e.copy(vh[:, 4 * t4:4 * t4 + 4, :], psb(b).rearrange("p (a b) -> p a b", a=4)),
                   reads=[("ps", b)], writes=[("vh", t4)])
            units = []
            for G in range(4):
                nb = 4 * G + 4
                for k_, bidx in enumerate(range(nb - 1, -1, -1)):
                    i_d = bidx - 4 * G
                    c0 = 128 * i_d if i_d >= 0 else 0
                    units.append(dict(G=G, bidx=bidx, diag=(i_d >= 0), c0=c0, ncol=512 - c0, first=(k_ == 0), last=(bidx == 0)))
            nU = len(units)

            def S1(u):
                U = units[u]
                G, bidx, diag, c0, ncol = U["G"], U["bidx"], U["diag"], U["c0"], U["ncol"]
                zb = 2 + u % 4
                E = E_t[u % 2]
                sp = sp_t[u % 3]
                kblk = kT[:, bidx * 128:(bidx + 1) * 128]
                qcols = qT[:, G * 512 + c0:G * 512 + 512]

                def fz(e):
                    ins = e.matmul(psb(zb, ncol), kblk, qcols, start=True, stop=not diag)
                    if diag:
                        ins = e.matmul(psb(zb, 128), ident_bf, negmask, start=False, stop=True)
                    return ins
                op("pe", fz, reads=[("kT", bidx // 4), ("qT", G), "ident", "negmask"], writes=[("ps", zb)])
                op("act", lambda e: e.activation(E[:, 0:ncol], psb(zb, ncol), AF.Exp), reads=[("ps", zb)], writes=[("E", u % 2)])

            def S1b(u):
                ncol = units[u]["ncol"]
                E = E_t[u % 2]
                sp = sp_t[u % 3]
                op("act", lambda e: e.activation(sp[:, 0:ncol], E[:, 0:ncol], AF.Ln, bias=1.0),
                   reads=[("E", u % 2)], writes=[("sp", u % 3)])

            def S2(u):
                U = units[u]
                G, bidx, diag, c0, ncol, first = U["G"], U["bidx"], U["diag"], U["c0"], U["ncol"], U["first"]
                zb = 2 + u % 4
                sp = sp_t[u % 3]
                A = A_t[u % 3]
                Rcur = R_t[u % 3]
                Rnext = R_t[(u + 1) % 3]

                kblk2 = kT[:, bidx * 128:(bidx + 1) * 128]
                qcols2 = qT[:, G * 512 + c0:G * 512 + 512]

                def fp(e):
                    if os.environ.get("KZ"):
                        e.matmul(psb(zb, ncol), kblk2, qcols2, start=True, stop=not diag)
                        if diag:
                            e.matmul(psb(zb, 128), ident_bf, negmask, start=False, stop=True)
                    ins = e.matmul(psb(zb, ncol), negU, sp[:, 0:ncol], start=False, stop=True, skip_group_check=True)
                    if not first:
                        if diag:
                            ins = e.matmul(psb(zb, ncol - 128, 128), negOnes, Rcur[:, c0 + 128:512], start=False, stop=True,
                                           skip_group_check=True)
                        else:
                            ins = e.matmul(psb(zb, 512), negOnes, Rcur, start=False, stop=True, skip_group_check=True)
                    return ins
                op("pe", fp, reads=[("sp", u % 3), ("R", u % 3), "negU", "negones"], writes=[("ps", zb)])
                if not U["last"]:
                    if first:
                        op("pool", lambda e: e.tensor_copy(Rnext[:, c0:512], sp[:, 0:ncol]),
                           reads=[("sp", u % 3)], writes=[("R", (u + 1) % 3)])
                    elif diag:
                        op("pool", lambda e: e.tensor_copy(Rnext[:, c0:c0 + 128], sp[:, 0:128]),
                           reads=[("sp", u % 3)], writes=[("R", (u + 1) % 3)])
                        op("pool", lambda e: e.tensor_tensor(Rnext[:, c0 + 128:512], Rcur[:, c0 + 128:512], sp[:, 128:ncol], ALU.add),
                           reads=[("sp", u % 3), ("R", u % 3)], writes=[("R", (u + 1) % 3)])
                    else:
                        op("pool", lambda e: e.tensor_tensor(Rnext, Rcur, sp, ALU.add),
                           reads=[("sp", u % 3), ("R", u % 3)], writes=[("R", (u + 1) % 3)])

            def S2b(u):
                ncol = units[u]["ncol"]
                zb = 2 + u % 4
                A = A_t[u % 3]
                op("act", lambda e: e.activation(A[:, 0:ncol], psb(zb, ncol), AF.Exp), reads=[("ps", zb)], writes=[("A", u % 3)])

            def S3(u, h=h):
                U = units[u]
                G, bidx, diag, c0, ncol, first, last = U["G"], U["bidx"], U["diag"], U["c0"], U["ncol"], U["first"], U["last"]
                yb = 6 + G % 2
                A = A_t[u % 3]
                vblk = vh[:, bidx, :]

                def fy(e):
                    if first:
                        ins = e.matmul(psb(yb, ncol, c0), vblk, A[:, 0:ncol], start=True, stop=True)
                    elif diag:
                        e.matmul(psb(yb, 128, c0), vblk, A[:, 0:128], start=False, stop=True, skip_group_check=True)
                        ins = e.matmul(psb(yb, ncol - 128, c0 + 128), vblk, A[:, 128:ncol], start=False, stop=True,
                                       skip_group_check=True)
                    else:
                        ins = e.matmul(psb(yb, 512), vblk, A, start=False, stop=True, skip_group_check=True)
                    return ins
                op("pe", fy, reads=[("vh", bidx // 4), ("A", u % 3)], writes=[("ps", yb)])
                if last:
                    op("dve", lambda e: e.tensor_tensor(yT[:, 8 + h, G * 512:(G + 1) * 512], psb(yb),
                                                        zs[:, G * 512:(G + 1) * 512], ALU.mult),
                       reads=[("ps", yb), ("zs", G)], writes=[("yT", 8 + h, G)])

            for i in range(nU + 3):
                if i < nU:
                    S1(i)
                if 0 <= i - 1 < nU:
                    S2(i - 1)
                if 0 <= i - 2 < nU:
                    S2b(i - 2)
                if i < nU:
                    S1b(i)
                if 0 <= i - 3 < nU:
                    S3(i - 3)
        sch.barrier()

        if STOP <= 4:
            finish_dbg()
            return
        def cbuf(p_):
            o = SCR + p_ * 15360
            return dict(sq=view(o, BF16, [128, 512]), tq=view(o + 1024, F32, [128, 512]), rq=view(o + 3072, F32, [128, 512]),
                        qh=view(o + 5120, BF16, [128, 512]), ez=view(o + 6144, F32, [128, 512]), zsc=view(o + 8192, BF16, [128, 512]),
                        PT=[view(o + 9216 + i * 1024, BF16, [128, 512]) for i in range(2)],
                        rd=view(o + 11264, F32, [128, 512]), t2=view(o + 13312, F32, [128, 512]))
        CB = [cbuf(0), cbuf(1)]
        assert SCR + 2 * 15360 <= 211968
        cits = [(h, G) for h in range(4) for G in range(4)]
        cslot = {}

        def c_projq(it):
            h, G = cits[it]
            if G == 0:
                cslot[h] = wload()
            proj_fm(cslot[h], 0, G, it % 2)

        def c_projz(it):
            h, G = cits[it]
            proj_fm(cslot[h], 128, G, 2 + it % 2)

        c_projq(0)
        c_projz(0)
        for it, (h, G) in enumerate(cits):
            p_ = it % 2
            cb = CB[p_]
            qb = it % 2
            zb = 2 + it % 2
            sq, tq, rq, qh, ez, zsc, PT, rd, t2 = (cb["sq"], cb["tq"], cb["rq"], cb["qh"], cb["ez"], cb["zsc"], cb["PT"], cb["rd"], cb["t2"])
            K = lambda n_: (n_, p_)
            op("act", lambda e, sq=sq, qb=qb: e.activation(sq, psb(qb), AF.Square), reads=[("ps", qb)], writes=[K("sq")])
            op("pe", lambda e, sq=sq: e.matmul(psb(6), ones_bf, sq, start=True, stop=True), reads=[K("sq"), "ones"], writes=[("ps", 6)])
            if it + 1 < len(cits):
                c_projq(it + 1)
            op("act", lambda e, tq=tq: e.activation(tq, psb(6), AF.Ln, bias=EPS, scale=1.0 / 128.0), reads=[("ps", 6)], writes=[K("tq")])
            op("act", lambda e, tq=tq, rq=rq: e.activation(rq, tq, AF.Exp, scale=-0.5), reads=[K("tq")], writes=[K("rq")])
            op("dve", lambda e, qh=qh, rq=rq, qb=qb: e.tensor_tensor(qh, psb(qb), rq, ALU.mult),
               reads=[("ps", qb), K("sq"), K("rq")], writes=[K("qh")])
            op("act", lambda e, ez=ez, zb=zb: e.activation(ez, psb(zb), AF.Exp, scale=-1.0), reads=[("ps", zb)], writes=[K("ez")])
            op("act", lambda e, ez=ez: e.activation(ez, ez, AF.Ln, bias=1.0), reads=[K("ez")], writes=[K("ez")])
            op("act", lambda e, ez=ez: e.activation(ez, ez, AF.Exp, scale=-1.0), reads=[K("ez")], writes=[K("ez")])
            op("dve", lambda e, zsc=zsc, ez=ez, zb=zb: e.tensor_tensor(zsc, psb(zb), ez, ALU.mult), reads=[("ps", zb), K("ez")], writes=[K("zsc")])
            for mb in range(2):
                op("pe", lambda e, mb=mb, h=h, qh=qh: e.matmul(psb(4 + mb), knT[:, h, mb * 128:(mb + 1) * 128], qh, start=True, stop=True),
                   reads=[("knT", h), K("qh")], writes=[("ps", 4 + mb)])
            if it + 1 < len(cits):
                c_projz(it + 1)
            for mb in range(2):
                op("act", lambda e, mb=mb, PT=PT: e.activation(PT[mb], psb(4 + mb), AF.Exp), reads=[("ps", 4 + mb)], writes=[K("PT%d" % mb)])

            def fo(e, h=h, PT=PT):
                e.matmul(psb(7), memv[:, 0, h * 128:(h + 1) * 128], PT[0], start=True, stop=False)
                e.matmul(psb(7), memv[:, 1, h * 128:(h + 1) * 128], PT[1], start=False, stop=True)
                e.matmul(psb(6), ones_bf, PT[0], start=True, stop=False)
                return e.matmul(psb(6), ones_bf, PT[1], start=False, stop=True)
            op("pe", fo, reads=[K("PT0"), K("PT1"), "ones"] + [("memv", mb, jj) for mb in range(2) for jj in range(2)],
               writes=[("ps", 7), ("ps", 6)])
            op("act", lambda e, rd=rd: e.activation(rd, psb(6), AF.Ln), reads=[("ps", 6)], writes=[K("rd")])
            op("act", lambda e, rd=rd: e.activation(rd, rd, AF.Exp, scale=-1.0), reads=[K("rd")], writes=[K("rd")])
            op("dve", lambda e, t2=t2, rd=rd: e.tensor_tensor(t2, psb(7), rd, ALU.mult), reads=[("ps", 7), K("rd")], writes=[K("t2")])
            op("dve", lambda e, h=h, G=G, t2=t2, zsc=zsc: e.tensor_tensor(yT[:, 12 + h, G * 512:(G + 1) * 512], t2, zsc, ALU.mult),
               reads=[K("t2"), K("zsc")], writes=[("yT", 12 + h, G)])
        sch.barrier()

        if dbg and l == int(os.environ.get("KL", "0")):
            finish_dbg()
        if STOP <= 5:
            return

        NXR = 6
        xr_t = [view(HT_OFF + i * 1024, F32, [128, 256]) for i in range(NXR)]
        os_t = [view(HT_OFF + 8192 + i * 1024, F32, [128, 256]) for i in range(4)]
        p3 = [(n, tb) for n in range(8) for tb in range(16)]

        def p3_load(k, src_ap=src_ap):
            n, tb = p3[k]
            xr = xr_t[k % NXR]
            dma("sp", lambda e: e.dma_start(out=xr, in_=src_ap[tb * 128:(tb + 1) * 128, n * 256:(n + 1) * 256]),
                reads=[], writes=[("xr", k % NXR)], key=("xr", k % NXR))

        for k in range(3):
            p3_load(k)
        slot = None
        for k, (n, tb) in enumerate(p3):
            if tb == 0:
                slot = wload()
            if k + 3 < len(p3):
                p3_load(k + 3)
            bank = k % 8
            xr = xr_t[k % NXR]
            ost = os_t[k % 4]
            pairs = [(yT[:, c, tb * 128:(tb + 1) * 128], Wr[slot][:, c, :]) for c in range(NC)]

            def fn(e, pairs=pairs, bank=bank):
                for i, (l_, r) in enumerate(pairs):
                    ins = e.matmul(psb(bank, 256), l_, r, start=(i == 0), stop=(i == NC - 1))
                return ins
            op("pe", fn, reads=[("W", slot)], writes=[("ps", bank)])
            op("dve", lambda e, ost=ost, xr=xr, bank=bank: e.tensor_tensor(ost, psb(bank, 256), xr, ALU.add),
               reads=[("ps", bank), ("xr", k % NXR)], writes=[("os", k % 4)])
            dma("sp", lambda e, ost=ost, tb=tb, n=n, dst_ap=dst_ap: e.dma_start(out=dst_ap[tb * 128:(tb + 1) * 128, n * 256:(n + 1) * 256], in_=ost),
                reads=[("os", k % 4)], writes=[("xdst", tb, n)], key=("os", k % 4))
        sch.barrier()


_NC_CACHE = {}


def kernel(x, mem, norm_g, w_in, sgu_ln_g, sgu_ln_b, sgu_w, sgu_b, mem_norm_g, w_mem_kv, q_norm_g, k_norm_g, w_out):
    if "nc" not in _NC_CACHE:
        _NC_CACHE["nc"] = build_nc()
    nc = _NC_CACHE["nc"]
    f = lambda a: np.ascontiguousarray(np.asarray(a, dtype=np.float32))
    shared = {"norm_g": f(norm_g), "w_in": f(w_in), "sgu_ln_g": f(sgu_ln_g), "sgu_ln_b": f(sgu_ln_b),
              "sgu_w": f(sgu_w), "sgu_b": f(sgu_b), "mem_norm_g": f(mem_norm_g), "w_mem_kv": f(w_mem_kv),
              "q_norm_g": f(q_norm_g), "k_norm_g": f(k_norm_g), "w_out": f(w_out)}
    x = f(x)
    mem = f(mem)
    in_maps = [dict(shared, x=x[b], mem=mem[b]) for b in range(8)]
    res = run_bass_kernel_spmd(nc, in_maps, core_ids=list(range(8)))
    return np.stack([np.asarray(r["out"], dtype=np.float32) for r in res.results], axis=0)
```

```python
import math
import numpy as np
import concourse.bass as bass
import concourse.mybir as mybir
from concourse.bass_utils import run_bass_kernel_spmd

F32 = mybir.dt.float32
BF16 = mybir.dt.bfloat16
U8 = mybir.dt.uint8
AF = mybir.ActivationFunctionType
ALU = mybir.AluOpType

S = 2048
D = 2048
MEM = 256
NL = 2
INW = 6144
EPS = 1e-6
NEG = -30000.0
NC = 16
SCALE = 1.0 / math.sqrt(128.0)

ENGS = ("pe", "act", "dve", "pool", "sp")


class Sched:
    def __init__(self, nc, esems, dma_sems):
        self.nc = nc
        self.prog = {e: [] for e in ENGS}
        self.cnt = {e: 0 for e in ENGS}
        self.esem = esems
        self.free_dsem = list(dma_sems)
        self.dsem = {}
        self.seen = {e: {} for e in ENGS}
        self.lastw = {}
        self.readers = {}

    def _waits(self, eng, reads, writes):
        toks = []
        for r in reads:
            t = self.lastw.get(r)
            if t is not None:
                toks.append(("raw", t))
        for w in writes:
            t = self.lastw.get(w)
            if t is not None:
                toks.append(("waw", t))
            for t in self.readers.get(w, ()):
                toks.append(("war", t))
        waits = []
        seen = self.seen[eng]
        for kind, (key, val, peng) in toks:
            if peng == eng and eng == "pe":
                continue
            if seen.get(key, 0) >= val:
                continue
            seen[key] = val
            waits.append((key, val))
        return waits

    def _commit(self, tok, reads, writes):
        for r in reads:
            self.readers.setdefault(r, []).append(tok)
        for w in writes:
            self.lastw[w] = tok
            self.readers[w] = []

    budget = None

    def _drop(self):
        if self.budget is None:
            return False
        if self.budget <= 0:
            return True
        self.budget -= 1
        return False

    def op(self, eng, fn, reads=(), writes=()):
        if self._drop():
            return
        waits = self._waits(eng, reads, writes)
        self.cnt[eng] += 1
        tok = (eng, self.cnt[eng], eng)
        self.prog[eng].append((waits, fn, eng, 1))
        self._commit(tok, reads, writes)

    def dma(self, queue, fn, reads, writes, key, force=False):
        if not force and self._drop():
            return (("d", key), 0, "dma")
        waits = self._waits(queue, reads, writes)
        if key not in self.dsem:
            self.dsem[key] = [self.free_dsem.pop(), 0]
        self.dsem[key][1] += 16
        tok = (("d", key), self.dsem[key][1], "dma")
        self.prog[queue].append((waits, fn, ("d", key), 16))
        self._commit(tok, reads, writes)
        return tok

    def semh(self, key):
        if isinstance(key, tuple) and key[0] == "d":
            return self.dsem[key[1]][0]
        return self.esem[key]

    def barrier(self):
        for e in ENGS:
            waits = []
            for o in ENGS:
                if o != e and self.cnt[o] > 0 and self.seen[e].get(o, 0) < self.cnt[o]:
                    self.seen[e][o] = self.cnt[o]
                    waits.append((o, self.cnt[o]))
            for k, (h, c) in self.dsem.items():
                kk = ("d", k)
                if c > 0 and self.seen[e].get(kk, 0) < c:
                    self.seen[e][kk] = c
                    waits.append((kk, c))
            if waits:
                self.prog[e].append((waits, None, None, 0))
        self.lastw = {}
        self.readers = {}

    def replay(self, eng, e):
        for waits, fn, skey, inc in self.prog[eng]:
            for key, val in waits:
                e.wait_ge(self.semh(key), val)
            if fn is None:
                continue
            ins = fn(e)
            ins.then_inc(self.semh(skey), inc)


def build_nc(n_layers=NL, dbg=False):
    nc = bass.Bass("TRN2", target_bir_lowering=False)
    x_d = nc.dram_tensor("x", [S, D], F32, kind="ExternalInput").ap()
    mem_d = nc.dram_tensor("mem", [MEM, D], F32, kind="ExternalInput").ap()
    norm_g_d = nc.dram_tensor("norm_g", [NL, D], F32, kind="ExternalInput").ap()
    w_in_d = nc.dram_tensor("w_in", [NL, D, INW], F32, kind="ExternalInput").ap()
    lng_d = nc.dram_tensor("sgu_ln_g", [NL, 1024], F32, kind="ExternalInput").ap()
    lnb_d = nc.dram_tensor("sgu_ln_b", [NL, 1024], F32, kind="ExternalInput").ap()
    sguw_d = nc.dram_tensor("sgu_w", [NL, 8, 128, 128], F32, kind="ExternalInput").ap()
    sgub_d = nc.dram_tensor("sgu_b", [NL, 8, 128], F32, kind="ExternalInput").ap()
    mng_d = nc.dram_tensor("mem_norm_g", [NL, D], F32, kind="ExternalInput").ap()
    wkv_d = nc.dram_tensor("w_mem_kv", [NL, D, 1024], F32, kind="ExternalInput").ap()
    qg_d = nc.dram_tensor("q_norm_g", [NL, 128], F32, kind="ExternalInput").ap()
    kg_d = nc.dram_tensor("k_norm_g", [NL, 128], F32, kind="ExternalInput").ap()
    wout_d = nc.dram_tensor("w_out", [NL, D, D], F32, kind="ExternalInput").ap()
    out_d = nc.dram_tensor("out", [S, D], F32, kind="ExternalOutput").ap()
    x1_d = nc.dram_tensor("x1_scratch", [S, D], F32).ap()
    if dbg:
        dbg_h = nc.dram_tensor("dbg_h", [128, NC * S], BF16, kind="ExternalOutput").ap()
        dbg_y = nc.dram_tensor("dbg_y", [128, NC * S], BF16, kind="ExternalOutput").ap()

    ARENA = 211968
    arena = nc.alloc_sbuf_tensor("arena", [128, ARENA], U8)
    ps = [nc.alloc_psum_tensor("ps%d" % i, [128, 512], F32) for i in range(8)]

    def view(off, dt, shape):
        esz = 4 if dt == F32 else 2
        n = 1
        for s_ in shape[1:]:
            n *= s_
        ap = arena[0:shape[0], off:off + n * esz].bitcast(dt)
        if len(shape) == 3:
            ap = ap.rearrange("p (a b) -> p a b", a=shape[1])
        return ap

    HT_OFF = 0
    YT_OFF = 65536
    W_OFF = 131072
    C_OFF = W_OFF + 3 * 8192
    SCR = C_OFF + 14592
    assert SCR + 32768 <= ARENA

    hT = view(HT_OFF, BF16, [128, NC, S])
    yT = view(YT_OFF, BF16, [128, NC, S])
    VA = view(YT_OFF + 32768, BF16, [128, 16, 1024])
    Wr = [view(W_OFF + i * 8192, BF16, [128, NC, 256]) for i in range(3)]
    ident_bf = view(C_OFF + 0, BF16, [128, 128])
    negU = view(C_OFF + 256, BF16, [128, 128])
    negOnes = view(C_OFF + 512, BF16, [128, 128])
    ones_bf = view(C_OFF + 768, BF16, [128, 128])
    negmask = view(C_OFF + 1024, BF16, [128, 128])
    ident_f = view(C_OFF + 1280, F32, [128, 128])
    neghalf = view(C_OFF + 1792, F32, [128, 512])
    kgq = view(C_OFF + 3840, F32, [128, 1])
    qg = view(C_OFF + 3844, F32, [128, 1])
    kg = view(C_OFF + 3848, F32, [128, 1])
    st = view(C_OFF + 3872, F32, [128, 4, 6])
    mv = view(C_OFF + 3968, F32, [128, 2])
    t1 = view(C_OFF + 3976, F32, [128, 1])
    rstd = view(C_OFF + 3980, F32, [128, 1])
    ST2 = [st, view(C_OFF + 3984, F32, [128, 4, 6])]
    MV2 = [mv, view(C_OFF + 4080, F32, [128, 2])]
    T12 = [t1, view(C_OFF + 4088, F32, [128, 1])]
    RS2 = [rstd, view(C_OFF + 4092, F32, [128, 1])]
    knT = view(C_OFF + 4096, BF16, [128, 4, 256])
    memv = view(C_OFF + 6144, BF16, [128, 2, 512])
    wT = view(C_OFF + 8192, BF16, [128, 8, 128])
    bhi = view(C_OFF + 10240, BF16, [128, 1024])
    blo = view(C_OFF + 12288, BF16, [128, 1024])
    ones0 = view(C_OFF + 14336, BF16, [128, 128])
    XT = [view(YT_OFF + i * 8192, F32, [128, D]) for i in range(3)]
    XN = [view(YT_OFF + 24576 + i * 4096, BF16, [128, D]) for i in range(2)]
    GB = view(YT_OFF + 32768, F32, [128, D])
    MGB = view(YT_OFF + 40960, F32, [128, D])
    XS = [view(YT_OFF + 49152 + i * 8192, F32, [128, D]) for i in range(2)]

    with nc.semaphore("s_pe") as s_pe, nc.semaphore("s_act") as s_act, \
            nc.semaphore("s_dve") as s_dve, nc.semaphore("s_pool") as s_pool, \
            nc.semaphore("s_sp") as s_sp:
        import contextlib
        with contextlib.ExitStack() as es:
            dma_sems = [es.enter_context(nc.semaphore("dsem%d" % i)) for i in range(24)]
            sch = Sched(nc, {"pe": s_pe, "act": s_act, "dve": s_dve, "pool": s_pool, "sp": s_sp}, dma_sems)
            _emit_program(nc, sch, locals(), n_layers, dbg)
            with nc.Block() as block:
                @block.tensor
                def _(e):
                    sch.replay("pe", e)

                @block.scalar
                def _(e):
                    sch.replay("act", e)

                @block.vector
                def _(e):
                    sch.replay("dve", e)

                @block.gpsimd
                def _(e):
                    sch.replay("pool", e)

                @block.sync
                def _(e):
                    sch.replay("sp", e)
    return nc


def _emit_program(nc, sch, V, n_layers, dbg):
    import os
    STOP = int(os.environ.get("KSTOP", "99")) if dbg else 99
    g = V
    ps = g["ps"]
    hT, yT, VA, Wr = g["hT"], g["yT"], g["VA"], g["Wr"]
    view = g["view"]
    SCR = g["SCR"]
    HT_OFF = g["HT_OFF"]
    ident_bf, negU, negOnes, ones_bf, negmask, ident_f, neghalf = (
        g["ident_bf"], g["negU"], g["negOnes"], g["ones_bf"], g["negmask"], g["ident_f"], g["neghalf"])
    kgq, qg, kg, st, mv, t1, rstd = g["kgq"], g["qg"], g["kg"], g["st"], g["mv"], g["t1"], g["rstd"]
    knT, memv, wT, bhi, blo, ones0 = g["knT"], g["memv"], g["wT"], g["bhi"], g["blo"], g["ones0"]
    XT, XN, GB, MGB = g["XT"], g["XN"], g["GB"], g["MGB"]
    ST2, MV2, T12, RS2 = g["ST2"], g["MV2"], g["T12"], g["RS2"]
    XS = g["XS"]
    op, dma = sch.op, sch.dma

    def psb(i, n=512, off=0):
        return ps[i][:, off:off + n]

    def ps_bf(i):
        return ps[i][:, :].bitcast(BF16)

    op("pool", lambda e: e.memset(ones_bf, 1.0), writes=["ones"])
    op("pool", lambda e: e.memset(negOnes, -1.0), writes=["negones"])
    op("pool", lambda e: e.memset(neghalf, -0.5), writes=["neghalf"])
    op("pool", lambda e: e.memset(negmask, NEG), writes=["negmask"])
    op("pool", lambda e: e.memset(ident_f, 1.0), writes=["identf"])
    op("pool", lambda e: e.affine_select(ident_bf, ones_bf, [[-1, 128]], ALU.is_equal, 0.0, base=0, channel_multiplier=1),
       reads=["ones"], writes=["ident"])
    op("pool", lambda e: e.affine_select(ident_f, ident_f, [[-1, 128]], ALU.is_equal, 0.0, base=0, channel_multiplier=1),
       reads=["identf"], writes=["identf"])
    op("pool", lambda e: e.affine_select(negU, negOnes, [[-1, 128]], ALU.is_ge, 0.0, base=0, channel_multiplier=1),
       reads=["negones"], writes=["negU"])
    op("pool", lambda e: e.affine_select(negmask, negmask, [[-1, 128]], ALU.is_ge, 0.0, base=0, channel_multiplier=1),
       reads=["negmask"], writes=["negmask"])
    op("pool", lambda e: e.memset(bhi, 0.0), writes=["bhi"])
    op("pool", lambda e: e.memset(blo, 0.0), writes=["blo"])
    op("pool", lambda e: e.affine_select(ones0, ones_bf, [[0, 128]], ALU.is_equal, 0.0, base=0, channel_multiplier=1),
       reads=["ones"], writes=["ones0"])
    sch.barrier()

    wlist = []
    for l_ in range(n_layers):
        for j in range(4):
            wlist.append([(g["wkv_d"][l_, :, j * 256:(j + 1) * 256], 0, 256)])
        for j in range(4):
            wlist.append([(g["w_in_d"][l_, :, 1024 + j * 256:1024 + (j + 1) * 256], 0, 256)])
        for j in range(8):
            wlist.append([(g["w_in_d"][l_, :, j * 128:(j + 1) * 128], 0, 128),
                          (g["w_in_d"][l_, :, 2048 + j * 128:2048 + (j + 1) * 128], 128, 128)])
        for h_ in range(4):
            cq = 3072 + h_ * 128
            wlist.append([(g["w_in_d"][l_, :, cq:cq + 128], 0, 128), (g["w_in_d"][l_, :, cq + 512:cq + 640], 128, 128)])
            wlist.append([(g["w_in_d"][l_, :, cq + 1024:cq + 1152], 0, 128), (g["w_in_d"][l_, :, cq + 1536:cq + 1664], 128, 128)])
        for h_ in range(4):
            cq = 5120 + h_ * 128
            wlist.append([(g["w_in_d"][l_, :, cq:cq + 128], 0, 128), (g["w_in_d"][l_, :, cq + 512:cq + 640], 128, 128)])
        for n_ in range(8):
            wlist.append([(g["wout_d"][l_, :, n_ * 256:(n_ + 1) * 256], 0, 256)])
    wstate = {"issued": 0, "used": 0}

    def wissue_upto(k):
        while wstate["issued"] < min(k, len(wlist)):
            idx = wstate["issued"]
            wstate["issued"] += 1
            i = idx % 3
            for kk, (src, c0, ncols) in enumerate(wlist[idx]):
                dst = Wr[i][:, :, c0:c0 + ncols]
                s_ = src.rearrange("(c p) f -> p c f", p=128)
                tok = dma("pool", (lambda e, dst=dst, s_=s_: e.dma_start(out=dst, in_=s_)),
                          reads=[], writes=([("W", i)] if kk == 0 else []), key=("W", i))
            sch.lastw[("W", i)] = tok

    def wload(parts=None):
        k = wstate["used"]
        wstate["used"] += 1
        wissue_upto(k + 3)
        return k % 3

    def hkeys(G):
        return [("hT", tb) for tb in range(4 * G, 4 * G + 4)] + [("hTa", tb) for tb in range(4 * G, 4 * G + 4)]

    def proj_fm(slot, c0, G, bank):
        pairs = [(Wr[slot][:, c, c0:c0 + 128], hT[:, c, G * 512:(G + 1) * 512]) for c in range(NC)]

        def fn(e, pairs=pairs, bank=bank):
            for i, (l, r) in enumerate(pairs):
                ins = e.matmul(psb(bank), l, r, start=(i == 0), stop=(i == NC - 1))
            return ins
        op("pe", fn, reads=[("W", slot)] + hkeys(G), writes=[("ps", bank)])

    def norm_transpose(src_ap, gbt, dst_fn, nblk, tag):
        def stA(tb):
            xs = tb % 3
            xt = XT[xs]
            dma("sp", (lambda e, xt=xt, tb=tb: e.dma_start(out=xt, in_=src_ap[tb * 128:(tb + 1) * 128, :])),
                reads=[(tag + "src", tb)], writes=[("xt", xs)], key=("xt", xs))

        def stR(tb):
            xs = tb % 3
            pz = tb % 2
            xt = XT[xs]
            t1_, rs_ = T12[pz], RS2[pz]
            junk = XS[0].bitcast(BF16)[:, 0:D]
            op("act", lambda e, xt=xt, t1_=t1_, junk=junk: e.activation(junk, xt, AF.Square, accum_out=t1_),
               reads=[("xt", xs)], writes=[("t1", pz), "junk"])
            op("act", lambda e, t1_=t1_: e.activation(t1_, t1_, AF.Ln, bias=EPS, scale=1.0 / D), reads=[("t1", pz)], writes=[("t1", pz)])
            op("act", lambda e, t1_=t1_, rs_=rs_: e.activation(rs_, t1_, AF.Exp, scale=-0.5), reads=[("t1", pz)], writes=[("rstd", pz)])

        def stB1(tb):
            ns = tb % 2
            xn = XN[ns]
            xt = XT[tb % 3]
            rs_ = RS2[ns]
            op("dve", (lambda e, xn=xn, xt=xt, rs_=rs_: e.scalar_tensor_tensor(xn, xt, rs_, gbt, ALU.mult, ALU.mult)),
               reads=[("xt", tb % 3), ("rstd", ns), tag + "gb"], writes=[("xn", ns)])
            b0 = 2 * (tb % 2)
            for half in range(2):
                bank = b0 + half

                def fn(e, xn=xn, bank=bank, half=half):
                    pb = ps_bf(bank)
                    for cc in range(8):
                        c = half * 8 + cc
                        ins = e.transpose(pb[:, cc * 128:(cc + 1) * 128], xn[:, c * 128:(c + 1) * 128], ident_bf)
                    return ins
                op("pe", fn, reads=[("xn", ns), "ident"], writes=[("ps", bank)])

        def stB2(tb):
            b0 = 2 * (tb % 2)
            for half in range(2):
                bank = b0 + half
                dst, keys = dst_fn(tb, half)
                src3 = ps_bf(bank).rearrange("p (a b) -> p a b", a=8)
                if half == 0:
                    op("act", (lambda e, dst=dst, src3=src3: e.copy(dst, src3)), reads=[("ps", bank)], writes=keys)
                else:
                    op("dve", (lambda e, dst=dst, src3=src3: e.tensor_copy(dst, src3)), reads=[("ps", bank)], writes=keys)

        for tb in range(min(2, nblk)):
            stA(tb)
            stR(tb)
        for tb in range(nblk + 1):
            if tb < nblk:
                stB1(tb)
            if 0 <= tb - 1 < nblk:
                stB2(tb - 1)
            if tb + 2 < nblk:
                stA(tb + 2)
                stR(tb + 2)

    def finish_dbg():
        sch.budget = None
        for c in range(NC):
            dma("sp", lambda e, c=c: e.dma_start(out=g["dbg_h"][:, c * S:(c + 1) * S], in_=hT[:, c, :]), reads=[], writes=["dbgh"], key="o0")
            dma("sp", lambda e, c=c: e.dma_start(out=g["dbg_y"][:, c * S:(c + 1) * S], in_=yT[:, c, :]), reads=[], writes=["dbgy"], key="o1")
        sch.barrier()

    if STOP <= 0:
        finish_dbg()
        return
    for l in range(n_layers):
        src_ap = g["x_d"] if l == 0 else g["out_d"]
        dst_ap = g["out_d"]
        dma("sp", lambda e, l=l: e.dma_start(out=GB, in_=g["norm_g_d"][l:l + 1, :].partition_broadcast(128)),
            reads=[], writes=["xgb"], key="c0")
        dma("sp", lambda e, l=l: e.dma_start(out=MGB, in_=g["mng_d"][l:l + 1, :].partition_broadcast(128)),
            reads=[], writes=["mgb"], key="c1")
        dma("sp", lambda e, l=l: e.dma_start(out=qg, in_=g["qg_d"][l:l + 1, :].rearrange("a p -> p a")),
            reads=[], writes=["qg"], key="c2")
        dma("sp", lambda e, l=l: e.dma_start(out=kg, in_=g["kg_d"][l:l + 1, :].rearrange("a p -> p a")),
            reads=[], writes=["kg"], key="c3")
        sguw = view(SCR, F32, [128, 8, 128])
        dma("sp", lambda e, l=l, sguw=sguw: e.dma_start(out=sguw, in_=g["sguw_d"][l].rearrange("g t s -> t g s")),
            reads=[], writes=["sguw"], key="c4")
        bf_ = view(SCR + 4096, F32, [1, 1024])
        bl_ = view(SCR + 8192, F32, [1, 1024])
        dma("sp", lambda e, l=l, bf_=bf_: e.dma_start(out=bf_, in_=g["sgub_d"][l:l + 1].rearrange("a g t -> a (g t)")),
            reads=[], writes=["bf"], key="c5")
        op("dve", lambda e: e.scalar_tensor_tensor(kgq, qg, SCALE, kg, ALU.mult, ALU.mult), reads=["qg", "kg"], writes=["kgq"])
        op("dve", lambda e, bf_=bf_: e.tensor_copy(bhi[0:1, :], bf_), reads=["bf"], writes=["bhi"])
        op("dve", lambda e, bf_=bf_, bl_=bl_: e.tensor_tensor(bl_, bf_, bhi[0:1, :], ALU.subtract), reads=["bf", "bhi"], writes=["bl"])
        op("dve", lambda e, bl_=bl_: e.tensor_copy(blo[0:1, :], bl_), reads=["bl"], writes=["blo"])
        wtf = view(SCR + 12288, F32, [128, 128])
        for gi in range(8):
            op("pe", lambda e, gi=gi, sguw=sguw: e.transpose(psb(7, 128), sguw[:, gi, :], ident_f),
               reads=["sguw", "identf"], writes=[("ps", 7)])
            op("act", lambda e, wtf=wtf: e.copy(wtf, psb(7, 128)), reads=[("ps", 7)], writes=["wtf"])
            op("pool", lambda e, gi=gi, wtf=wtf: e.affine_select(wT[:, gi, :], wtf, [[1, 128]], ALU.is_ge, 0.0,
                                                                  base=0, channel_multiplier=-1),
               reads=["wtf"], writes=["wT"])

        def dst_h(tb, half):
            return hT[:, half * 8:(half + 1) * 8, tb * 128:(tb + 1) * 128], [("hT", tb)] if half == 1 else [("hTa", tb)]
        norm_transpose(src_ap, GB, dst_h, 16, "x")

        if STOP <= 1:
            sch.barrier()
            finish_dbg()
            return
        if dbg and "KBUD" in os.environ:
            sch.budget = int(os.environ["KBUD"])
        memh = view(SCR + 16384, BF16, [128, NC, 256])

        def dst_m(tb, half):
            return memh[:, half * 8:(half + 1) * 8, tb * 128:(tb + 1) * 128], [("memh", tb, half)]
        norm_transpose(g["mem_d"], MGB, dst_m, 2, "m")
        memh_keys = [("memh", tb, half) for tb in range(2) for half in range(2)]
        kf = view(SCR + 24576, F32, [128, 256])
        ksq = view(SCR + 25600, BF16, [128, 256])
        kt_ = view(SCR + 26112, F32, [128, 256])
        rk = view(SCR + 27136, F32, [128, 256])
        for j in range(4):
            slot = wload([(g["wkv_d"][l, :, j * 256:(j + 1) * 256], 0, 256)])
            if j < 2:
                for hh in range(2):
                    h = 2 * j + hh
                    pairs = [(Wr[slot][:, c, hh * 128:(hh + 1) * 128], memh[:, c, :]) for c in range(NC)]

                    def fn(e, pairs=pairs):
                        for i, (l_, r) in enumerate(pairs):
                            ins = e.matmul(psb(4, 256), l_, r, start=(i == 0), stop=(i == NC - 1))
                        return ins
                    op("pe", fn, reads=[("W", slot)] + memh_keys, writes=[("ps", 4)])
                    op("act", lambda e: e.activation(ksq, psb(4, 256), AF.Square), reads=[("ps", 4)], writes=["ksq"])
                    op("dve", lambda e: e.tensor_copy(kf, psb(4, 256)), reads=[("ps", 4), "ksq"], writes=["kf"])
                    op("pe", lambda e: e.matmul(psb(6, 256), ones_bf, ksq, start=True, stop=True),
                       reads=["ksq", "ones"], writes=[("ps", 6)])
                    op("act", lambda e: e.activation(kt_, psb(6, 256), AF.Ln, bias=EPS, scale=1.0 / 128.0),
                       reads=[("ps", 6)], writes=["kt"])
                    op("act", lambda e: e.activation(rk, kt_, AF.Exp, scale=-0.5), reads=["kt"], writes=["rk"])
                    op("dve", lambda e, h=h: e.scalar_tensor_tensor(knT[:, h, :], kf, kgq, rk, ALU.mult, ALU.mult),
                       reads=["kf", "rk", "kgq"], writes=[("knT", h)])
            else:
                jj = j - 2
                for mb in range(2):
                    pairs = [(memh[:, c, mb * 128:(mb + 1) * 128], Wr[slot][:, c, :]) for c in range(NC)]

                    def fn(e, pairs=pairs):
                        for i, (l_, r) in enumerate(pairs):
                            ins = e.matmul(psb(5, 256), l_, r, start=(i == 0), stop=(i == NC - 1))
                        return ins
                    op("pe", fn, reads=[("W", slot)] + memh_keys, writes=[("ps", 5)])
                    op("act", lambda e, mb=mb, jj=jj: e.copy(memv[:, mb, jj * 256:(jj + 1) * 256], psb(5, 256)),
                       reads=[("ps", 5)], writes=[("memv", mb, jj)])
        sch.barrier()

        if STOP <= 2:
            finish_dbg()
            return
        lng = view(SCR + 0, F32, [128, 1024])
        lnb = view(SCR + 4096, F32, [128, 1024])
        dma("sp", lambda e, l=l: e.dma_start(out=lng, in_=g["lng_d"][l:l + 1, :].partition_broadcast(128)),
            reads=[], writes=["lng"], key="c0")
        dma("sp", lambda e, l=l: e.dma_start(out=lnb, in_=g["lnb_d"][l:l + 1, :].partition_broadcast(128)),
            reads=[], writes=["lnb"], key="c1")
        gu_t = [view(SCR + 8192 + i * 2048, F32, [128, 512]) for i in range(2)]
        th_t = [view(SCR + 12288 + i * 2048, F32, [128, 512]) for i in range(2)]
        lnt = [view(SCR + 16384 + i * 4096, F32, [128, 1024]) for i in range(2)]
        it = 0
        for j in range(4):
            slot = wload([(g["w_in_d"][l, :, 1024 + j * 256:1024 + (j + 1) * 256], 0, 256)])
            for tb in range(16):
                bank = 4 + it % 2
                it += 1
                pairs = [(hT[:, c, tb * 128:(tb + 1) * 128], Wr[slot][:, c, :]) for c in range(NC)]

                def fn(e, pairs=pairs, bank=bank):
                    for i, (l_, r) in enumerate(pairs):
                        ins = e.matmul(psb(bank, 256), l_, r, start=(i == 0), stop=(i == NC - 1))
                    return ins
                op("pe", fn, reads=[("W", slot), ("hT", tb), ("hTa", tb)], writes=[("ps", bank)])
                op("act", lambda e, tb=tb, j=j, bank=bank: e.activation(VA[:, tb, j * 256:(j + 1) * 256], psb(bank, 256), AF.Gelu),
                   reads=[("ps", bank)], writes=[("VA", tb, j)])
        def lnA(tb):
            pz = tb % 2
            st_, mv_, t1_, rs_ = ST2[pz], MV2[pz], T12[pz], RS2[pz]
            for k in range(2):
                op("dve", lambda e, tb=tb, k=k, st_=st_: e.bn_stats(st_[:, k, :], VA[:, tb, k * 512:(k + 1) * 512]),
                   reads=[("VA", tb, 2 * k), ("VA", tb, 2 * k + 1)], writes=[("lst", pz, k)])
            op("dve", lambda e, st_=st_, mv_=mv_: e.bn_aggr(mv_, st_[:, 0:2, :].rearrange("p a b -> p (a b)")),
               reads=[("lst", pz, 0), ("lst", pz, 1)], writes=[("lmv", pz)])
            op("dve", lambda e, mv_=mv_, t1_=t1_: e.tensor_scalar(t1_, mv_[:, 1:2], EPS, None, ALU.add),
               reads=[("lmv", pz)], writes=[("lt1", pz)])
            op("pool", lambda e, t1_=t1_, rs_=rs_: e.tensor_tensor(rs_, t1_, neghalf[:, 0:1], ALU.pow),
               reads=[("lt1", pz)], writes=[("lrs", pz)])

        def lnB(tb):
            pz = tb % 2
            mv_, rs_ = MV2[pz], RS2[pz]
            tmp = lnt[tb % 2]
            op("dve", lambda e, tb=tb, tmp=tmp, mv_=mv_: e.scalar_tensor_tensor(tmp, VA[:, tb, :], mv_[:, 0:1], lng, ALU.subtract, ALU.mult),
               reads=[("VA", tb, j) for j in range(4)] + [("lmv", pz), "lng"], writes=[("lnt", tb % 2)])
            op("dve", lambda e, tb=tb, tmp=tmp, rs_=rs_: e.scalar_tensor_tensor(VA[:, tb, :], tmp, rs_, lnb, ALU.mult, ALU.add),
               reads=[("lnt", tb % 2), ("lrs", pz), "lnb"], writes=[("VA", tb, j) for j in range(4)])

        lnA(0)
        it = 0
        for j in range(8):
            slot = wload([(g["w_in_d"][l, :, j * 128:(j + 1) * 128], 0, 128),
                          (g["w_in_d"][l, :, 2048 + j * 128:2048 + (j + 1) * 128], 128, 128)])
            for G in range(4):
                bu = it % 2
                bz = 2 + it % 2
                gu = gu_t[it % 2]
                th = th_t[it % 2]
                if it % 2 == 0:
                    tb_ln = it // 2
                    if tb_ln + 1 < 16:
                        lnA(tb_ln + 1)
                    lnB(tb_ln)
                it += 1
                proj_fm(slot, 0, G, bu)
                proj_fm(slot, 128, G, bz)
                op("act", lambda e, gu=gu, bu=bu: e.activation(gu, psb(bu), AF.Gelu), reads=[("ps", bu)], writes=[("gu", bu)])
                op("act", lambda e, th=th, bz=bz: e.activation(th, psb(bz), AF.Tanh, scale=0.5), reads=[("ps", bz)], writes=[("th", bz)])
                op("dve", lambda e, th=th, bz=bz: e.scalar_tensor_tensor(th, th, 1.0, psb(bz), ALU.add, ALU.mult),
                   reads=[("th", bz), ("ps", bz)], writes=[("th", bz)])
                op("dve", lambda e, th=th, gu=gu, j=j, G=G: e.scalar_tensor_tensor(yT[:, j, G * 512:(G + 1) * 512], th, 0.5, gu, ALU.mult, ALU.mult),
                   reads=[("th", bz), ("gu", bu)], writes=[("yT", j, G)])
        it = 0
        for G in range(4):
            for gi in range(8):
                bank = 6 + it % 2
                it += 1

                def fn(e, G=G, gi=gi, bank=bank):
                    for cc in range(4):
                        c = 4 * G + cc
                        o = psb(bank, 128, cc * 128)
                        e.matmul(o, VA[:, c, gi * 128:(gi + 1) * 128], wT[:, gi, :], start=True, stop=False)
                        e.matmul(o, ones0, bhi[:, gi * 128:(gi + 1) * 128], start=False, stop=False)
                        ins = e.matmul(o, ones0, blo[:, gi * 128:(gi + 1) * 128], start=False, stop=True)
                    return ins
                op("pe", fn, reads=[("VA", 4 * G + cc, jj) for cc in range(4) for jj in range(4)] + ["wT", "bhi", "blo", "ones0"],
                   writes=[("ps", bank)])
                op("dve", lambda e, G=G, gi=gi, bank=bank: e.tensor_tensor(yT[:, gi, G * 512:(G + 1) * 512], psb(bank),
                                                                            yT[:, gi, G * 512:(G + 1) * 512], ALU.mult),
                   reads=[("ps", bank), ("yT", gi, G)], writes=[("yT", gi, G)])
        sch.barrier()

        if STOP <= 3:
            finish_dbg()
            return
        qT = view(SCR + 0, BF16, [128, S])
        kT = view(SCR + 4096, BF16, [128, S])
        vh = view(SCR + 8192, BF16, [128, 16, 128])
        zs = view(SCR + 12288, BF16, [128, S])
        E_t = [view(SCR + 16384 + i * 2048, F32, [128, 512]) for i in range(2)]
        sp_t = [view(SCR + 20480 + i * 1024, BF16, [128, 512]) for i in range(3)]
        R_t = [view(SCR + 23552 + i * 1024, BF16, [128, 512]) for i in range(3)]
        A_t = [view(SCR + 26624 + i * 1024, BF16, [128, 512]) for i in range(3)]
        ez_t = [view(SCR + 29696 + i * 2048, F32, [128, 512]) for i in range(2)]
        assert SCR + 33792 <= 211968
        pj = 0
        for h in range(4):
            c_q = 3072 + h * 128
            slot1 = wload([(g["w_in_d"][l, :, c_q:c_q + 128], 0, 128),
                           (g["w_in_d"][l, :, c_q + 512:c_q + 640], 128, 128)])
            for G in range(4):
                b = pj % 2
                pj += 1
                proj_fm(slot1, 0, G, b)
                op("act", lambda e, G=G, b=b: e.activation(qT[:, G * 512:(G + 1) * 512], psb(b), AF.Copy, scale=SCALE),
                   reads=[("ps", b)], writes=[("qT", G)])
                b = pj % 2
                pj += 1
                proj_fm(slot1, 128, G, b)
                op("dve", lambda e, G=G, b=b: e.tensor_copy(kT[:, G * 512:(G + 1) * 512], psb(b)),
                   reads=[("ps", b)], writes=[("kT", G)])
            slot2 = wload()
            for G in range(4):
                b = pj % 2
                pj += 1
                ez = ez_t[G % 2]
                proj_fm(slot2, 128, G, b)
                op("act", lambda e, ez=ez, b=b: e.activation(ez, psb(b), AF.Exp, scale=-1.0), reads=[("ps", b)], writes=[("ez", G % 2)])
                op("act", lambda e, ez=ez: e.activation(ez, ez, AF.Ln, bias=1.0), reads=[("ez", G % 2)], writes=[("ez", G % 2)])
                op("act", lambda e, ez=ez: e.activation(ez, ez, AF.Exp, scale=-1.0), reads=[("ez", G % 2)], writes=[("ez", G % 2)])
                op("dve", lambda e, ez=ez, G=G, b=b: e.tensor_tensor(zs[:, G * 512:(G + 1) * 512], psb(b), ez, ALU.mult),
                   reads=[("ps", b), ("ez", G % 2)], writes=[("zs", G)])
            for t4 in range(4):
                b = pj % 2
                pj += 1

                def fn(e, t4=t4, b=b, slot2=slot2):
                    for tt in range(4):
                        tb = 4 * t4 + tt
                        for c in range(NC):
                            ins = e.matmul(psb(b, 128, tt * 128), hT[:, c, tb * 128:(tb + 1) * 128], Wr[slot2][:, c, 0:128],
                                           start=(c == 0), stop=(c == NC - 1))
                    return ins
                op("pe", fn, reads=[("W", slot2)] + hkeys(t4), writes=[("ps", b)])
                op("act", lambda e, t4=t4, b=b: e.copy(vh[:, 4 * t4:4 * t4 + 4, :], psb(b).rearrange("p (a b) -> p a b", a=4)),
                   reads=[("ps", b)], writes=[("vh", t4)])
            units = []
            for G in range(4):
                nb = 4 * G + 4
                for k_, bidx in enumerate(range(nb - 1, -1, -1)):
                    i_d = bidx - 4 * G
                    c0 = 128 * i_d if i_d >= 0 else 0
                    units.append(dict(G=G, bidx=bidx, diag=(i_d >= 0), c0=c0, ncol=512 - c0, first=(k_ == 0), last=(bidx == 0)))
            nU = len(units)

            def S1(u):
                U = units[u]
                G, bidx, diag, c0, ncol = U["G"], U["bidx"], U["diag"], U["c0"], U["ncol"]
                zb = 2 + u % 4
                E = E_t[u % 2]
                sp = sp_t[u % 3]
                kblk = kT[:, bidx * 128:(bidx + 1) * 128]
                qcols = qT[:, G * 512 + c0:G * 512 + 512]

                def fz(e):
                    ins = e.matmul(psb(zb, ncol), kblk, qcols, start=True, stop=not diag)
                    if diag:
                        ins = e.matmul(psb(zb, 128), ident_bf, negmask, start=False, stop=True)
                    return ins
                op("pe", fz, reads=[("kT", bidx // 4), ("qT", G), "ident", "negmask"], writes=[("ps", zb)])
                op("act", lambda e: e.activation(E[:, 0:ncol], psb(zb, ncol), AF.Exp), reads=[("ps", zb)], writes=[("E", u % 2)])

            def S1b(u):
                ncol = units[u]["ncol"]
                E = E_t[u % 2]
                sp = sp_t[u % 3]
                op("act", lambda e: e.activation(sp[:, 0:ncol], E[:, 0:ncol], AF.Ln, bias=1.0),
                   reads=[("E", u % 2)], writes=[("sp", u % 3)])

            def S2(u):
                U = units[u]
                G, bidx, diag, c0, ncol, first = U["G"], U["bidx"], U["diag"], U["c0"], U["ncol"], U["first"]
                zb = 2 + u % 4
                sp = sp_t[u % 3]
                A = A_t[u % 3]
                Rcur = R_t[u % 3]
                Rnext = R_t[(u + 1) % 3]

                kblk2 = kT[:, bidx * 128:(bidx + 1) * 128]
                qcols2 = qT[:, G * 512 + c0:G * 512 + 512]

                def fp(e):
                    if os.environ.get("KZ"):
                        e.matmul(psb(zb, ncol), kblk2, qcols2, start=True, stop=not diag)
                        if diag:
                            e.matmul(psb(zb, 128), ident_bf, negmask, start=False, stop=True)
                    ins = e.matmul(psb(zb, ncol), negU, sp[:, 0:ncol], start=False, stop=True, skip_group_check=True)
                    if not first:
                        if diag:
                            ins = e.matmul(psb(zb, ncol - 128, 128), negOnes, Rcur[:, c0 + 128:512], start=False, stop=True,
                                           skip_group_check=True)
                        else:
                            ins = e.matmul(psb(zb, 512), negOnes, Rcur, start=False, stop=True, skip_group_check=True)
                    return ins
                op("pe", fp, reads=[("sp", u % 3), ("R", u % 3), "negU", "negones"], writes=[("ps", zb)])
                if not U["last"]:
                    if first:
                        op("pool", lambda e: e.tensor_copy(Rnext[:, c0:512], sp[:, 0:ncol]),
                           reads=[("sp", u % 3)], writes=[("R", (u + 1) % 3)])
                    elif diag:
                        op("pool", lambda e: e.tensor_copy(Rnext[:, c0:c0 + 128], sp[:, 0:128]),
                           reads=[("sp", u % 3)], writes=[("R", (u + 1) % 3)])
                        op("pool", lambda e: e.tensor_tensor(Rnext[:, c0 + 128:512], Rcur[:, c0 + 128:512], sp[:, 128:ncol], ALU.add),
                           reads=[("sp", u % 3), ("R", u % 3)], writes=[("R", (u + 1) % 3)])
                    else:
                        op("pool", lambda e: e.tensor_tensor(Rnext, Rcur, sp, ALU.add),
                           reads=[("sp", u % 3), ("R", u % 3)], writes=[("R", (u + 1) % 3)])

            def S2b(u):
                ncol = units[u]["ncol"]
                zb = 2 + u % 4
                A = A_t[u % 3]
                op("act", lambda e: e.activation(A[:, 0:ncol], psb(zb, ncol), AF.Exp), reads=[("ps", zb)], writes=[("A", u % 3)])

            def S3(u, h=h):
                U = units[u]
                G, bidx, diag, c0, ncol, first, last = U["G"], U["bidx"], U["diag"], U["c0"], U["ncol"], U["first"], U["last"]
                yb = 6 + G % 2
                A = A_t[u % 3]
                vblk = vh[:, bidx, :]

                def fy(e):
                    if first:
                        ins = e.matmul(psb(yb, ncol, c0), vblk, A[:, 0:ncol], start=True, stop=True)
                    elif diag:
                        e.matmul(psb(yb, 128, c0), vblk, A[:, 0:128], start=False, stop=True, skip_group_check=True)
                        ins = e.matmul(psb(yb, ncol - 128, c0 + 128), vblk, A[:, 128:ncol], start=False, stop=True,
                                       skip_group_check=True)
                    else:
                        ins = e.matmul(psb(yb, 512), vblk, A, start=False, stop=True, skip_group_check=True)
                    return ins
                op("pe", fy, reads=[("vh", bidx // 4), ("A", u % 3)], writes=[("ps", yb)])
                if last:
                    op("dve", lambda e: e.tensor_tensor(yT[:, 8 + h, G * 512:(G + 1) * 512], psb(yb),
                                                        zs[:, G * 512:(G + 1) * 512], ALU.mult),
                       reads=[("ps", yb), ("zs", G)], writes=[("yT", 8 + h, G)])

            for i in range(nU + 3):
                if i < nU:
                    S1(i)
                if 0 <= i - 1 < nU:
                    S2(i - 1)
                if 0 <= i - 2 < nU:
                    S2b(i - 2)
                if i < nU:
                    S1b(i)
                if 0 <= i - 3 < nU:
                    S3(i - 3)
        sch.barrier()

        if STOP <= 4:
            finish_dbg()
            return
        def cbuf(p_):
            o = SCR + p_ * 15360
            return dict(sq=view(o, BF16, [128, 512]), tq=view(o + 1024, F32, [128, 512]), rq=view(o + 3072, F32, [128, 512]),
                        qh=view(o + 5120, BF16, [128, 512]), ez=view(o + 6144, F32, [128, 512]), zsc=view(o + 8192, BF16, [128, 512]),
                        PT=[view(o + 9216 + i * 1024, BF16, [128, 512]) for i in range(2)],
                        rd=view(o + 11264, F32, [128, 512]), t2=view(o + 13312, F32, [128, 512]))
        CB = [cbuf(0), cbuf(1)]
        assert SCR + 2 * 15360 <= 211968
        cits = [(h, G) for h in range(4) for G in range(4)]
        cslot = {}

        def c_projq(it):
            h, G = cits[it]
            if G == 0:
                cslot[h] = wload()
            proj_fm(cslot[h], 0, G, it % 2)

        def c_projz(it):
            h, G = cits[it]
            proj_fm(cslot[h], 128, G, 2 + it % 2)

        c_projq(0)
        c_projz(0)
        for it, (h, G) in enumerate(cits):
            p_ = it % 2
            cb = CB[p_]
            qb = it % 2
            zb = 2 + it % 2
            sq, tq, rq, qh, ez, zsc, PT, rd, t2 = (cb["sq"], cb["tq"], cb["rq"], cb["qh"], cb["ez"], cb["zsc"], cb["PT"], cb["rd"], cb["t2"])
            K = lambda n_: (n_, p_)
            op("act", lambda e, sq=sq, qb=qb: e.activation(sq, psb(qb), AF.Square), reads=[("ps", qb)], writes=[K("sq")])
            op("pe", lambda e, sq=sq: e.matmul(psb(6), ones_bf, sq, start=True, stop=True), reads=[K("sq"), "ones"], writes=[("ps", 6)])
            if it + 1 < len(cits):
                c_projq(it + 1)
            op("act", lambda e, tq=tq: e.activation(tq, psb(6), AF.Ln, bias=EPS, scale=1.0 / 128.0), reads=[("ps", 6)], writes=[K("tq")])
            op("act", lambda e, tq=tq, rq=rq: e.activation(rq, tq, AF.Exp, scale=-0.5), reads=[K("tq")], writes=[K("rq")])
            op("dve", lambda e, qh=qh, rq=rq, qb=qb: e.tensor_tensor(qh, psb(qb), rq, ALU.mult),
               reads=[("ps", qb), K("sq"), K("rq")], writes=[K("qh")])
            op("act", lambda e, ez=ez, zb=zb: e.activation(ez, psb(zb), AF.Exp, scale=-1.0), reads=[("ps", zb)], writes=[K("ez")])
            op("act", lambda e, ez=ez: e.activation(ez, ez, AF.Ln, bias=1.0), reads=[K("ez")], writes=[K("ez")])
            op("act", lambda e, ez=ez: e.activation(ez, ez, AF.Exp, scale=-1.0), reads=[K("ez")], writes=[K("ez")])
            op("dve", lambda e, zsc=zsc, ez=ez, zb=zb: e.tensor_tensor(zsc, psb(zb), ez, ALU.mult), reads=[("ps", zb), K("ez")], writes=[K("zsc")])
            for mb in range(2):
                op("pe", lambda e, mb=mb, h=h, qh=qh: e.matmul(psb(4 + mb), knT[:, h, mb * 128:(mb + 1) * 128], qh, start=True, stop=True),
                   reads=[("knT", h), K("qh")], writes=[("ps", 4 + mb)])
            if it + 1 < len(cits):
                c_projz(it + 1)
            for mb in range(2):
                op("act", lambda e, mb=mb, PT=PT: e.activation(PT[mb], psb(4 + mb), AF.Exp), reads=[("ps", 4 + mb)], writes=[K("PT%d" % mb)])

            def fo(e, h=h, PT=PT):
                e.matmul(psb(7), memv[:, 0, h * 128:(h + 1) * 128], PT[0], start=True, stop=False)
                e.matmul(psb(7), memv[:, 1, h * 128:(h + 1) * 128], PT[1], start=False, stop=True)
                e.matmul(psb(6), ones_bf, PT[0], start=True, stop=False)
                return e.matmul(psb(6), ones_bf, PT[1], start=False, stop=True)
            op("pe", fo, reads=[K("PT0"), K("PT1"), "ones"] + [("memv", mb, jj) for mb in range(2) for jj in range(2)],
               writes=[("ps", 7), ("ps", 6)])
            op("act", lambda e, rd=rd: e.activation(rd, psb(6), AF.Ln), reads=[("ps", 6)], writes=[K("rd")])
            op("act", lambda e, rd=rd: e.activation(rd, rd, AF.Exp, scale=-1.0), reads=[K("rd")], writes=[K("rd")])
            op("dve", lambda e, t2=t2, rd=rd: e.tensor_tensor(t2, psb(7), rd, ALU.mult), reads=[("ps", 7), K("rd")], writes=[K("t2")])
            op("dve", lambda e, h=h, G=G, t2=t2, zsc=zsc: e.tensor_tensor(yT[:, 12 + h, G * 512:(G + 1) * 512], t2, zsc, ALU.mult),
               reads=[K("t2"), K("zsc")], writes=[("yT", 12 + h, G)])
        sch.barrier()

        if dbg and l == int(os.environ.get("KL", "0")):
            finish_dbg()
        if STOP <= 5:
            return

        NXR = 6
        xr_t = [view(HT_OFF + i * 1024, F32, [128, 256]) for i in range(NXR)]
        os_t = [view(HT_OFF + 8192 + i * 1024, F32, [128, 256]) for i in range(4)]
        p3 = [(n, tb) for n in range(8) for tb in range(16)]

        def p3_load(k, src_ap=src_ap):
            n, tb = p3[k]
            xr = xr_t[k % NXR]
            dma("sp", lambda e: e.dma_start(out=xr, in_=src_ap[tb * 128:(tb + 1) * 128, n * 256:(n + 1) * 256]),
                reads=[], writes=[("xr", k % NXR)], key=("xr", k % NXR))

        for k in range(3):
            p3_load(k)
        slot = None
        for k, (n, tb) in enumerate(p3):
            if tb == 0:
                slot = wload()
            if k + 3 < len(p3):
                p3_load(k + 3)
            bank = k % 8
            xr = xr_t[k % NXR]
            ost = os_t[k % 4]
            pairs = [(yT[:, c, tb * 128:(tb + 1) * 128], Wr[slot][:, c, :]) for c in range(NC)]

            def fn(e, pairs=pairs, bank=bank):
                for i, (l_, r) in enumerate(pairs):
                    ins = e.matmul(psb(bank, 256), l_, r, start=(i == 0), stop=(i == NC - 1))
                return ins
            op("pe", fn, reads=[("W", slot)], writes=[("ps", bank)])
            op("dve", lambda e, ost=ost, xr=xr, bank=bank: e.tensor_tensor(ost, psb(bank, 256), xr, ALU.add),
               reads=[("ps", bank), ("xr", k % NXR)], writes=[("os", k % 4)])
            dma("sp", lambda e, ost=ost, tb=tb, n=n, dst_ap=dst_ap: e.dma_start(out=dst_ap[tb * 128:(tb + 1) * 128, n * 256:(n + 1) * 256], in_=ost),
                reads=[("os", k % 4)], writes=[("xdst", tb, n)], key=("os", k % 4))
        sch.barrier()


_NC_CACHE = {}


def kernel(x, mem, norm_g, w_in, sgu_ln_g, sgu_ln_b, sgu_w, sgu_b, mem_norm_g, w_mem_kv, q_norm_g, k_norm_g, w_out):
    if "nc" not in _NC_CACHE:
        _NC_CACHE["nc"] = build_nc()
    nc = _NC_CACHE["nc"]
    f = lambda a: np.ascontiguousarray(np.asarray(a, dtype=np.float32))
    shared = {"norm_g": f(norm_g), "w_in": f(w_in), "sgu_ln_g": f(sgu_ln_g), "sgu_ln_b": f(sgu_ln_b),
              "sgu_w": f(sgu_w), "sgu_b": f(sgu_b), "mem_norm_g": f(mem_norm_g), "w_mem_kv": f(w_mem_kv),
              "q_norm_g": f(q_norm_g), "k_norm_g": f(k_norm_g), "w_out": f(w_out)}
    x = f(x)
    mem = f(mem)
    in_maps = [dict(shared, x=x[b], mem=mem[b]) for b in range(8)]
    res = run_bass_kernel_spmd(nc, in_maps, core_ids=list(range(8)))
    return np.stack([np.asarray(r["out"], dtype=np.float32) for r in res.results], axis=0)
```

```python
import math
import numpy as np
import concourse.bass as bass
import concourse.mybir as mybir
from concourse.bass_utils import run_bass_kernel_spmd

F32 = mybir.dt.float32
BF16 = mybir.dt.bfloat16
U8 = mybir.dt.uint8
AF = mybir.ActivationFunctionType
ALU = mybir.AluOpType

S = 2048
D = 2048
MEM = 256
NL = 2
INW = 6144
EPS = 1e-6
NEG = -30000.0
NC = 16
SCALE = 1.0 / math.sqrt(128.0)

ENGS = ("pe", "act", "dve", "pool", "sp")


class Sched:
    def __init__(self, nc, esems, dma_sems):
        self.nc = nc
        self.prog = {e: [] for e in ENGS}
        self.cnt = {e: 0 for e in ENGS}
        self.esem = esems
        self.free_dsem = list(dma_sems)
        self.dsem = {}
        self.seen = {e: {} for e in ENGS}
        self.lastw = {}
        self.readers = {}

    def _waits(self, eng, reads, writes):
        toks = []
        for r in reads:
            t = self.lastw.get(r)
            if t is not None:
                toks.append(("raw", t))
        for w in writes:
            t = self.lastw.get(w)
            if t is not None:
                toks.append(("waw", t))
            for t in self.readers.get(w, ()):
                toks.append(("war", t))
        waits = []
        seen = self.seen[eng]
        for kind, (key, val, peng) in toks:
            if peng == eng and eng == "pe":
                continue
            if seen.get(key, 0) >= val:
                continue
            seen[key] = val
            waits.append((key, val))
        return waits

    def _commit(self, tok, reads, writes):
        for r in reads:
            self.readers.setdefault(r, []).append(tok)
        for w in writes:
            self.lastw[w] = tok
            self.readers[w] = []

    budget = None

    def _drop(self):
        if self.budget is None:
            return False
        if self.budget <= 0:
            return True
        self.budget -= 1
        return False

    def op(self, eng, fn, reads=(), writes=()):
        if self._drop():
            return
        waits = self._waits(eng, reads, writes)
        self.cnt[eng] += 1
        tok = (eng, self.cnt[eng], eng)
        self.prog[eng].append((waits, fn, eng, 1))
        self._commit(tok, reads, writes)

    def dma(self, queue, fn, reads, writes, key, force=False):
        if not force and self._drop():
            return (("d", key), 0, "dma")
        waits = self._waits(queue, reads, writes)
        if key not in self.dsem:
            self.dsem[key] = [self.free_dsem.pop(), 0]
        self.dsem[key][1] += 16
        tok = (("d", key), self.dsem[key][1], "dma")
        self.prog[queue].append((waits, fn, ("d", key), 16))
        self._commit(tok, reads, writes)
        return tok

    def semh(self, key):
        if isinstance(key, tuple) and key[0] == "d":
            return self.dsem[key[1]][0]
        return self.esem[key]

    def barrier(self):
        for e in ENGS:
            waits = []
            for o in ENGS:
                if o != e and self.cnt[o] > 0 and self.seen[e].get(o, 0) < self.cnt[o]:
                    self.seen[e][o] = self.cnt[o]
                    waits.append((o, self.cnt[o]))
            for k, (h, c) in self.dsem.items():
                kk = ("d", k)
                if c > 0 and self.seen[e].get(kk, 0) < c:
                    self.seen[e][kk] = c
                    waits.append((kk, c))
            if waits:
                self.prog[e].append((waits, None, None, 0))
        self.lastw = {}
        self.readers = {}

    def replay(self, eng, e):
        for waits, fn, skey, inc in self.prog[eng]:
            for key, val in waits:
                e.wait_ge(self.semh(key), val)
            if fn is None:
                continue
            ins = fn(e)
            ins.then_inc(self.semh(skey), inc)


def build_nc(n_layers=NL, dbg=False):
    nc = bass.Bass("TRN2", target_bir_lowering=False)
    x_d = nc.dram_tensor("x", [S, D], F32, kind="ExternalInput").ap()
    mem_d = nc.dram_tensor("mem", [MEM, D], F32, kind="ExternalInput").ap()
    norm_g_d = nc.dram_tensor("norm_g", [NL, D], F32, kind="ExternalInput").ap()
    w_in_d = nc.dram_tensor("w_in", [NL, D, INW], F32, kind="ExternalInput").ap()
    lng_d = nc.dram_tensor("sgu_ln_g", [NL, 1024], F32, kind="ExternalInput").ap()
    lnb_d = nc.dram_tensor("sgu_ln_b", [NL, 1024], F32, kind="ExternalInput").ap()
    sguw_d = nc.dram_tensor("sgu_w", [NL, 8, 128, 128], F32, kind="ExternalInput").ap()
    sgub_d = nc.dram_tensor("sgu_b", [NL, 8, 128], F32, kind="ExternalInput").ap()
    mng_d = nc.dram_tensor("mem_norm_g", [NL, D], F32, kind="ExternalInput").ap()
    wkv_d = nc.dram_tensor("w_mem_kv", [NL, D, 1024], F32, kind="ExternalInput").ap()
    qg_d = nc.dram_tensor("q_norm_g", [NL, 128], F32, kind="ExternalInput").ap()
    kg_d = nc.dram_tensor("k_norm_g", [NL, 128], F32, kind="ExternalInput").ap()
    wout_d = nc.dram_tensor("w_out", [NL, D, D], F32, kind="ExternalInput").ap()
    out_d = nc.dram_tensor("out", [S, D], F32, kind="ExternalOutput").ap()
    x1_d = nc.dram_tensor("x1_scratch", [S, D], F32).ap()
    if dbg:
        dbg_h = nc.dram_tensor("dbg_h", [128, NC * S], BF16, kind="ExternalOutput").ap()
        dbg_y = nc.dram_tensor("dbg_y", [128, NC * S], BF16, kind="ExternalOutput").ap()

    ARENA = 211968
    arena = nc.alloc_sbuf_tensor("arena", [128, ARENA], U8)
    ps = [nc.alloc_psum_tensor("ps%d" % i, [128, 512], F32) for i in range(8)]

    def view(off, dt, shape):
        esz = 4 if dt == F32 else 2
        n = 1
        for s_ in shape[1:]:
            n *= s_
        ap = arena[0:shape[0], off:off + n * esz].bitcast(dt)
        if len(shape) == 3:
            ap = ap.rearrange("p (a b) -> p a b", a=shape[1])
        return ap

    HT_OFF = 0
    YT_OFF = 65536
    W_OFF = 131072
    C_OFF = W_OFF + 3 * 8192
    SCR = C_OFF + 14592
    assert SCR + 32768 <= ARENA

    hT = view(HT_OFF, BF16, [128, NC, S])
    yT = view(YT_OFF, BF16, [128, NC, S])
    VA = view(YT_OFF + 32768, BF16, [128, 16, 1024])
    Wr = [view(W_OFF + i * 8192, BF16, [128, NC, 256]) for i in range(3)]
    ident_bf = view(C_OFF + 0, BF16, [128, 128])
    negU = view(C_OFF + 256, BF16, [128, 128])
    negOnes = view(C_OFF + 512, BF16, [128, 128])
    ones_bf = view(C_OFF + 768, BF16, [128, 128])
    negmask = view(C_OFF + 1024, BF16, [128, 128])
    ident_f = view(C_OFF + 1280, F32, [128, 128])
    neghalf = view(C_OFF + 1792, F32, [128, 512])
    kgq = view(C_OFF + 3840, F32, [128, 1])
    qg = view(C_OFF + 3844, F32, [128, 1])
    kg = view(C_OFF + 3848, F32, [128, 1])
    st = view(C_OFF + 3872, F32, [128, 4, 6])
    mv = view(C_OFF + 3968, F32, [128, 2])
    t1 = view(C_OFF + 3976, F32, [128, 1])
    rstd = view(C_OFF + 3980, F32, [128, 1])
    ST2 = [st, view(C_OFF + 3984, F32, [128, 4, 6])]
    MV2 = [mv, view(C_OFF + 4080, F32, [128, 2])]
    T12 = [t1, view(C_OFF + 4088, F32, [128, 1])]
    RS2 = [rstd, view(C_OFF + 4092, F32, [128, 1])]
    knT = view(C_OFF + 4096, BF16, [128, 4, 256])
    memv = view(C_OFF + 6144, BF16, [128, 2, 512])
    wT = view(C_OFF + 8192, BF16, [128, 8, 128])
    bhi = view(C_OFF + 10240, BF16, [128, 1024])
    blo = view(C_OFF + 12288, BF16, [128, 1024])
    ones0 = view(C_OFF + 14336, BF16, [128, 128])
    XT = [view(YT_OFF + i * 8192, F32, [128, D]) for i in range(3)]
    XN = [view(YT_OFF + 24576 + i * 4096, BF16, [128, D]) for i in range(2)]
    GB = view(YT_OFF + 32768, F32, [128, D])
    MGB = view(YT_OFF + 40960, F32, [128, D])
    XS = [view(YT_OFF + 49152 + i * 8192, F32, [128, D]) for i in range(2)]

    with nc.semaphore("s_pe") as s_pe, nc.semaphore("s_act") as s_act, \
            nc.semaphore("s_dve") as s_dve, nc.semaphore("s_pool") as s_pool, \
            nc.semaphore("s_sp") as s_sp:
        import contextlib
        with contextlib.ExitStack() as es:
            dma_sems = [es.enter_context(nc.semaphore("dsem%d" % i)) for i in range(24)]
            sch = Sched(nc, {"pe": s_pe, "act": s_act, "dve": s_dve, "pool": s_pool, "sp": s_sp}, dma_sems)
            _emit_program(nc, sch, locals(), n_layers, dbg)
            with nc.Block() as block:
                @block.tensor
                def _(e):
                    sch.replay("pe", e)

                @block.scalar
                def _(e):
                    sch.replay("act", e)

                @block.vector
                def _(e):
                    sch.replay("dve", e)

                @block.gpsimd
                def _(e):
                    sch.replay("pool", e)

                @block.sync
                def _(e):
                    sch.replay("sp", e)
    return nc


def _emit_program(nc, sch, V, n_layers, dbg):
    import os
    STOP = int(os.environ.get("KSTOP", "99")) if dbg else 99
    g = V
    ps = g["ps"]
    hT, yT, VA, Wr = g["hT"], g["yT"], g["VA"], g["Wr"]
    view = g["view"]
    SCR = g["SCR"]
    HT_OFF = g["HT_OFF"]
    ident_bf, negU, negOnes, ones_bf, negmask, ident_f, neghalf = (
        g["ident_bf"], g["negU"], g["negOnes"], g["ones_bf"], g["negmask"], g["ident_f"], g["neghalf"])
    kgq, qg, kg, st, mv, t1, rstd = g["kgq"], g["qg"], g["kg"], g["st"], g["mv"], g["t1"], g["rstd"]
    knT, memv, wT, bhi, blo, ones0 = g["knT"], g["memv"], g["wT"], g["bhi"], g["blo"], g["ones0"]
    XT, XN, GB, MGB = g["XT"], g["XN"], g["GB"], g["MGB"]
    ST2, MV2, T12, RS2 = g["ST2"], g["MV2"], g["T12"], g["RS2"]
    XS = g["XS"]
    op, dma = sch.op, sch.dma

    def psb(i, n=512, off=0):
        return ps[i][:, off:off + n]

    def ps_bf(i):
        return ps[i][:, :].bitcast(BF16)

    op("pool", lambda e: e.memset(ones_bf, 1.0), writes=["ones"])
    op("pool", lambda e: e.memset(negOnes, -1.0), writes=["negones"])
    op("pool", lambda e: e.memset(neghalf, -0.5), writes=["neghalf"])
    op("pool", lambda e: e.memset(negmask, NEG), writes=["negmask"])
    op("pool", lambda e: e.memset(ident_f, 1.0), writes=["identf"])
    op("pool", lambda e: e.affine_select(ident_bf, ones_bf, [[-1, 128]], ALU.is_equal, 0.0, base=0, channel_multiplier=1),
       reads=["ones"], writes=["ident"])
    op("pool", lambda e: e.affine_select(ident_f, ident_f, [[-1, 128]], ALU.is_equal, 0.0, base=0, channel_multiplier=1),
       reads=["identf"], writes=["identf"])
    op("pool", lambda e: e.affine_select(negU, negOnes, [[-1, 128]], ALU.is_ge, 0.0, base=0, channel_multiplier=1),
       reads=["negones"], writes=["negU"])
    op("pool", lambda e: e.affine_select(negmask, negmask, [[-1, 128]], ALU.is_ge, 0.0, base=0, channel_multiplier=1),
       reads=["negmask"], writes=["negmask"])
    op("pool", lambda e: e.memset(bhi, 0.0), writes=["bhi"])
    op("pool", lambda e: e.memset(blo, 0.0), writes=["blo"])
    op("pool", lambda e: e.affine_select(ones0, ones_bf, [[0, 128]], ALU.is_equal, 0.0, base=0, channel_multiplier=1),
       reads=["ones"], writes=["ones0"])
    sch.barrier()

    wlist = []
    for l_ in range(n_layers):
        for j in range(4):
            wlist.append([(g["wkv_d"][l_, :, j * 256:(j + 1) * 256], 0, 256)])
        for j in range(4):
            wlist.append([(g["w_in_d"][l_, :, 1024 + j * 256:1024 + (j + 1) * 256], 0, 256)])
        for j in range(8):
            wlist.append([(g["w_in_d"][l_, :, j * 128:(j + 1) * 128], 0, 128),
                          (g["w_in_d"][l_, :, 2048 + j * 128:2048 + (j + 1) * 128], 128, 128)])
        for h_ in range(4):
            cq = 3072 + h_ * 128
            wlist.append([(g["w_in_d"][l_, :, cq:cq + 128], 0, 128), (g["w_in_d"][l_, :, cq + 512:cq + 640], 128, 128)])
            wlist.append([(g["w_in_d"][l_, :, cq + 1024:cq + 1152], 0, 128), (g["w_in_d"][l_, :, cq + 1536:cq + 1664], 128, 128)])
        for h_ in range(4):
            cq = 5120 + h_ * 128
            wlist.append([(g["w_in_d"][l_, :, cq:cq + 128], 0, 128), (g["w_in_d"][l_, :, cq + 512:cq + 640], 128, 128)])
        for n_ in range(8):
            wlist.append([(g["wout_d"][l_, :, n_ * 256:(n_ + 1) * 256], 0, 256)])
    wstate = {"issued": 0, "used": 0}

    def wissue_upto(k):
        while wstate["issued"] < min(k, len(wlist)):
            idx = wstate["issued"]
            wstate["issued"] += 1
            i = idx % 3
            for kk, (src, c0, ncols) in enumerate(wlist[idx]):
                dst = Wr[i][:, :, c0:c0 + ncols]
                s_ = src.rearrange("(c p) f -> p c f", p=128)
                tok = dma("pool", (lambda e, dst=dst, s_=s_: e.dma_start(out=dst, in_=s_)),
                          reads=[], writes=([("W", i)] if kk == 0 else []), key=("W", i))
            sch.lastw[("W", i)] = tok

    def wload(parts=None):
        k = wstate["used"]
        wstate["used"] += 1
        wissue_upto(k + 3)
        return k % 3

    def hkeys(G):
        return [("hT", tb) for tb in range(4 * G, 4 * G + 4)] + [("hTa", tb) for tb in range(4 * G, 4 * G + 4)]

    def proj_fm(slot, c0, G, bank):
        pairs = [(Wr[slot][:, c, c0:c0 + 128], hT[:, c, G * 512:(G + 1) * 512]) for c in range(NC)]

        def fn(e, pairs=pairs, bank=bank):
            for i, (l, r) in enumerate(pairs):
                ins = e.matmul(psb(bank), l, r, start=(i == 0), stop=(i == NC - 1))
            return ins
        op("pe", fn, reads=[("W", slot)] + hkeys(G), writes=[("ps", bank)])

    def norm_transpose(src_ap, gbt, dst_fn, nblk, tag):
        def stA(tb):
            xs = tb % 3
            xt = XT[xs]
            dma("sp", (lambda e, xt=xt, tb=tb: e.dma_start(out=xt, in_=src_ap[tb * 128:(tb + 1) * 128, :])),
                reads=[(tag + "src", tb)], writes=[("xt", xs)], key=("xt", xs))

        def stR(tb):
            xs = tb % 3
            pz = tb % 2
            xt = XT[xs]
            t1_, rs_ = T12[pz], RS2[pz]
            junk = XS[0].bitcast(BF16)[:, 0:D]
            op("act", lambda e, xt=xt, t1_=t1_, junk=junk: e.activation(junk, xt, AF.Square, accum_out=t1_),
               reads=[("xt", xs)], writes=[("t1", pz), "junk"])
            op("act", lambda e, t1_=t1_: e.activation(t1_, t1_, AF.Ln, bias=EPS, scale=1.0 / D), reads=[("t1", pz)], writes=[("t1", pz)])
            op("act", lambda e, t1_=t1_, rs_=rs_: e.activation(rs_, t1_, AF.Exp, scale=-0.5), reads=[("t1", pz)], writes=[("rstd", pz)])

        def stB1(tb):
            ns = tb % 2
            xn = XN[ns]
            xt = XT[tb % 3]
            rs_ = RS2[ns]
            op("dve", (lambda e, xn=xn, xt=xt, rs_=rs_: e.scalar_tensor_tensor(xn, xt, rs_, gbt, ALU.mult, ALU.mult)),
               reads=[("xt", tb % 3), ("rstd", ns), tag + "gb"], writes=[("xn", ns)])
            b0 = 2 * (tb % 2)
            for half in range(2):
                bank = b0 + half

                def fn(e, xn=xn, bank=bank, half=half):
                    pb = ps_bf(bank)
                    for cc in range(8):
                        c = half * 8 + cc
                        ins = e.transpose(pb[:, cc * 128:(cc + 1) * 128], xn[:, c * 128:(c + 1) * 128], ident_bf)
                    return ins
                op("pe", fn, reads=[("xn", ns), "ident"], writes=[("ps", bank)])

        def stB2(tb):
            b0 = 2 * (tb % 2)
            for half in range(2):
                bank = b0 + half
                dst, keys = dst_fn(tb, half)
                src3 = ps_bf(bank).rearrange("p (a b) -> p a b", a=8)
                if half == 0:
                    op("act", (lambda e, dst=dst, src3=src3: e.copy(dst, src3)), reads=[("ps", bank)], writes=keys)
                else:
                    op("dve", (lambda e, dst=dst, src3=src3: e.tensor_copy(dst, src3)), reads=[("ps", bank)], writes=keys)

        for tb in range(min(2, nblk)):
            stA(tb)
            stR(tb)
        for tb in range(nblk + 1):
            if tb < nblk:
                stB1(tb)
            if 0 <= tb - 1 < nblk:
                stB2(tb - 1)
            if tb + 2 < nblk:
                stA(tb + 2)
                stR(tb + 2)

    def finish_dbg():
        sch.budget = None
        for c in range(NC):
            dma("sp", lambda e, c=c: e.dma_start(out=g["dbg_h"][:, c * S:(c + 1) * S], in_=hT[:, c, :]), reads=[], writes=["dbgh"], key="o0")
            dma("sp", lambda e, c=c: e.dma_start(out=g["dbg_y"][:, c * S:(c + 1) * S], in_=yT[:, c, :]), reads=[], writes=["dbgy"], key="o1")
        sch.barrier()

    if STOP <= 0:
        finish_dbg()
        return
    for l in range(n_layers):
        src_ap = g["x_d"] if l == 0 else g["out_d"]
        dst_ap = g["out_d"]
        dma("sp", lambda e, l=l: e.dma_start(out=GB, in_=g["norm_g_d"][l:l + 1, :].partition_broadcast(128)),
            reads=[], writes=["xgb"], key="c0")
        dma("sp", lambda e, l=l: e.dma_start(out=MGB, in_=g["mng_d"][l:l + 1, :].partition_broadcast(128)),
            reads=[], writes=["mgb"], key="c1")
        dma("sp", lambda e, l=l: e.dma_start(out=qg, in_=g["qg_d"][l:l + 1, :].rearrange("a p -> p a")),
            reads=[], writes=["qg"], key="c2")
        dma("sp", lambda e, l=l: e.dma_start(out=kg, in_=g["kg_d"][l:l + 1, :].rearrange("a p -> p a")),
            reads=[], writes=["kg"], key="c3")
        sguw = view(SCR, F32, [128, 8, 128])
        dma("sp", lambda e, l=l, sguw=sguw: e.dma_start(out=sguw, in_=g["sguw_d"][l].rearrange("g t s -> t g s")),
            reads=[], writes=["sguw"], key="c4")
        bf_ = view(SCR + 4096, F32, [1, 1024])
        bl_ = view(SCR + 8192, F32, [1, 1024])
        dma("sp", lambda e, l=l, bf_=bf_: e.dma_start(out=bf_, in_=g["sgub_d"][l:l + 1].rearrange("a g t -> a (g t)")),
            reads=[], writes=["bf"], key="c5")
        op("dve", lambda e: e.scalar_tensor_tensor(kgq, qg, SCALE, kg, ALU.mult, ALU.mult), reads=["qg", "kg"], writes=["kgq"])
        op("dve", lambda e, bf_=bf_: e.tensor_copy(bhi[0:1, :], bf_), reads=["bf"], writes=["bhi"])
        op("dve", lambda e, bf_=bf_, bl_=bl_: e.tensor_tensor(bl_, bf_, bhi[0:1, :], ALU.subtract), reads=["bf", "bhi"], writes=["bl"])
        op("dve", lambda e, bl_=bl_: e.tensor_copy(blo[0:1, :], bl_), reads=["bl"], writes=["blo"])
        wtf = view(SCR + 12288, F32, [128, 128])
        for gi in range(8):
            op("pe", lambda e, gi=gi, sguw=sguw: e.transpose(psb(7, 128), sguw[:, gi, :], ident_f),
               reads=["sguw", "identf"], writes=[("ps", 7)])
            op("act", lambda e, wtf=wtf: e.copy(wtf, psb(7, 128)), reads=[("ps", 7)], writes=["wtf"])
            op("pool", lambda e, gi=gi, wtf=wtf: e.affine_select(wT[:, gi, :], wtf, [[1, 128]], ALU.is_ge, 0.0,
                                                                  base=0, channel_multiplier=-1),
               reads=["wtf"], writes=["wT"])

        memh = view(SCR + 16384, BF16, [128, NC, 256])

        def dst_m(tb, half):
            return memh[:, half * 8:(half + 1) * 8, tb * 128:(tb + 1) * 128], [("memh", tb, half)]
        norm_transpose(g["mem_d"], MGB, dst_m, 2, "m")
        memh_keys = [("memh", tb, half) for tb in range(2) for half in range(2)]
        kf = view(SCR + 24576, F32, [128, 256])
        ksq = view(SCR + 25600, BF16, [128, 256])
        kt_ = view(SCR + 26112, F32, [128, 256])
        rk = view(SCR + 27136, F32, [128, 256])
        for j in range(4):
            slot = wload([(g["wkv_d"][l, :, j * 256:(j + 1) * 256], 0, 256)])
            if j < 2:
                for hh in range(2):
                    h = 2 * j + hh
                    pairs = [(Wr[slot][:, c, hh * 128:(hh + 1) * 128], memh[:, c, :]) for c in range(NC)]

                    def fn(e, pairs=pairs):
                        for i, (l_, r) in enumerate(pairs):
                            ins = e.matmul(psb(4, 256), l_, r, start=(i == 0), stop=(i == NC - 1))
                        return ins
                    op("pe", fn, reads=[("W", slot)] + memh_keys, writes=[("ps", 4)])
                    op("act", lambda e: e.activation(ksq, psb(4, 256), AF.Square), reads=[("ps", 4)], writes=["ksq"])
                    op("dve", lambda e: e.tensor_copy(kf, psb(4, 256)), reads=[("ps", 4), "ksq"], writes=["kf"])
                    op("pe", lambda e: e.matmul(psb(6, 256), ones_bf, ksq, start=True, stop=True),
                       reads=["ksq", "ones"], writes=[("ps", 6)])
                    op("act", lambda e: e.activation(kt_, psb(6, 256), AF.Ln, bias=EPS, scale=1.0 / 128.0),
                       reads=[("ps", 6)], writes=["kt"])
                    op("act", lambda e: e.activation(rk, kt_, AF.Exp, scale=-0.5), reads=["kt"], writes=["rk"])
                    op("dve", lambda e, h=h: e.scalar_tensor_tensor(knT[:, h, :], kf, kgq, rk, ALU.mult, ALU.mult),
                       reads=["kf", "rk", "kgq"], writes=[("knT", h)])
            else:
                jj = j - 2
                for mb in range(2):
                    pairs = [(memh[:, c, mb * 128:(mb + 1) * 128], Wr[slot][:, c, :]) for c in range(NC)]

                    def fn(e, pairs=pairs):
                        for i, (l_, r) in enumerate(pairs):
                            ins = e.matmul(psb(5, 256), l_, r, start=(i == 0), stop=(i == NC - 1))
                        return ins
                    op("pe", fn, reads=[("W", slot)] + memh_keys, writes=[("ps", 5)])
                    op("act", lambda e, mb=mb, jj=jj: e.copy(memv[:, mb, jj * 256:(jj + 1) * 256], psb(5, 256)),
                       reads=[("ps", 5)], writes=[("memv", mb, jj)])
        def dst_h(tb, half):
            return hT[:, half * 8:(half + 1) * 8, tb * 128:(tb + 1) * 128], [("hT", tb)] if half == 1 else [("hTa", tb)]
        norm_transpose(src_ap, GB, dst_h, 16, "x")

        sch.barrier()

        if STOP <= 2:
            finish_dbg()
            return
        lng = view(SCR + 0, F32, [128, 1024])
        lnb = view(SCR + 4096, F32, [128, 1024])
        dma("sp", lambda e, l=l: e.dma_start(out=lng, in_=g["lng_d"][l:l + 1, :].partition_broadcast(128)),
            reads=[], writes=["lng"], key="c0")
        dma("sp", lambda e, l=l: e.dma_start(out=lnb, in_=g["lnb_d"][l:l + 1, :].partition_broadcast(128)),
            reads=[], writes=["lnb"], key="c1")
        gu_t = [view(SCR + 8192 + i * 2048, F32, [128, 512]) for i in range(2)]
        th_t = [view(SCR + 12288 + i * 2048, F32, [128, 512]) for i in range(2)]
        lnt = [view(SCR + 16384 + i * 4096, F32, [128, 1024]) for i in range(2)]
        it = 0
        for j in range(4):
            slot = wload([(g["w_in_d"][l, :, 1024 + j * 256:1024 + (j + 1) * 256], 0, 256)])
            for tb in range(16):
                bank = 4 + it % 2
                it += 1
                pairs = [(hT[:, c, tb * 128:(tb + 1) * 128], Wr[slot][:, c, :]) for c in range(NC)]

                def fn(e, pairs=pairs, bank=bank):
                    for i, (l_, r) in enumerate(pairs):
                        ins = e.matmul(psb(bank, 256), l_, r, start=(i == 0), stop=(i == NC - 1))
                    return ins
                op("pe", fn, reads=[("W", slot), ("hT", tb), ("hTa", tb)], writes=[("ps", bank)])
                op("act", lambda e, tb=tb, j=j, bank=bank: e.activation(VA[:, tb, j * 256:(j + 1) * 256], psb(bank, 256), AF.Gelu),
                   reads=[("ps", bank)], writes=[("VA", tb, j)])
        def lnA(tb):
            pz = tb % 2
            st_, mv_, t1_, rs_ = ST2[pz], MV2[pz], T12[pz], RS2[pz]
            for k in range(2):
                op("dve", lambda e, tb=tb, k=k, st_=st_: e.bn_stats(st_[:, k, :], VA[:, tb, k * 512:(k + 1) * 512]),
                   reads=[("VA", tb, 2 * k), ("VA", tb, 2 * k + 1)], writes=[("lst", pz, k)])
            op("dve", lambda e, st_=st_, mv_=mv_: e.bn_aggr(mv_, st_[:, 0:2, :].rearrange("p a b -> p (a b)")),
               reads=[("lst", pz, 0), ("lst", pz, 1)], writes=[("lmv", pz)])
            op("dve", lambda e, mv_=mv_, t1_=t1_: e.tensor_scalar(t1_, mv_[:, 1:2], EPS, None, ALU.add),
               reads=[("lmv", pz)], writes=[("lt1", pz)])
            op("pool", lambda e, t1_=t1_, rs_=rs_: e.tensor_tensor(rs_, t1_, neghalf[:, 0:1], ALU.pow),
               reads=[("lt1", pz)], writes=[("lrs", pz)])

        def lnB(tb):
            pz = tb % 2
            mv_, rs_ = MV2[pz], RS2[pz]
            tmp = lnt[tb % 2]
            op("dve", lambda e, tb=tb, tmp=tmp, mv_=mv_: e.scalar_tensor_tensor(tmp, VA[:, tb, :], mv_[:, 0:1], lng, ALU.subtract, ALU.mult),
               reads=[("VA", tb, j) for j in range(4)] + [("lmv", pz), "lng"], writes=[("lnt", tb % 2)])
            op("dve", lambda e, tb=tb, tmp=tmp, rs_=rs_: e.scalar_tensor_tensor(VA[:, tb, :], tmp, rs_, lnb, ALU.mult, ALU.add),
               reads=[("lnt", tb % 2), ("lrs", pz), "lnb"], writes=[("VA", tb, j) for j in range(4)])

        lnA(0)
        it = 0
        for j in range(8):
            slot = wload([(g["w_in_d"][l, :, j * 128:(j + 1) * 128], 0, 128),
                          (g["w_in_d"][l, :, 2048 + j * 128:2048 + (j + 1) * 128], 128, 128)])
            for G in range(4):
                bu = it % 2
                bz = 2 + it % 2
                gu = gu_t[it % 2]
                th = th_t[it % 2]
                if it % 2 == 0:
                    tb_ln = it // 2
                    if tb_ln + 1 < 16:
                        lnA(tb_ln + 1)
                    lnB(tb_ln)
                it += 1
                proj_fm(slot, 0, G, bu)
                proj_fm(slot, 128, G, bz)
                op("act", lambda e, gu=gu, bu=bu: e.activation(gu, psb(bu), AF.Gelu), reads=[("ps", bu)], writes=[("gu", bu)])
                op("act", lambda e, th=th, bz=bz: e.activation(th, psb(bz), AF.Tanh, scale=0.5), reads=[("ps", bz)], writes=[("th", bz)])
                op("dve", lambda e, th=th, bz=bz: e.scalar_tensor_tensor(th, th, 1.0, psb(bz), ALU.add, ALU.mult),
                   reads=[("th", bz), ("ps", bz)], writes=[("th", bz)])
                op("dve", lambda e, th=th, gu=gu, j=j, G=G: e.scalar_tensor_tensor(yT[:, j, G * 512:(G + 1) * 512], th, 0.5, gu, ALU.mult, ALU.mult),
                   reads=[("th", bz), ("gu", bu)], writes=[("yT", j, G)])
        it = 0
        for G in range(4):
            for gi in range(8):
                bank = 6 + it % 2
                it += 1

                def fn(e, G=G, gi=gi, bank=bank):
                    for cc in range(4):
                        c = 4 * G + cc
                        o = psb(bank, 128, cc * 128)
                        e.matmul(o, VA[:, c, gi * 128:(gi + 1) * 128], wT[:, gi, :], start=True, stop=False)
                        e.matmul(o, ones0, bhi[:, gi * 128:(gi + 1) * 128], start=False, stop=False)
                        ins = e.matmul(o, ones0, blo[:, gi * 128:(gi + 1) * 128], start=False, stop=True)
                    return ins
                op("pe", fn, reads=[("VA", 4 * G + cc, jj) for cc in range(4) for jj in range(4)] + ["wT", "bhi", "blo", "ones0"],
                   writes=[("ps", bank)])
                op("dve", lambda e, G=G, gi=gi, bank=bank: e.tensor_tensor(yT[:, gi, G * 512:(G + 1) * 512], psb(bank),
                                                                            yT[:, gi, G * 512:(G + 1) * 512], ALU.mult),
                   reads=[("ps", bank), ("yT", gi, G)], writes=[("yT", gi, G)])
        sch.barrier()

        if STOP <= 3:
            finish_dbg()
            return
        qT = view(SCR + 0, BF16, [128, S])
        kT = view(SCR + 4096, BF16, [128, S])
        vh = view(SCR + 8192, BF16, [128, 16, 128])
        zs = view(SCR + 12288, BF16, [128, S])
        E_t = [view(SCR + 16384 + i * 2048, F32, [128, 512]) for i in range(2)]
        sp_t = [view(SCR + 20480 + i * 1024, BF16, [128, 512]) for i in range(3)]
        R_t = [view(SCR + 23552 + i * 1024, BF16, [128, 512]) for i in range(3)]
        A_t = [view(SCR + 26624 + i * 1024, BF16, [128, 512]) for i in range(3)]
        ez_t = [view(SCR + 29696 + i * 2048, F32, [128, 512]) for i in range(2)]
        assert SCR + 33792 <= 211968
        pj = 0
        for h in range(4):
            c_q = 3072 + h * 128
            slot1 = wload([(g["w_in_d"][l, :, c_q:c_q + 128], 0, 128),
                           (g["w_in_d"][l, :, c_q + 512:c_q + 640], 128, 128)])
            for G in range(4):
                b = pj % 2
                pj += 1
                proj_fm(slot1, 0, G, b)
                op("act", lambda e, G=G, b=b: e.activation(qT[:, G * 512:(G + 1) * 512], psb(b), AF.Copy, scale=SCALE),
                   reads=[("ps", b)], writes=[("qT", G)])
                b = pj % 2
                pj += 1
                proj_fm(slot1, 128, G, b)
                op("dve", lambda e, G=G, b=b: e.tensor_copy(kT[:, G * 512:(G + 1) * 512], psb(b)),
                   reads=[("ps", b)], writes=[("kT", G)])
            slot2 = wload()
            for G in range(4):
                b = pj % 2
                pj += 1
                ez = ez_t[G % 2]
                proj_fm(slot2, 128, G, b)
                op("act", lambda e, ez=ez, b=b: e.activation(ez, psb(b), AF.Exp, scale=-1.0), reads=[("ps", b)], writes=[("ez", G % 2)])
                op("act", lambda e, ez=ez: e.activation(ez, ez, AF.Ln, bias=1.0), reads=[("ez", G % 2)], writes=[("ez", G % 2)])
                op("act", lambda e, ez=ez: e.activation(ez, ez, AF.Exp, scale=-1.0), reads=[("ez", G % 2)], writes=[("ez", G % 2)])
                op("dve", lambda e, ez=ez, G=G, b=b: e.tensor_tensor(zs[:, G * 512:(G + 1) * 512], psb(b), ez, ALU.mult),
                   reads=[("ps", b), ("ez", G % 2)], writes=[("zs", G)])
            for t4 in range(4):
                b = pj % 2
                pj += 1

                def fn(e, t4=t4, b=b, slot2=slot2):
                    for tt in range(4):
                        tb = 4 * t4 + tt
                        for c in range(NC):
                            ins = e.matmul(psb(b, 128, tt * 128), hT[:, c, tb * 128:(tb + 1) * 128], Wr[slot2][:, c, 0:128],
                                           start=(c == 0), stop=(c == NC - 1))
                    return ins
                op("pe", fn, reads=[("W", slot2)] + hkeys(t4), writes=[("ps", b)])
                op("act", lambda e, t4=t4, b=b: e.copy(vh[:, 4 * t4:4 * t4 + 4, :], psb(b).rearrange("p (a b) -> p a b", a=4)),
                   reads=[("ps", b)], writes=[("vh", t4)])
            units = []
            for G in range(4):
                nb = 4 * G + 4
                for k_, bidx in enumerate(range(nb - 1, -1, -1)):
                    i_d = bidx - 4 * G
                    c0 = 128 * i_d if i_d >= 0 else 0
                    units.append(dict(G=G, bidx=bidx, diag=(i_d >= 0), c0=c0, ncol=512 - c0, first=(k_ == 0), last=(bidx == 0)))
            nU = len(units)

            def S1(u):
                U = units[u]
                G, bidx, diag, c0, ncol = U["G"], U["bidx"], U["diag"], U["c0"], U["ncol"]
                zb = 2 + u % 4
                E = E_t[u % 2]
                sp = sp_t[u % 3]
                kblk = kT[:, bidx * 128:(bidx + 1) * 128]
                qcols = qT[:, G * 512 + c0:G * 512 + 512]

                def fz(e):
                    ins = e.matmul(psb(zb, ncol), kblk, qcols, start=True, stop=not diag)
                    if diag:
                        ins = e.matmul(psb(zb, 128), ident_bf, negmask, start=False, stop=True)
                    return ins
                op("pe", fz, reads=[("kT", bidx // 4), ("qT", G), "ident", "negmask"], writes=[("ps", zb)])
                op("act", lambda e: e.activation(E[:, 0:ncol], psb(zb, ncol), AF.Exp), reads=[("ps", zb)], writes=[("E", u % 2)])

            def S1b(u):
                ncol = units[u]["ncol"]
                E = E_t[u % 2]
                sp = sp_t[u % 3]
                op("act", lambda e: e.activation(sp[:, 0:ncol], E[:, 0:ncol], AF.Ln, bias=1.0),
                   reads=[("E", u % 2)], writes=[("sp", u % 3)])

            def S2(u):
                U = units[u]
                G, bidx, diag, c0, ncol, first = U["G"], U["bidx"], U["diag"], U["c0"], U["ncol"], U["first"]
                zb = 2 + u % 4
                sp = sp_t[u % 3]
                A = A_t[u % 3]
                Rcur = R_t[u % 3]
                Rnext = R_t[(u + 1) % 3]

                kblk2 = kT[:, bidx * 128:(bidx + 1) * 128]
                qcols2 = qT[:, G * 512 + c0:G * 512 + 512]

                def fp(e):
                    if os.environ.get("KZ"):
                        e.matmul(psb(zb, ncol), kblk2, qcols2, start=True, stop=not diag)
                        if diag:
                            e.matmul(psb(zb, 128), ident_bf, negmask, start=False, stop=True)
                    ins = e.matmul(psb(zb, ncol), negU, sp[:, 0:ncol], start=False, stop=True, skip_group_check=True)
                    if not first:
                        if diag:
                            ins = e.matmul(psb(zb, ncol - 128, 128), negOnes, Rcur[:, c0 + 128:512], start=False, stop=True,
                                           skip_group_check=True)
                        else:
                            ins = e.matmul(psb(zb, 512), negOnes, Rcur, start=False, stop=True, skip_group_check=True)
                    return ins
                op("pe", fp, reads=[("sp", u % 3), ("R", u % 3), "negU", "negones"], writes=[("ps", zb)])
                if not U["last"]:
                    if first:
                        op("pool", lambda e: e.tensor_copy(Rnext[:, c0:512], sp[:, 0:ncol]),
                           reads=[("sp", u % 3)], writes=[("R", (u + 1) % 3)])
                    elif diag:
                        op("pool", lambda e: e.tensor_copy(Rnext[:, c0:c0 + 128], sp[:, 0:128]),
                           reads=[("sp", u % 3)], writes=[("R", (u + 1) % 3)])
                        op("pool", lambda e: e.tensor_tensor(Rnext[:, c0 + 128:512], Rcur[:, c0 + 128:512], sp[:, 128:ncol], ALU.add),
                           reads=[("sp", u % 3), ("R", u % 3)], writes=[("R", (u + 1) % 3)])
                    else:
                        op("pool", lambda e: e.tensor_tensor(Rnext, Rcur, sp, ALU.add),
                           reads=[("sp", u % 3), ("R", u % 3)], writes=[("R", (u + 1) % 3)])

            def S2b(u):
                ncol = units[u]["ncol"]
                zb = 2 + u % 4
                A = A_t[u % 3]
                op("act", lambda e: e.activation(A[:, 0:ncol], psb(zb, ncol), AF.Exp), reads=[("ps", zb)], writes=[("A", u % 3)])

            def S3(u, h=h):
                U = units[u]
                G, bidx, diag, c0, ncol, first, last = U["G"], U["bidx"], U["diag"], U["c0"], U["ncol"], U["first"], U["last"]
                yb = 6 + G % 2
                A = A_t[u % 3]
                vblk = vh[:, bidx, :]

                def fy(e):
                    if first:
                        ins = e.matmul(psb(yb, ncol, c0), vblk, A[:, 0:ncol], start=True, stop=True)
                    elif diag:
                        e.matmul(psb(yb, 128, c0), vblk, A[:, 0:128], start=False, stop=True, skip_group_check=True)
                        ins = e.matmul(psb(yb, ncol - 128, c0 + 128), vblk, A[:, 128:ncol], start=False, stop=True,
                                       skip_group_check=True)
                    else:
                        ins = e.matmul(psb(yb, 512), vblk, A, start=False, stop=True, skip_group_check=True)
                    return ins
                op("pe", fy, reads=[("vh", bidx // 4), ("A", u % 3)], writes=[("ps", yb)])
                if last:
                    op("dve", lambda e: e.tensor_tensor(yT[:, 8 + h, G * 512:(G + 1) * 512], psb(yb),
                                                        zs[:, G * 512:(G + 1) * 512], ALU.mult),
                       reads=[("ps", yb), ("zs", G)], writes=[("yT", 8 + h, G)])

            for i in range(nU + 3):
                if i < nU:
                    S1(i)
                if 0 <= i - 1 < nU:
                    S2(i - 1)
                if 0 <= i - 2 < nU:
                    S2b(i - 2)
                if i < nU:
                    S1b(i)
                if 0 <= i - 3 < nU:
                    S3(i - 3)
        sch.barrier()

        if STOP <= 4:
            finish_dbg()
            return
        def cbuf(p_):
            o = SCR + p_ * 15360
            return dict(sq=view(o, BF16, [128, 512]), tq=view(o + 1024, F32, [128, 512]), rq=view(o + 3072, F32, [128, 512]),
                        qh=view(o + 5120, BF16, [128, 512]), ez=view(o + 6144, F32, [128, 512]), zsc=view(o + 8192, BF16, [128, 512]),
                        PT=[view(o + 9216 + i * 1024, BF16, [128, 512]) for i in range(2)],
                        rd=view(o + 11264, F32, [128, 512]), t2=view(o + 13312, F32, [128, 512]))
        CB = [cbuf(0), cbuf(1)]
        assert SCR + 2 * 15360 <= 211968
        cits = [(h, G) for h in range(4) for G in range(4)]
        cslot = {}

        def c_projq(it):
            h, G = cits[it]
            if G == 0:
                cslot[h] = wload()
            proj_fm(cslot[h], 0, G, it % 2)

        def c_projz(it):
            h, G = cits[it]
            proj_fm(cslot[h], 128, G, 2 + it % 2)

        c_projq(0)
        c_projz(0)
        for it, (h, G) in enumerate(cits):
            p_ = it % 2
            cb = CB[p_]
            qb = it % 2
            zb = 2 + it % 2
            sq, tq, rq, qh, ez, zsc, PT, rd, t2 = (cb["sq"], cb["tq"], cb["rq"], cb["qh"], cb["ez"], cb["zsc"], cb["PT"], cb["rd"], cb["t2"])
            K = lambda n_: (n_, p_)
            op("act", lambda e, sq=sq, qb=qb: e.activation(sq, psb(qb), AF.Square), reads=[("ps", qb)], writes=[K("sq")])
            op("pe", lambda e, sq=sq: e.matmul(psb(6), ones_bf, sq, start=True, stop=True), reads=[K("sq"), "ones"], writes=[("ps", 6)])
            if it + 1 < len(cits):
                c_projq(it + 1)
            op("act", lambda e, tq=tq: e.activation(tq, psb(6), AF.Ln, bias=EPS, scale=1.0 / 128.0), reads=[("ps", 6)], writes=[K("tq")])
            op("act", lambda e, tq=tq, rq=rq: e.activation(rq, tq, AF.Exp, scale=-0.5), reads=[K("tq")], writes=[K("rq")])
            op("dve", lambda e, qh=qh, rq=rq, qb=qb: e.tensor_tensor(qh, psb(qb), rq, ALU.mult),
               reads=[("ps", qb), K("sq"), K("rq")], writes=[K("qh")])
            op("act", lambda e, ez=ez, zb=zb: e.activation(ez, psb(zb), AF.Exp, scale=-1.0), reads=[("ps", zb)], writes=[K("ez")])
            op("act", lambda e, ez=ez: e.activation(ez, ez, AF.Ln, bias=1.0), reads=[K("ez")], writes=[K("ez")])
            op("act", lambda e, ez=ez: e.activation(ez, ez, AF.Exp, scale=-1.0), reads=[K("ez")], writes=[K("ez")])
            op("dve", lambda e, zsc=zsc, ez=ez, zb=zb: e.tensor_tensor(zsc, psb(zb), ez, ALU.mult), reads=[("ps", zb), K("ez")], writes=[K("zsc")])
            for mb in range(2):
                op("pe", lambda e, mb=mb, h=h, qh=qh: e.matmul(psb(4 + mb), knT[:, h, mb * 128:(mb + 1) * 128], qh, start=True, stop=True),
                   reads=[("knT", h), K("qh")], writes=[("ps", 4 + mb)])
            if it + 1 < len(cits):
                c_projz(it + 1)
            for mb in range(2):
                op("act", lambda e, mb=mb, PT=PT: e.activation(PT[mb], psb(4 + mb), AF.Exp), reads=[("ps", 4 + mb)], writes=[K("PT%d" % mb)])

            def fo(e, h=h, PT=PT):
                e.matmul(psb(7), memv[:, 0, h * 128:(h + 1) * 128], PT[0], start=True, stop=False)
                e.matmul(psb(7), memv[:, 1, h * 128:(h + 1) * 128], PT[1], start=False, stop=True)
                e.matmul(psb(6), ones_bf, PT[0], start=True, stop=False)
                return e.matmul(psb(6), ones_bf, PT[1], start=False, stop=True)
            op("pe", fo, reads=[K("PT0"), K("PT1"), "ones"] + [("memv", mb, jj) for mb in range(2) for jj in range(2)],
               writes=[("ps", 7), ("ps", 6)])
            op("act", lambda e, rd=rd: e.activation(rd, psb(6), AF.Ln), reads=[("ps", 6)], writes=[K("rd")])
            op("act", lambda e, rd=rd: e.activation(rd, rd, AF.Exp, scale=-1.0), reads=[K("rd")], writes=[K("rd")])
            op("dve", lambda e, t2=t2, rd=rd: e.tensor_tensor(t2, psb(7), rd, ALU.mult), reads=[("ps", 7), K("rd")], writes=[K("t2")])
            op("dve", lambda e, h=h, G=G, t2=t2, zsc=zsc: e.tensor_tensor(yT[:, 12 + h, G * 512:(G + 1) * 512], t2, zsc, ALU.mult),
               reads=[K("t2"), K("zsc")], writes=[("yT", 12 + h, G)])
        sch.barrier()

        if dbg and l == int(os.environ.get("KL", "0")):
            finish_dbg()
        if STOP <= 5:
            return

        NXR = 6
        xr_t = [view(HT_OFF + i * 1024, F32, [128, 256]) for i in range(NXR)]
        os_t = [view(HT_OFF + 8192 + i * 1024, F32, [128, 256]) for i in range(4)]
        p3 = [(n, tb) for n in range(8) for tb in range(16)]

        def p3_load(k, src_ap=src_ap):
            n, tb = p3[k]
            xr = xr_t[k % NXR]
            dma("sp", lambda e: e.dma_start(out=xr, in_=src_ap[tb * 128:(tb + 1) * 128, n * 256:(n + 1) * 256]),
                reads=[], writes=[("xr", k % NXR)], key=("xr", k % NXR))

        for k in range(3):
            p3_load(k)
        slot = None
        for k, (n, tb) in enumerate(p3):
            if tb == 0:
                slot = wload()
            if k + 3 < len(p3):
                p3_load(k + 3)
            bank = k % 8
            xr = xr_t[k % NXR]
            ost = os_t[k % 4]
            pairs = [(yT[:, c, tb * 128:(tb + 1) * 128], Wr[slot][:, c, :]) for c in range(NC)]

            def fn(e, pairs=pairs, bank=bank):
                for i, (l_, r) in enumerate(pairs):
                    ins = e.matmul(psb(bank, 256), l_, r, start=(i == 0), stop=(i == NC - 1))
                return ins
            op("pe", fn, reads=[("W", slot)], writes=[("ps", bank)])
            op("dve", lambda e, ost=ost, xr=xr, bank=bank: e.tensor_tensor(ost, psb(bank, 256), xr, ALU.add),
               reads=[("ps", bank), ("xr", k % NXR)], writes=[("os", k % 4)])
            dma("sp", lambda e, ost=ost, tb=tb, n=n, dst_ap=dst_ap: e.dma_start(out=dst_ap[tb * 128:(tb + 1) * 128, n * 256:(n + 1) * 256], in_=ost),
                reads=[("os", k % 4)], writes=[("xdst", tb, n)], key=("os", k % 4))
        sch.barrier()


_NC_CACHE = {}


def kernel(x, mem, norm_g, w_in, sgu_ln_g, sgu_ln_b, sgu_w, sgu_b, mem_norm_g, w_mem_kv, q_norm_g, k_norm_g, w_out):
    if "nc" not in _NC_CACHE:
        _NC_CACHE["nc"] = build_nc()
    nc = _NC_CACHE["nc"]
    f = lambda a: np.ascontiguousarray(np.asarray(a, dtype=np.float32))
    shared = {"norm_g": f(norm_g), "w_in": f(w_in), "sgu_ln_g": f(sgu_ln_g), "sgu_ln_b": f(sgu_ln_b),
              "sgu_w": f(sgu_w), "sgu_b": f(sgu_b), "mem_norm_g": f(mem_norm_g), "w_mem_kv": f(w_mem_kv),
              "q_norm_g": f(q_norm_g), "k_norm_g": f(k_norm_g), "w_out": f(w_out)}
    x = f(x)
    mem = f(mem)
    in_maps = [dict(shared, x=x[b], mem=mem[b]) for b in range(8)]
    res = run_bass_kernel_spmd(nc, in_maps, core_ids=list(range(8)))
    return np.stack([np.asarray(r["out"], dtype=np.float32) for r in res.results], axis=0)
```
